# Optimizing a Trainium2 kernel written in Bass

```python
import math
import jax
import jax.numpy as jnp
from jax import lax
import numpy as np

D_MODEL = 1024
BATCH = 16
SEQ = 2048
DEPTH = 4

GRID_W = 64
CTX_LEN = 256
N_BRANCH = 4
BRANCH_W = D_MODEL // N_BRANCH
HEAD_DIM = 64
N_HEADS = BRANCH_W // HEAD_DIM
FOURIER_GROUPS = N_HEADS
CHUNK = 64
CONV_K = 5
SSD_STATE = 64
SSD_GROUPS = 2
N_EXPERTS = 16
EXPERT_FF = 1024
EC_CAPACITY_FACTOR = 2
DEEPNORM_ALPHA = (2.0 * DEPTH) ** 0.25
DEEPNORM_BETA = (8.0 * DEPTH) ** -0.25
NORM_EPS = 1e-6

IN_SPLITS = (
    ("a_q", BRANCH_W), ("a_f_fwd", BRANCH_W), ("a_f_bwd", BRANCH_W), ("a_v", BRANCH_W), ("a_g", BRANCH_W),
    ("b_qkv", 3 * BRANCH_W), ("b_g", BRANCH_W), ("b_a", 2 * N_HEADS), ("b_beta", 2 * N_HEADS),
    ("c_u", BRANCH_W),
    ("d_xbc", BRANCH_W + 2 * SSD_GROUPS * SSD_STATE), ("d_z", BRANCH_W), ("d_dt", 2 * N_HEADS),
)
IN_COLS = sum(s for _, s in IN_SPLITS)

kernel_name = "hybrid_bidir_diffusion_trunk"


def _layer_norm(x):
    xf = x.astype(jnp.float32)
    mu = jnp.mean(xf, axis=-1, keepdims=True)
    var = jnp.mean(jnp.square(xf - mu), axis=-1, keepdims=True)
    return (xf - mu) * lax.rsqrt(var + NORM_EPS)


def post_norm(x, g, b):
    return (_layer_norm(x) * g + b).astype(x.dtype)


def modulate(x, shift, scale):
    return (_layer_norm(x) * (1.0 + scale) + shift).astype(x.dtype)


def _rms(x):
    return x * lax.rsqrt(jnp.mean(jnp.square(x), axis=-1, keepdims=True) + NORM_EPS)


def to_heads(x):
    b_, t_, w_ = x.shape
    return x.reshape(b_, t_, N_HEADS, w_ // N_HEADS).transpose(0, 2, 1, 3)


def from_heads(x):
    b_, h_, t_, d_ = x.shape
    return x.transpose(0, 2, 1, 3).reshape(b_, t_, h_ * d_)


def head_rms_norm(o, w):
    return from_heads(_rms(o)) * w


def l2norm(x):
    return x * lax.rsqrt(jnp.sum(jnp.square(x), axis=-1, keepdims=True) + NORM_EPS)


def split_columns(z):
    sizes = [s for _, s in IN_SPLITS]
    parts = jnp.split(z, np.cumsum(sizes)[:-1].tolist(), axis=-1)
    return {name: p for (name, _), p in zip(IN_SPLITS, parts)}


def conv_centred(x, w):
    pad = CONV_K // 2
    return lax.conv_general_dilated(x, w[:, None, :], window_strides=(1,), padding=[(pad, pad)],
                                    dimension_numbers=("NWC", "WIO", "NWC"),
                                    feature_group_count=x.shape[-1])


def sincos_grid(rows, cols, dim):
    quarter = dim // 4
    omega = 1.0 / (10000.0 ** (jnp.arange(quarter, dtype=jnp.float32) / quarter))
    er = jnp.arange(rows, dtype=jnp.float32)[:, None] * omega
    ec = jnp.arange(cols, dtype=jnp.float32)[:, None] * omega
    er = jnp.concatenate([jnp.sin(er), jnp.cos(er)], axis=-1)
    ec = jnp.concatenate([jnp.sin(ec), jnp.cos(ec)], axis=-1)
    emb = jnp.concatenate([jnp.broadcast_to(er[:, None, :], (rows, cols, dim // 2)),
                           jnp.broadcast_to(ec[None, :, :], (rows, cols, dim // 2))], axis=-1)
    return emb.reshape(rows * cols, dim)


def _chunk(x):
    return x.reshape(x.shape[0], x.shape[1], x.shape[2] // CHUNK, CHUNK, *x.shape[3:])


def _tri():
    return jnp.tril(jnp.ones((CHUNK, CHUNK), dtype=bool))


def gla_vector_scan(q, k, v, logf, s0):
    tri = _tri()
    xs = tuple(jnp.moveaxis(_chunk(a), 2, 0) for a in (q, k, v, logf))

    def step(S, inp):
        qc, kc, vc, lf = inp
        g = jnp.cumsum(lf, axis=2)
        rel = jnp.where(tri[:, :, None], g[:, :, :, None, :] - g[:, :, None, :, :], -jnp.inf)
        att = jnp.einsum("bhid,bhijd,bhjd->bhij", qc, jnp.exp(rel), kc)
        g_last = g[:, :, -1:, :]
        o = att @ vc + (qc * jnp.exp(g)) @ S
        S = jnp.exp(g_last[:, :, 0, :, None]) * S + jnp.einsum("bhjd,bhje->bhde", kc * jnp.exp(g_last - g), vc)
        return S, o

    S, o = lax.scan(step, s0, xs)
    return jnp.moveaxis(o, 0, 2).reshape(v.shape), S


def ssd_scan(q, k, v, loga, s0):
    tri = _tri()
    qc, kc, vc, la = _chunk(q), _chunk(k), _chunk(v), _chunk(loga)
    g = jnp.cumsum(la, axis=-1)
    L = jnp.exp(jnp.where(tri, g[..., :, None] - g[..., None, :], -jnp.inf))
    o_intra = jnp.einsum("bhnij,bhnje->bhnie", jnp.einsum("bhnid,bhnjd->bhnij", qc, kc) * L, vc)
    g_last = g[..., -1]
    dS = jnp.einsum("bhnjd,bhnje->bhnde", kc * jnp.exp(g_last[..., None] - g)[..., None], vc)

    def step(S, inp):
        dS_c, gl_c = inp
        return jnp.exp(gl_c)[..., None, None] * S + dS_c, S

    S_T, S_in = lax.scan(step, s0, (jnp.moveaxis(dS, 2, 0), jnp.moveaxis(g_last, 2, 0)))
    S_in = jnp.moveaxis(S_in, 0, 2)
    o = o_intra + jnp.einsum("bhnid,bhnde->bhnie", qc * jnp.exp(g)[..., None], S_in)
    return o.reshape(v.shape), S_T


def gated_delta_scan(q, k, v, loga, beta, s0):
    tri = _tri()
    strict = tri & ~jnp.eye(CHUNK, dtype=bool)
    qc, kc, vc, la, bt = _chunk(q), _chunk(k), _chunk(v), _chunk(loga), _chunk(beta)
    g = jnp.cumsum(la, axis=-1)
    L = jnp.exp(jnp.where(tri, g[..., :, None] - g[..., None, :], -jnp.inf))
    a_mat = jnp.where(strict, bt[..., :, None] * jnp.einsum("bhnid,bhnjd->bhnij", kc, kc) * L, 0.0)
    t_mat = a_mat + jnp.eye(CHUNK, dtype=a_mat.dtype)
    u = lax.linalg.triangular_solve(t_mat, bt[..., None] * vc, left_side=True, lower=True, unit_diagonal=True)
    w = lax.linalg.triangular_solve(t_mat, bt[..., None] * kc * jnp.exp(g)[..., None],
                                    left_side=True, lower=True, unit_diagonal=True)
    qk = jnp.einsum("bhnid,bhnjd->bhnij", qc, kc) * L
    g_last = g[..., -1]
    q_dec = qc * jnp.exp(g)[..., None]
    k_dec = kc * jnp.exp(g_last[..., None] - g)[..., None]
    xs = tuple(jnp.moveaxis(a, 2, 0) for a in (u, w, qk, q_dec, k_dec, g_last))

    def step(S, inp):
        u_c, w_c, qk_c, qd_c, kd_c, gl_c = inp
        v_new = u_c - w_c @ S
        o_c = qd_c @ S + qk_c @ v_new
        S = jnp.exp(gl_c)[..., None, None] * S + jnp.einsum("bhjd,bhje->bhde", kd_c, v_new)
        return S, o_c

    S, o = lax.scan(step, s0, xs)
    return jnp.moveaxis(o, 0, 2).reshape(v.shape), S


def bidirectional(scan_fn, ctx_fwd, lat_fwd, ctx_bwd, lat_bwd, s0):
    def flip(t):
        return tuple(jnp.flip(a, axis=2) for a in t)
    oc_f, sc_f = scan_fn(*ctx_fwd, s0)
    ol_f, _ = scan_fn(*lat_fwd, sc_f)
    oc_b, sc_b = scan_fn(*flip(ctx_bwd), s0)
    ol_b, _ = scan_fn(*flip(lat_bwd), sc_b)
    return oc_f + jnp.flip(oc_b, axis=2), ol_f + jnp.flip(ol_b, axis=2)


def hgrn_lower_bounds(logits):
    cum = jnp.cumsum(jax.nn.softmax(logits.astype(jnp.float32), axis=1), axis=1)
    return cum - cum[:, :1]


def hgrn2_mixer(zc, zl, lb, norm_w):
    def inputs(z, d):
        zf = z[("a_f_fwd", "a_f_bwd")[d]].astype(jnp.float32)
        lbd = lb[d]
        logf = jnp.logaddexp(jnp.log(lbd), jnp.log1p(-lbd) + jax.nn.log_sigmoid(zf))
        inp = (1.0 - lbd) * jax.nn.sigmoid(-zf)
        q = jax.nn.silu(z["a_q"].astype(jnp.float32))
        v = z["a_v"].astype(jnp.float32)
        return (to_heads(q), to_heads(inp), to_heads(v), to_heads(logf))

    b_ = zc["a_q"].shape[0]
    s0 = jnp.zeros((b_, N_HEADS, HEAD_DIM, HEAD_DIM), jnp.float32)
    oc, ol = bidirectional(gla_vector_scan, inputs(zc, 0), inputs(zl, 0), inputs(zc, 1), inputs(zl, 1), s0)

    def finish(o, z):
        return head_rms_norm(o, norm_w) * jax.nn.silu(z["a_g"].astype(jnp.float32))
    return finish(oc, zc), finish(ol, zl)


def gated_deltanet_mixer(zc, zl, conv_w, a_log, dt_bias, norm_w):
    def inputs(z):
        qkv = jax.nn.silu(conv_centred(z["b_qkv"], conv_w)).astype(jnp.float32)
        q, k, v = jnp.split(qkv, 3, axis=-1)
        q = l2norm(to_heads(q)) * HEAD_DIM ** -0.5
        k = l2norm(to_heads(k))
        v = to_heads(v)
        a = z["b_a"].astype(jnp.float32)
        bb = z["b_beta"].astype(jnp.float32)
        dirs = []
        for d in range(2):
            sl = slice(d * N_HEADS, (d + 1) * N_HEADS)
            loga = -jnp.exp(a_log[d].astype(jnp.float32)) * jax.nn.softplus(a[..., sl] + dt_bias[d])
            beta = jax.nn.sigmoid(bb[..., sl])
            dirs.append((q, k, v, jnp.swapaxes(loga, 1, 2), jnp.swapaxes(beta, 1, 2)))
        return dirs

    ctx_in, lat_in = inputs(zc), inputs(zl)
    b_ = zc["b_g"].shape[0]
    s0 = jnp.zeros((b_, N_HEADS, HEAD_DIM, HEAD_DIM), jnp.float32)
    oc, ol = bidirectional(gated_delta_scan, ctx_in[0], lat_in[0], ctx_in[1], lat_in[1], s0)

    def finish(o, z):
        return head_rms_norm(o, norm_w) * jax.nn.silu(z["b_g"].astype(jnp.float32))
    return finish(oc, zc), finish(ol, zl)


def fourier_mixer(u):
    b_, t_, _ = u.shape
    ug = u.astype(jnp.float32).reshape(b_, t_, FOURIER_GROUPS, BRANCH_W // FOURIER_GROUPS)
    return jnp.fft.fft2(ug, axes=(1, 3), norm="ortho").real.reshape(b_, t_, BRANCH_W)


def ssd_mixer(zc, zl, conv_w, conv_b, a_log, dt_bias, d_skip, norm_w):
    def group_heads(t):
        b_, t_, _ = t.shape
        t = jnp.repeat(t.reshape(b_, t_, SSD_GROUPS, SSD_STATE), N_HEADS // SSD_GROUPS, axis=2)
        return t.transpose(0, 2, 1, 3)

    def inputs(z):
        xbc = jax.nn.silu(conv_centred(z["d_xbc"], conv_w) + conv_b).astype(jnp.float32)
        xs, bs, cs = jnp.split(xbc, [BRANCH_W, BRANCH_W + SSD_GROUPS * SSD_STATE], axis=-1)
        xh, bh, ch = to_heads(xs), group_heads(bs), group_heads(cs)
        dt_raw = z["d_dt"].astype(jnp.float32)
        dirs = []
        for d in range(2):
            dt = jax.nn.softplus(dt_raw[..., d * N_HEADS:(d + 1) * N_HEADS] + dt_bias[d])
            dt = jnp.swapaxes(dt, 1, 2)
            loga = -jnp.exp(a_log[d].astype(jnp.float32))[:, None] * dt
            dirs.append((ch, bh, xh * dt[..., None], loga))
        return xh, dirs

    xh_c, ctx_in = inputs(zc)
    xh_l, lat_in = inputs(zl)
    b_ = xh_c.shape[0]
    s0 = jnp.zeros((b_, N_HEADS, SSD_STATE, HEAD_DIM), jnp.float32)
    oc, ol = bidirectional(ssd_scan, ctx_in[0], lat_in[0], ctx_in[1], lat_in[1], s0)

    def finish(o, xh, z):
        y = from_heads(o + d_skip[:, None, None] * xh) * jax.nn.silu(z["d_z"].astype(jnp.float32))
        return _rms(y) * norm_w
    return finish(oc, xh_c, zc), finish(ol, xh_l, zl)


def merge_branches(h, outs, w_gate, b_gate, w_branch, w_out):
    t = [jax.nn.sigmoid(h @ w_gate[g] + b_gate[g]) * (outs[g].astype(h.dtype) @ w_branch[g])
         for g in range(N_BRANCH)]
    return (t[0] + t[1] + t[2] + t[3]) @ w_out


def token_mixer(hc, hl, w_in, lb, hgrn_norm_w, gdn_conv_w, gdn_a_log, gdn_dt_bias, gdn_norm_w,
                ssd_conv_w, ssd_conv_b, ssd_a_log, ssd_dt_bias, ssd_d, ssd_norm_w,
                w_gate, b_gate, w_branch, w_out, with_ctx):
    zc = split_columns(hc @ w_in)
    zl = split_columns(hl @ w_in)
    a_c, a_l = hgrn2_mixer(zc, zl, lb, hgrn_norm_w)
    b_c, b_l = gated_deltanet_mixer(zc, zl, gdn_conv_w, gdn_a_log, gdn_dt_bias, gdn_norm_w)
    d_c, d_l = ssd_mixer(zc, zl, ssd_conv_w, ssd_conv_b, ssd_a_log, ssd_dt_bias, ssd_d, ssd_norm_w)
    y_l = merge_branches(hl, (a_l, b_l, fourier_mixer(zl["c_u"]), d_l), w_gate, b_gate, w_branch, w_out)
    y_c = None
    if with_ctx:
        y_c = merge_branches(hc, (a_c, b_c, fourier_mixer(zc["c_u"]), d_c), w_gate, b_gate, w_branch, w_out)
    return y_c, y_l


def expert_choice_ffn(h, w_router, w_ff_gate, w_ff_up, w_ff_down):
    b_, t_, _ = h.shape
    cap = EC_CAPACITY_FACTOR * t_ // N_EXPERTS
    aff = jax.nn.softmax(jnp.einsum("btd,de->bte", h, w_router).astype(jnp.float32), axis=-1)
    weight, idx = lax.top_k(jnp.swapaxes(aff, 1, 2), cap)
    bidx = jnp.arange(b_)[:, None, None]
    xe = h[bidx, idx]
    hid = jax.nn.silu(jnp.einsum("becd,edf->becf", xe, w_ff_gate)) * jnp.einsum("becd,edf->becf", xe, w_ff_up)
    ye = jnp.einsum("becf,efd->becd", hid, w_ff_down) * weight[..., None].astype(h.dtype)
    return jnp.zeros_like(h).at[bidx, idx].add(ye)


def setup_inputs(seed: int = 0) -> dict:
    key = jax.random.key(seed)
    ks = iter(jax.random.split(key, 48))
    f32 = jnp.float32
    L, D, W, H, E, F = DEPTH, D_MODEL, BRANCH_W, N_HEADS, N_EXPERTS, EXPERT_FF
    XBC = W + 2 * SSD_GROUPS * SSD_STATE

    def nrm(shape, scale):
        return jax.random.normal(next(ks), shape, f32) * scale

    def gain(shape):
        return 1.0 + nrm(shape, 0.02)

    def a_log_init(shape):
        return jnp.log(jax.random.uniform(next(ks), shape, f32, 1.0, 16.0))

    def dt_bias_init(shape):
        dt = jnp.exp(jax.random.uniform(next(ks), shape, f32, math.log(1e-3), math.log(1e-1)))
        return dt + jnp.log(-jnp.expm1(-dt))

    return {
        "x": nrm((BATCH, SEQ, D), 1.0),
        "c": nrm((BATCH, D), 1.0),
        "ctx": nrm((BATCH, CTX_LEN, D), 1.0),
        "c_ctx": nrm((D,), 1.0),
        "ada_w": nrm((L, D, 6 * D), 0.5 * D ** -0.5),
        "ada_b": nrm((L, 6 * D), 0.02),
        "w_in": nrm((L, D, IN_COLS), D ** -0.5),
        "hgrn_lb_logits": nrm((2, L, W), 0.5),
        "hgrn_norm_w": gain((L, W)),
        "gdn_conv_w": nrm((L, CONV_K, 3 * W), CONV_K ** -0.5),
        "gdn_a_log": a_log_init((L, 2, H)),
        "gdn_dt_bias": dt_bias_init((L, 2, H)),
        "gdn_norm_w": gain((L, W)),
        "ssd_conv_w": nrm((L, CONV_K, XBC), CONV_K ** -0.5),
        "ssd_conv_b": nrm((L, XBC), 0.02),
        "ssd_a_log": a_log_init((L, 2, H)),
        "ssd_dt_bias": dt_bias_init((L, 2, H)),
        "ssd_d": gain((L, H)),
        "ssd_norm_w": gain((L, W)),
        "w_gate": nrm((L, N_BRANCH, D, D), D ** -0.5),
        "b_gate": nrm((L, N_BRANCH, D), 0.02),
        "w_branch": nrm((L, N_BRANCH, W, D), W ** -0.5),
        "w_out": nrm((L, D, D), DEEPNORM_BETA * D ** -0.5),
        "ln1_g": gain((L, D)),
        "ln1_b": nrm((L, D), 0.02),
        "w_router": nrm((L, D, E), D ** -0.5),
        "w_ff_gate": nrm((L, E, D, F), D ** -0.5),
        "w_ff_up": nrm((L, E, D, F), D ** -0.5),
        "w_ff_down": nrm((L, E, F, D), DEEPNORM_BETA * F ** -0.5),
        "ln2_g": gain((L, D)),
        "ln2_b": nrm((L, D), 0.02),
    }


def reference(x, c, ctx, c_ctx, ada_w, ada_b, w_in, hgrn_lb_logits, hgrn_norm_w,
              gdn_conv_w, gdn_a_log, gdn_dt_bias, gdn_norm_w,
              ssd_conv_w, ssd_conv_b, ssd_a_log, ssd_dt_bias, ssd_d, ssd_norm_w,
              w_gate, b_gate, w_branch, w_out, ln1_g, ln1_b,
              w_router, w_ff_gate, w_ff_up, w_ff_down, ln2_g, ln2_b):
    n_lat = x.shape[1]
    rows = n_lat // GRID_W
    xl = x + sincos_grid(rows, GRID_W, D_MODEL).astype(x.dtype)
    xc = ctx
    lb_all = hgrn_lower_bounds(hgrn_lb_logits)
    for l in range(DEPTH):
        with_ctx = l < DEPTH - 1
        mod_l = jax.nn.silu(c) @ ada_w[l] + ada_b[l]
        mod_c = jax.nn.silu(c_ctx) @ ada_w[l] + ada_b[l]
        sh1_l, sc1_l, g1_l, sh2_l, sc2_l, g2_l = jnp.split(mod_l[:, None, :], 6, axis=-1)
        sh1_c, sc1_c, g1_c, sh2_c, sc2_c, g2_c = jnp.split(mod_c, 6, axis=-1)
        hl = modulate(xl, sh1_l, sc1_l)
        hc = modulate(xc, sh1_c, sc1_c)
        y_c, y_l = token_mixer(hc, hl, w_in[l], lb_all[:, l], hgrn_norm_w[l],
                               gdn_conv_w[l], gdn_a_log[l], gdn_dt_bias[l], gdn_norm_w[l],
                               ssd_conv_w[l], ssd_conv_b[l], ssd_a_log[l], ssd_dt_bias[l], ssd_d[l], ssd_norm_w[l],
                               w_gate[l], b_gate[l], w_branch[l], w_out[l], with_ctx)
        xl = post_norm(DEEPNORM_ALPHA * xl + g1_l * y_l, ln1_g[l], ln1_b[l])
        hl2 = modulate(xl, sh2_l, sc2_l)
        xl = post_norm(DEEPNORM_ALPHA * xl + g2_l * expert_choice_ffn(hl2, w_router[l], w_ff_gate[l], w_ff_up[l], w_ff_down[l]),
                       ln2_g[l], ln2_b[l])
        if with_ctx:
            xc = post_norm(DEEPNORM_ALPHA * xc + g1_c * y_c, ln1_g[l], ln1_b[l])
            hc2 = modulate(xc, sh2_c, sc2_c)
            xc = post_norm(DEEPNORM_ALPHA * xc + g2_c * expert_choice_ffn(hc2, w_router[l], w_ff_gate[l], w_ff_up[l], w_ff_down[l]),
                           ln2_g[l], ln2_b[l])
    return xl
```

```python
import math
from contextlib import ExitStack
import numpy as np
import concourse.bass as bass
import concourse.mybir as mybir
from concourse.bass_utils import run_bass_kernel_spmd

F32 = mybir.dt.float32
I32 = mybir.dt.int32
U32 = mybir.dt.uint32
AF = mybir.ActivationFunctionType
ALU = mybir.AluOpType
AX = mybir.AxisListType

D = 1024
W = 256
H = 4
HD = 64
E = 16
FF = 1024
IN_COLS = 3352
EPS = 1e-6
NDMA_SEMS = 40


class Res:
    __slots__ = ("name", "w", "r", "psum")

    def __init__(self, name):
        self.name = name
        self.w = None
        self.r = {}
        self.psum = False


class Sched:
    ENGS = ("pe", "dve", "act", "pool", "sp")

    def __init__(self):
        self.q = {e: [] for e in self.ENGS}
        self.cnt = {e: 0 for e in self.ENGS}
        self.seen = {e: {} for e in self.ENGS}
        self.dcnt = [0] * NDMA_SEMS
        self.dnext = 0
        self.n = 0

    def _waits(self, eng, reads, writes, extra=()):
        waits = {}

        def need(kv):
            if kv is None:
                return
            k, v = kv
            if v > waits.get(k, 0):
                waits[k] = v
        for R in reads:
            need(R.w)
            if R.psum:
                for k, v in R.r.items():
                    if k != eng:
                        need((k, v))
        for Wr in writes:
            need(Wr.w)
            for k, v in Wr.r.items():
                need((k, v))
        for kv in extra:
            need(kv)
        out = []
        seen = self.seen[eng]
        for k, v in waits.items():
            if seen.get(k, 0) < v:
                seen[k] = v
                out.append((k, v))
        return out

    def op(self, eng, fn, reads=(), writes=()):
        waits = self._waits(eng, reads, writes)
        if eng == "pe":
            waits = [kv for kv in waits if kv[0] != "pe"]
        self.cnt[eng] += 1
        c = self.cnt[eng]
        self.q[eng].append((waits, fn, (eng, 1)))
        for R in reads:
            if R.r.get(eng, 0) < c:
                R.r[eng] = c
        for Wr in writes:
            Wr.w = (eng, c)
            Wr.r = {}
        self.n += 1

    def dma(self, eng, out_ap, in_ap, reads=(), writes=(), **kw):
        s = self.dnext
        self.dnext = (self.dnext + 1) % NDMA_SEMS
        key = ("d", s)
        prev = self.dcnt[s]
        waits = self._waits(eng, reads, writes, extra=((key, prev),) if prev else ())
        self.dcnt[s] += 16
        c = self.dcnt[s]
        self.q[eng].append((waits, lambda e: e.dma_start(out=out_ap, in_=in_ap, **kw), (key, 16)))
        for R in reads:
            R.r[key] = c
        for Wr in writes:
            Wr.w = (key, c)
            Wr.r = {}
        self.n += 1

    def barrier(self):
        for eng in self.ENGS:
            waits = []
            seen = self.seen[eng]
            for o in self.ENGS:
                if o != eng and self.cnt[o] > seen.get(o, 0):
                    seen[o] = self.cnt[o]
                    waits.append((o, self.cnt[o]))
            for s in range(NDMA_SEMS):
                k = ("d", s)
                if self.dcnt[s] > seen.get(k, 0):
                    seen[k] = self.dcnt[s]
                    waits.append((k, self.dcnt[s]))
            if waits:
                self.q[eng].append((waits, None, None))

    def emit(self, nc, es):
        sems = {e: es.enter_context(nc.semaphore("s_" + e)) for e in self.ENGS}
        for s in range(NDMA_SEMS):
            sems[("d", s)] = es.enter_context(nc.semaphore("sd%d" % s))
        self.barrier()
        block = es.enter_context(nc.Block())

        def run(e, lst):
            for waits, fn, inc in lst:
                for k, v in waits:
                    e.wait_ge(sems[k], v)
                if fn is not None:
                    fn(e).then_inc(sems[inc[0]], inc[1])

        @block.tensor
        def _(e):
            run(e, self.q["pe"])

        @block.vector
        def _(e):
            run(e, self.q["dve"])

        @block.scalar
        def _(e):
            run(e, self.q["act"])

        @block.gpsimd
        def _(e):
            run(e, self.q["pool"])

        @block.sync
        def _(e):
            run(e, self.q["sp"])


class T:
    def __init__(self, ap, name):
        self.ap = ap
        self.res = Res(name)

    def __getitem__(self, k):
        return self.ap[k]


class Ctx:
    def __init__(self, nc, es):
        self.nc = nc
        self.es = es
        self.S = Sched()
        self.uid = 0

    def sb(self, name, shape, dt=F32):
        self.uid += 1
        t = self.es.enter_context(self.nc.sbuf_tensor("%s_%d" % (name, self.uid), list(shape), dt))
        return T(t, name)

    def ps(self, name, shape=(128, 512), dt=F32):
        self.uid += 1
        t = self.es.enter_context(self.nc.psum_tensor("%s_%d" % (name, self.uid), list(shape), dt))
        r = T(t, name)
        r.res.psum = True
        return r

    def dram(self, name, shape, dt=F32, kind="Internal"):
        t = self.nc.dram_tensor(name, list(shape), dt, kind=kind)
        return T(t.ap(), name)

    def mm(self, out_t, out_ap, lhsT, rhs, reads, start=True, stop=True):
        rd = [x.res for x in reads]
        wr = [out_t.res]
        if not start:
            rd = rd + [out_t.res]
        self.S.op("pe", lambda e: e.matmul(out_ap, lhsT, rhs, start=start, stop=stop), rd, wr)

    def tr(self, out_t, out_ap, in_ap, ident, reads):
        self.S.op("pe", lambda e: e.transpose(out_ap, in_ap, ident.ap[:]), [x.res for x in reads] + [ident.res], [out_t.res])

    def v(self, eng, fn, reads, writes):
        self.S.op(eng, fn, [x.res for x in reads], [x.res for x in writes])

    def dma(self, eng, out_ap, in_ap, reads, writes, **kw):
        self.S.dma(eng, out_ap, in_ap, [x.res for x in reads], [x.res for x in writes], **kw)

    def i(self, eng, name, *args, R=(), Wr=(), **kw):
        self.S.op(eng, lambda e: getattr(e, name)(*args, **kw), [x.res for x in R], [x.res for x in Wr])


def sincos_grid_np(rows, cols, dim):
    quarter = dim // 4
    omega = (1.0 / (10000.0 ** (np.arange(quarter, dtype=np.float32) / np.float32(quarter)))).astype(np.float32)
    er = np.arange(rows, dtype=np.float32)[:, None] * omega
    ec = np.arange(cols, dtype=np.float32)[:, None] * omega
    er = np.concatenate([np.sin(er), np.cos(er)], axis=-1)
    ec = np.concatenate([np.sin(ec), np.cos(ec)], axis=-1)
    emb = np.concatenate([np.broadcast_to(er[:, None, :], (rows, cols, dim // 2)),
                          np.broadcast_to(ec[None, :, :], (rows, cols, dim // 2))], axis=-1)
    return np.ascontiguousarray(emb.reshape(rows * cols, dim).astype(np.float32))


class Cfg:
    def __init__(self, NS=2, TC=256, TL=2048, DEPTH=4):
        self.NS, self.TC, self.TL, self.DEPTH = NS, TC, TL, DEPTH
        self.T = TC + TL
        self.NT = self.T // 128
        self.NST = NS + 1
        self.parts = ("ssd", "fnet", "hgrn", "gdn", "merge", "moe")
        self.gdn_stage = 9


WNAMES = [("ada_w", (D, 6 * D)), ("ada_b", (6 * D,)), ("w_in", (D, IN_COLS))]


def layernorm(C, xt, out_t, st):
    nd = float(xt.ap.shape[-1])
    C.v("dve", lambda e: e.reduce_sum(st[:, 0:1], xt[:], AX.X), [xt], [st])
    C.v("act", lambda e: e.activation(out_t[:], xt[:], AF.Square, accum_out=st[:, 1:2]), [xt, st], [out_t, st])
    C.v("dve", lambda e: e.tensor_scalar(st[:, 2:3], st[:, 0:1], 1.0 / nd, None, ALU.mult), [st], [st])
    C.v("dve", lambda e: e.tensor_tensor(st[:, 3:4], st[:, 2:3], st[:, 2:3], ALU.mult), [st], [st])
    C.v("dve", lambda e: e.scalar_tensor_tensor(st[:, 4:5], st[:, 1:2], 1.0 / nd, st[:, 3:4], ALU.mult, ALU.subtract), [st], [st])
    C.v("act", lambda e: e.activation(st[:, 5:6], st[:, 4:5], AF.Ln, bias=EPS), [st], [st])
    C.v("act", lambda e: e.activation(st[:, 5:6], st[:, 5:6], AF.Exp, scale=-0.5), [st], [st])
    C.v("dve", lambda e: e.scalar_tensor_tensor(st[:, 6:7], st[:, 2:3], -1.0, st[:, 5:6], ALU.mult, ALU.mult), [st], [st])
    C.v("act", lambda e: e.activation(out_t[:], xt[:], AF.Identity, bias=st[:, 6:7], scale=st[:, 5:6]), [xt, st], [out_t])


class Arena:
    def __init__(self, C, cols):
        self.C = C
        self.t = C.es.enter_context(C.nc.sbuf_tensor("arena", [128, cols], F32))
        self.cols = cols
        self.off = 0

    def alloc(self, name, cols, parts=128):
        assert self.off + cols <= self.cols, "SBUF arena overflow at %s (%d + %d)" % (name, self.off, cols)
        t = T(self.t[0:parts, self.off:self.off + cols], name)
        self.off += cols
        return t

    def mark(self):
        return self.off

    def release(self, mark):
        self.C.S.barrier()
        self.off = mark


def make_consts():
    t = np.arange(128)
    c = {}
    c["ident"] = np.eye(128, dtype=np.float32)
    c["ones"] = np.ones((128, 128), np.float32)
    c["tri_f"] = (t[:, None] <= t[None, :]).astype(np.float32)
    c["tri_b"] = (t[:, None] >= t[None, :]).astype(np.float32)
    c["tris_f"] = (t[:, None] > t[None, :]).astype(np.float32)
    c["tris_b"] = (t[:, None] < t[None, :]).astype(np.float32)
    NEG = -30000.0
    negf = np.where(t[None, :] >= t[:, None], 0.0, NEG).astype(np.float32)
    negb = np.where(t[None, :] <= t[:, None], 0.0, NEG).astype(np.float32)
    c["neg4_f"] = np.tile(negf, (1, 4))
    c["neg4_b"] = np.tile(negb, (1, 4))
    c.update(make_gla_consts())
    c.update(make_gdn_consts())
    c.update(make_moe_consts())
    names = list(c.keys())
    offs = {}
    o = 0
    for n in names:
        offs[n] = (o, c[n].shape[1])
        o += c[n].shape[1]
    return np.ascontiguousarray(np.concatenate([c[n] for n in names], axis=1)), offs


def ssd_mixer(C, A, PS, K, cfg, l, s, ZT, ZF, O, W_):
    TC, TT, NT = cfg.TC, cfg.T, cfg.NT
    NTC = TC // 128
    m0 = A.mark()
    prm = A.alloc("ssd_prm", 64)
    C.dma("sp", prm[:, 0:8], W_["ssd_dt_bias"][l:l + 1, :].partition_broadcast(128), [W_["ssd_dt_bias"]], [prm])
    C.dma("sp", prm[:, 8:16], W_["ssd_a_log"][l:l + 1, :].partition_broadcast(128), [W_["ssd_a_log"]], [prm])
    C.dma("sp", prm[:, 16:20], W_["ssd_d"][l:l + 1, :].partition_broadcast(128), [W_["ssd_d"]], [prm])
    C.v("act", lambda e: e.activation(prm[:, 8:16], prm[:, 8:16], AF.Exp), [prm], [prm])
    C.v("dve", lambda e: e.tensor_scalar(prm[:, 8:16], prm[:, 8:16], -1.0, None, ALU.mult), [prm], [prm])
    nw = A.alloc("ssd_nw", 256)
    C.dma("sp", nw[:], W_["ssd_norm_w"][l:l + 1, :].partition_broadcast(128), [W_["ssd_norm_w"]], [nw])
    cw = A.alloc("ssd_cw", 6 * 6)
    grp = [(0, 128), (128, 128), (256, 64), (320, 64), (384, 64), (448, 64)]
    cv = [A.alloc("ssd_cv%d" % g, TT) for g in range(6)]
    raw = A.alloc("ssd_raw", TT)
    for g, (c0, n) in enumerate(grp):
        C.dma("sp", cw[0:n, g * 6:g * 6 + 5], W_["ssd_cwT"][l, c0:c0 + n, :], [W_["ssd_cwT"]], [cw])
        C.dma("sp", cw[0:n, g * 6 + 5:g * 6 + 6], W_["ssd_cbT"][l, c0:c0 + n, :], [W_["ssd_cbT"]], [cw])
        C.dma("sp", raw[0:n, :], ZF[s, 14 * 128 + c0: 14 * 128 + c0 + n, :], [ZF], [raw])
        acc = cv[g]
        C.v("dve", lambda e, acc=acc, n=n, g=g: e.tensor_scalar(acc[0:n, :], raw[0:n, :], cw[0:n, g * 6 + 2:g * 6 + 3], None, ALU.mult),
            [raw, cw], [acc])
        for (a, b) in ((0, TC), (TC, TT)):
            for k in (0, 1, 3, 4):
                sh = k - 2
                lo, hi = max(a, a - sh), min(b, b - sh)
                C.v("dve", lambda e, acc=acc, n=n, g=g, k=k, lo=lo, hi=hi, sh=sh: e.scalar_tensor_tensor(
                    acc[0:n, lo:hi], raw[0:n, lo + sh:hi + sh], cw[0:n, g * 6 + k:g * 6 + k + 1], acc[0:n, lo:hi], ALU.mult, ALU.add),
                    [raw, cw, acc], [acc])
        C.v("act", lambda e, acc=acc, n=n, g=g: e.activation(acc[0:n, :], acc[0:n, :], AF.Silu, bias=cw[0:n, g * 6 + 5:g * 6 + 6]),
            [acc, cw], [acc])
    dtA = A.alloc("ssd_dt", NT * 8)
    laA = A.alloc("ssd_la", NT * 8)
    C.dma("sp", dtA[:].rearrange("p (t c) -> p t c", c=8), ZT[s, :, 1552:1560].rearrange("(t p) c -> p t c", p=128), [ZT], [dtA])
    dt3 = dtA[:].rearrange("p (t c) -> p t c", c=8)
    la3 = laA[:].rearrange("p (t c) -> p t c", c=8)
    C.v("dve", lambda e: e.tensor_tensor(dt3, dt3, prm[:, 0:8].unsqueeze(1).to_broadcast([128, NT, 8]), ALU.add), [dtA, prm], [dtA])
    C.v("act", lambda e: e.activation(dtA[:], dtA[:], AF.Exp), [dtA], [dtA])
    C.v("act", lambda e: e.activation(dtA[:], dtA[:], AF.Ln, bias=1.0), [dtA], [dtA])
    C.v("dve", lambda e: e.tensor_tensor(la3, dt3, prm[:, 8:16].unsqueeze(1).to_broadcast([128, NT, 8]), ALU.mult), [dtA, prm], [laA])
    OF = A.alloc("ssd_of", NT * 256)
    XS = A.alloc("ssd_xs", NT * 256)
    Sst = A.alloc("ssd_S", 256)
    R4 = A.alloc("ssd_R4", 512)
    LT = A.alloc("ssd_LT", 512)
    Eg = A.alloc("ssd_Eg", 512)
    AT = A.alloc("ssd_AT", 512)
    qd = A.alloc("ssd_qd", 512)
    vv = A.alloc("ssd_v", 256)
    kd = A.alloc("ssd_kd", 256)
    bst = A.alloc("ssd_bst", 128)
    sm = A.alloc("ssd_sm", 16)
    zt = A.alloc("ssd_z", 256)
    yy = A.alloc("ssd_y", 256)
    y2 = A.alloc("ssd_y2", 256)
    pG, pU, pS, pO, pT, pD = PS[0], PS[1], PS[2], PS[3], PS[4], PS[5]
    for d in range(2):
        sfx = "_f" if d == 0 else "_b"
        tri, tris, neg4 = K("tri" + sfx), K("tris" + sfx), K("neg4" + sfx)
        last = 127 if d == 0 else 0
        order = list(range(NT)) if d == 0 else (list(range(NTC - 1, -1, -1)) + list(range(NT - 1, NTC - 1, -1)))
        C.v("dve", lambda e: e.memset(Sst[0:64, :], 0.0), [], [Sst])
        for t in order:
            ts = slice(t * 128, (t + 1) * 128)
            la = laA[:, t * 8 + d * 4: t * 8 + d * 4 + 4]
            dt = dtA[:, t * 8 + d * 4: t * 8 + d * 4 + 4]
            if d == 0:
                for c in range(2):
                    C.tr(pT, pT[:, c * 128:(c + 1) * 128], cv[c][:, ts], K.ident, [cv[c]])
                C.v("act", lambda e, t=t: e.copy(XS[:, t * 256:(t + 1) * 256], pT[:, 0:256]), [pT], [XS])
            xs = XS[:, t * 256:(t + 1) * 256]
            for g in range(2):
                C.S.op("pe", lambda e, g=g, ts=ts: e.transpose(pT[:, 256 + g * 64: 256 + (g + 1) * 64], cv[2 + g][0:64, ts], K.ident[0:64, 0:64]),
                       [cv[2 + g].res, K.t.res], [pT.res])
            C.v("act", lambda e: e.copy(bst[:], pT[:, 256:384]), [pT], [bst])
            C.v("dve", lambda e, xs=xs, dt=dt: e.tensor_tensor(vv[:].rearrange("p (h e) -> p h e", h=4), xs.rearrange("p (h e) -> p h e", h=4),
                                                               dt.unsqueeze(2).to_broadcast([128, 4, 64]), ALU.mult), [XS, dtA], [vv])
            C.v("dve", lambda e, la=la, tri=tri: e.tensor_tensor(R4[:].rearrange("p (h i) -> p h i", h=4), tri.unsqueeze(1).to_broadcast([128, 4, 128]),
                                                                 la.unsqueeze(2).to_broadcast([128, 4, 128]), ALU.mult), [K.t, laA], [R4])
            C.mm(pG, pG[:, :], K("ones"), R4[:], [K.t, R4], start=True, stop=False)
            C.mm(pG, pG[:, :], K("ident"), neg4, [K.t], start=False, stop=True)
            C.mm(pU, pU[:, :], K("ones"), R4[:], [K.t, R4])
            C.mm(pS, pS[:, 0:4], tri, la, [K.t, laA], start=True, stop=True)
            C.mm(pS, pS[:, 4:8], tris, la, [K.t, laA], start=True, stop=True)
            C.v("dve", lambda e: e.tensor_scalar(sm[:, 0:4], pS[:, 0:4], -1.0, None, ALU.mult), [pS], [sm])
            C.v("act", lambda e: e.activation(sm[:, 4:8], pS[:, 4:8], AF.Exp), [pS], [sm])
            for h in range(4):
                C.v("act", lambda e, h=h: e.activation(LT[:, h * 128:(h + 1) * 128], pG[:, h * 128:(h + 1) * 128], AF.Exp, bias=sm[:, h:h + 1]),
                    [pG, sm], [LT])
            C.v("act", lambda e: e.activation(Eg[0:64, :], pU[0:64, :], AF.Exp), [pU], [Eg])
            for g in range(2):
                C.mm(pS, pS[:, 128 + g * 128: 256 + g * 128], cv[2 + g][0:64, ts], cv[4 + g][0:64, ts], [cv[2 + g], cv[4 + g]])
            C.v("dve", lambda e: e.tensor_tensor(AT[:].rearrange("p (g a i) -> p g a i", g=2, a=2),
                                                 pS[:, 128:384].rearrange("p (g i) -> p g i", g=2).unsqueeze(2).to_broadcast([128, 2, 2, 128]),
                                                 LT[:].rearrange("p (g a i) -> p g a i", g=2, a=2), ALU.mult), [pS, LT], [AT])
            for h in range(4):
                C.v("dve", lambda e, h=h, ts=ts: e.tensor_tensor(
                    qd[0:64, h * 128:(h + 1) * 128], cv[4 + h // 2][0:64, ts], Eg[0:64, h * 128:(h + 1) * 128], ALU.mult),
                    [cv[4 + h // 2], Eg], [qd])
            C.v("dve", lambda e: e.tensor_tensor(kd[:].rearrange("p (g a n) -> p g a n", g=2, a=2),
                                                 bst[:].rearrange("p (g n) -> p g n", g=2).unsqueeze(2).to_broadcast([128, 2, 2, 64]),
                                                 sm[:, 4:8].rearrange("p (g a) -> p g a", g=2).unsqueeze(3).to_broadcast([128, 2, 2, 64]), ALU.mult),
                [bst, sm], [kd])
            for h in range(4):
                C.mm(pO, pO[:, h * 64:(h + 1) * 64], AT[:, h * 128:(h + 1) * 128], vv[:, h * 64:(h + 1) * 64], [AT, vv], start=True, stop=False)
                C.mm(pO, pO[:, h * 64:(h + 1) * 64], qd[0:64, h * 128:(h + 1) * 128], Sst[0:64, h * 64:(h + 1) * 64], [qd, Sst], start=False, stop=True)
            for h in range(4):
                C.mm(pD, pD[0:64, h * 64:(h + 1) * 64], kd[:, h * 64:(h + 1) * 64], vv[:, h * 64:(h + 1) * 64], [kd, vv])
            C.v("dve", lambda e, last=last: e.tensor_tensor(Sst[0:64, :].rearrange("p (h e) -> p h e", h=4), Sst[0:64, :].rearrange("p (h e) -> p h e", h=4),
                                                            Eg[0:64, :].rearrange("p (h i) -> p h i", h=4)[:, :, last:last + 1].to_broadcast([64, 4, 64]), ALU.mult),
                [Sst, Eg], [Sst])
            C.v("dve", lambda e: e.tensor_tensor(Sst[0:64, :], Sst[0:64, :], pD[0:64, 0:256], ALU.add), [Sst, pD], [Sst])
            if d == 0:
                C.v("act", lambda e, t=t: e.copy(OF[:, t * 256:(t + 1) * 256], pO[:, 0:256]), [pO], [OF])
            else:
                C.dma("sp", zt[:], ZT[s, t * 128:(t + 1) * 128, 1296:1552], [ZT], [zt])
                C.v("dve", lambda e, t=t: e.tensor_tensor(yy[:], pO[:, 0:256], OF[:, t * 256:(t + 1) * 256], ALU.add), [pO, OF], [yy])
                C.v("dve", lambda e, xs=xs: e.tensor_tensor(y2[:].rearrange("p (h e) -> p h e", h=4), xs.rearrange("p (h e) -> p h e", h=4),
                                                            prm[:, 16:20].unsqueeze(2).to_broadcast([128, 4, 64]), ALU.mult), [XS, prm], [y2])
                C.v("dve", lambda e: e.tensor_tensor(yy[:], yy[:], y2[:], ALU.add), [yy, y2], [yy])
                C.v("act", lambda e: e.activation(zt[:], zt[:], AF.Silu), [zt], [zt])
                C.v("dve", lambda e: e.tensor_tensor(yy[:], yy[:], zt[:], ALU.mult), [yy, zt], [yy])
                C.v("act", lambda e: e.activation(y2[:], yy[:], AF.Square, accum_out=sm[:, 8:9]), [yy, sm], [y2, sm])
                C.v("act", lambda e: e.activation(sm[:, 9:10], sm[:, 8:9], AF.Ln, bias=EPS, scale=1.0 / 256.0), [sm], [sm])
                C.v("act", lambda e: e.activation(sm[:, 9:10], sm[:, 9:10], AF.Exp, scale=-0.5), [sm], [sm])
                C.v("dve", lambda e: e.scalar_tensor_tensor(y2[:], yy[:], sm[:, 9:10], nw[:], ALU.mult, ALU.mult), [yy, sm, nw], [y2])
                C.dma("sp", O[s, t * 128:(t + 1) * 128, 768:1024], y2[:], [y2], [O])
    A.release(m0)


def make_dft(Tn):
    NTs = Tn // 128
    t = np.arange(Tn, dtype=np.float64)
    ang = 2.0 * np.pi * ((t[:, None] * t[None, :]) % Tn) / Tn
    sc = 1.0 / math.sqrt(64.0 * Tn)
    m = np.stack([np.cos(ang) * sc, -np.sin(ang) * sc]).astype(np.float32)
    m = m.reshape(2, NTs, 128, NTs, 128).transpose(0, 3, 2, 1, 4)
    return np.ascontiguousarray(m)


def make_bd64():
    ch = np.arange(256)
    same = (ch[:, None] // 64) == (ch[None, :] // 64)
    ang = 2.0 * np.pi * (((ch[:, None] % 64) * (ch[None, :] % 64)) % 64) / 64.0
    m = np.stack([np.where(same, np.cos(ang), 0.0), np.where(same, np.sin(ang), 0.0)]).astype(np.float32)
    return np.ascontiguousarray(m.reshape(2, 2, 128, 256))


def fnet_mixer(C, A, PS, K, cfg, l, s, ZF, O, W_):
    TC, TT = cfg.TC, cfg.T
    m0 = A.mark()
    bd = A.alloc("fn_bd", 1024)
    bd4 = bd[:].rearrange("p (a k n) -> p a k n", a=2, k=2)
    C.dma("sp", bd4, W_["bd64"][:, :, :, :].rearrange("a k p n -> p a k n"), [W_["bd64"]], [bd])
    yos = [A.alloc("fn_yo", 256) for _ in range(2)]
    m1 = A.mark()
    for name, a, b in (("ctx", 0, TC), ("lat", TC, TT)):
        Tn = b - a
        NTs = Tn // 128
        cu = A.alloc("fn_cu", 2 * Tn)
        cu3 = cu[:].rearrange("p (k t) -> p k t", k=2)
        for kc in range(2):
            C.dma("sp", cu3[:, kc, :], ZF[s, (12 + kc) * 128:(13 + kc) * 128, a:b], [ZF], [cu])
        XCS = A.alloc("fn_xcs", NTs * 512)
        X4 = XCS[:].rearrange("p (t a n) -> p t a n", t=NTs, a=2)
        for tt in range(NTs):
            for cs in range(2):
                ps = PS[cs]
                for kc in range(2):
                    C.mm(ps, ps[:, 0:256], cu3[:, kc, tt * 128:(tt + 1) * 128], bd4[:, cs, kc, :], [cu, bd], start=(kc == 0), stop=(kc == 1))
                C.v("dve" if cs == 0 else "act",
                    (lambda e, ps=ps, tt=tt, cs=cs, X4=X4: e.tensor_copy(X4[:, tt, cs, :], ps[:, 0:256])) if cs == 0 else
                    (lambda e, ps=ps, tt=tt, cs=cs, X4=X4: e.copy(X4[:, tt, cs, :], ps[:, 0:256])), [ps], [XCS])
        dms = [A.alloc("fn_dm", 2 * NTs * 128) for _ in range(2)]
        for kt in range(NTs):
            dm = dms[kt % 2]
            dm4 = dm[:].rearrange("p (a t k) -> p a t k", a=2, t=NTs)
            C.dma("sp", dm4, W_["dft_" + name][:, kt, :, :, :].rearrange("a p t k -> p a t k"), [W_["dft_" + name]], [dm])
            ps = PS[2 + kt % 2]
            n = 0
            for cs in range(2):
                for tt in range(NTs):
                    C.mm(ps, ps[:, 0:256], dm4[:, cs, tt, :], X4[:, tt, cs, :], [dm, XCS], start=(n == 0), stop=(n == 2 * NTs - 1))
                    n += 1
            yo = yos[kt % 2]
            C.v("dve", lambda e, ps=ps, yo=yo: e.tensor_copy(yo[:], ps[:, 0:256]), [ps], [yo])
            C.dma("sp", O[s, a + kt * 128: a + (kt + 1) * 128, 512:768], yo[:], [yo], [O])
        A.release(m1)
    A.release(m0)


def make_gla_consts():
    t = np.arange(128)
    WQ = np.zeros((128, 5, 128), np.float32)
    WK = np.zeros((128, 4, 128), np.float32)
    MK = np.zeros((128, 4, 128), np.float32)
    T_, I_ = t[:, None], t[None, :]
    r = 16 * (I_ // 16)
    WQ[:, 0, :] = ((T_ >= r) & (T_ <= I_))
    WK[:, 0, :] = -((T_ >= r) & (T_ <= I_)).astype(np.float32)
    MK[:, 0, :] = ((T_ // 16 == I_ // 16) & (T_ <= I_))
    for lev, B in ((1, 32), (2, 64), (3, 128)):
        m = B * (I_ // B) + B // 2
        WQ[:, lev, :] = ((I_ >= m) & (T_ >= m) & (T_ <= I_))
        WK[:, lev, :] = ((I_ < m) & (T_ > I_) & (T_ < m))
        mj = B * (T_ // B) + B // 2
        MK[:, lev, :] = ((T_ // B == I_ // B) & (T_ < mj) & (I_ >= mj))
    WQ[:, 4, :] = (T_ <= I_)
    out = {}
    for sfx, rev in (("_f", False), ("_b", True)):
        wq, wk, mk = (WQ[::-1, :, ::-1], WK[::-1, :, ::-1], MK[::-1, :, ::-1]) if rev else (WQ, WK, MK)
        out["wq" + sfx] = np.ascontiguousarray(np.concatenate([wq.reshape(128, 640), np.ones((128, 1), np.float32)], 1))
        out["wk" + sfx] = np.ascontiguousarray(wk.reshape(128, 512))
        out["mk" + sfx] = np.ascontiguousarray(mk.reshape(128, 512))
    return out


def hgrn_mixer(C, A, PS, K, cfg, l, s, ZT, ZF, O, W_):
    TC, TT, NT, DEPTH = cfg.TC, cfg.T, cfg.NT, cfg.DEPTH
    NTC = TC // 128
    m0 = A.mark()
    nw = A.alloc("hg_nw", 256)
    C.dma("sp", nw[:], W_["hgrn_norm_w"][l:l + 1, :].partition_broadcast(128), [W_["hgrn_norm_w"]], [nw])
    qT = A.alloc("hg_qT", 2 * TT)
    for c in range(2):
        C.dma("sp", qT[:, c * TT:(c + 1) * TT], ZF[s, c * 128:(c + 1) * 128, :], [ZF], [qT])
    C.i("act", "activation", qT[:], qT[:], AF.Silu, R=[qT], Wr=[qT])
    vT = A.alloc("hg_v", NT * 256)
    C.dma("sp", vT[:].rearrange("p (t c) -> p t c", c=256), ZT[s, :, 512:768].rearrange("(t p) c -> p t c", p=128), [ZT], [vT])
    OF = A.alloc("hg_of", NT * 256)
    kT = A.alloc("hg_kT", 2 * TT)
    lf = A.alloc("hg_lf", NT * 256)
    kt = A.alloc("hg_kt", NT * 256)
    lbB = A.alloc("hg_lb", 256)
    omB = A.alloc("hg_om", 256)
    lg = A.alloc("hg_lg", DEPTH * 256)
    omc = A.alloc("hg_omc", 4)
    Sst = [A.alloc("hg_S%d" % i, 64) for i in range(2)]
    EQ = A.alloc("hg_EQ", 512)
    EX = A.alloc("hg_EX", 129)
    EK = A.alloc("hg_EK", 512)
    qS = A.alloc("hg_qS", 128)
    kh = A.alloc("hg_kh", 128)
    AM = A.alloc("hg_AM", 512)
    yy = A.alloc("hg_y", 256)
    y2 = A.alloc("hg_y2", 256)
    gt = A.alloc("hg_g", 256)
    sm = A.alloc("hg_sm", 8)
    pEq, pEx, pEk, pKh, pAt, pO, pD, pT = PS
    for d in range(2):
        sfx = "_f" if d == 0 else "_b"
        wq, wk, mk, tris = K("wq" + sfx), K("wk" + sfx), K("mk" + sfx), K("tris" + sfx)
        C.dma("sp", lg[:].rearrange("p (j c) -> p j c", j=DEPTH), W_["hgrn_lb_logits"][d:d + 1, :, :].partition_broadcast(128),
              [W_["hgrn_lb_logits"]], [lg])
        C.i("act", "activation", lg[:], lg[:], AF.Exp, R=[lg], Wr=[lg])
        C.i("dve", "tensor_copy", omB[:], lg[:, 0:256], R=[lg], Wr=[omB])
        for j in range(1, DEPTH):
            C.i("dve", "tensor_tensor", omB[:], omB[:], lg[:, j * 256:(j + 1) * 256], ALU.add, R=[lg, omB], Wr=[omB])
        C.i("dve", "reciprocal", omB[:], omB[:], R=[omB], Wr=[omB])
        C.i("dve", "memset", lbB[:], 0.0, Wr=[lbB])
        for j in range(1, l + 1):
            C.i("dve", "tensor_tensor", lbB[:], lbB[:], lg[:, j * 256:(j + 1) * 256], ALU.add, R=[lg, lbB], Wr=[lbB])
        C.i("dve", "tensor_tensor", lbB[:], lbB[:], omB[:], ALU.mult, R=[lbB, omB], Wr=[lbB])
        C.i("dve", "tensor_scalar", omB[:], lbB[:], -1.0, 1.0, ALU.mult, ALU.add, R=[lbB], Wr=[omB])
        for c in range(2):
            C.tr(pT, pT[:, 0:128], omB[:, c * 128:(c + 1) * 128], K.ident, [omB])
            C.i("dve", "tensor_copy", omc[:, c:c + 1], pT[:, 0:1], R=[pT], Wr=[omc])
            C.i("dve", "tensor_scalar", omc[:, 2 + c:3 + c], pT[:, 0:1], -1.0, None, ALU.mult, R=[pT], Wr=[omc])
        for c in range(2):
            C.dma("sp", kT[:, c * TT:(c + 1) * TT], ZF[s, (2 + 2 * d + c) * 128:(3 + 2 * d + c) * 128, :], [ZF], [kT])
            C.i("act", "activation", kT[:, c * TT:(c + 1) * TT], kT[:, c * TT:(c + 1) * TT], AF.Sigmoid, R=[kT], Wr=[kT])
            C.i("dve", "tensor_scalar", kT[:, c * TT:(c + 1) * TT], kT[:, c * TT:(c + 1) * TT], omc[:, 2 + c:3 + c], omc[:, c:c + 1],
                ALU.mult, ALU.add, R=[kT, omc], Wr=[kT])
        C.dma("sp", lf[:].rearrange("p (t c) -> p t c", c=256), ZT[s, :, d * 256:(d + 1) * 256].rearrange("(t p) c -> p t c", p=128), [ZT], [lf])
        C.i("act", "activation", lf[:], lf[:], AF.Sigmoid, R=[lf], Wr=[lf])
        lf3 = lf[:].rearrange("p (t c) -> p t c", c=256)
        kt3 = kt[:].rearrange("p (t c) -> p t c", c=256)
        omB3 = omB[:].unsqueeze(1).to_broadcast([128, NT, 256])
        C.i("dve", "tensor_tensor", lf3, lf3, omB3, ALU.mult, R=[lf, omB], Wr=[lf])
        C.i("dve", "tensor_tensor", kt3, omB3, lf3, ALU.subtract, R=[lf, omB], Wr=[kt])
        C.i("dve", "tensor_tensor", lf3, lf3, lbB[:].unsqueeze(1).to_broadcast([128, NT, 256]), ALU.add, R=[lf, lbB], Wr=[lf])
        C.i("act", "activation", lf[:], lf[:], AF.Ln, R=[lf], Wr=[lf])
        order = list(range(NT)) if d == 0 else (list(range(NTC - 1, -1, -1)) + list(range(NT - 1, NTC - 1, -1)))
        for hp in range(2):
            C.i("dve", "memset", Sst[hp][:], 0.0, Wr=[Sst[hp]])
        for t in order:
            ts = slice(t * 128, (t + 1) * 128)
            for hp in range(2):
                lfs = lf[:, t * 256 + hp * 128: t * 256 + (hp + 1) * 128]
                qts = qT[:, hp * TT + t * 128: hp * TT + (t + 1) * 128]
                kts = kT[:, hp * TT + t * 128: hp * TT + (t + 1) * 128]
                C.mm(pEq, pEq[:, :], lfs, wq[:, 0:512], [lf, K.t])
                C.mm(pEx, pEx[:, 0:129], lfs, wq[:, 512:641], [lf, K.t])
                C.mm(pEk, pEk[:, :], lfs, wk, [lf, K.t])
                C.mm(pKh, pKh[:, 0:128], tris, lfs, [lf, K.t])
                C.i("act", "activation", EQ[:], pEq[:, :], AF.Exp, R=[pEq], Wr=[EQ])
                C.i("act", "activation", EX[:], pEx[:, 0:129], AF.Exp, R=[pEx], Wr=[EX])
                C.i("act", "activation", EK[:], pEk[:, :], AF.Exp, R=[pEk], Wr=[EK])
                C.i("act", "activation", kh[:], pKh[:, 0:128], AF.Exp, R=[pKh], Wr=[kh])
                C.i("dve", "tensor_tensor", EQ[:].rearrange("p (l i) -> p l i", l=4), EQ[:].rearrange("p (l i) -> p l i", l=4),
                    qts.unsqueeze(1).to_broadcast([128, 4, 128]), ALU.mult, R=[EQ, qT], Wr=[EQ])
                C.i("dve", "tensor_tensor", EK[:].rearrange("p (l i) -> p l i", l=4), EK[:].rearrange("p (l i) -> p l i", l=4),
                    kts.unsqueeze(1).to_broadcast([128, 4, 128]), ALU.mult, R=[EK, kT], Wr=[EK])
                C.i("dve", "tensor_tensor", qS[:], EX[:, 0:128], qts, ALU.mult, R=[EX, qT], Wr=[qS])
                C.i("dve", "tensor_tensor", kh[:], kh[:], kt[:, t * 256 + hp * 128: t * 256 + (hp + 1) * 128], ALU.mult, R=[kh, kt], Wr=[kh])
                for hh in range(2):
                    pr = slice(hh * 64, (hh + 1) * 64)
                    h = hp * 2 + hh
                    for lev in range(4):
                        C.mm(pAt, pAt[:, lev * 128:(lev + 1) * 128], EK[pr, lev * 128:(lev + 1) * 128], EQ[pr, lev * 128:(lev + 1) * 128], [EK, EQ])
                    C.i("dve", "tensor_tensor", AM[:], pAt[:, :], mk, ALU.mult, R=[pAt, K.t], Wr=[AM])
                    vs = vT[:, t * 256 + h * 64: t * 256 + (h + 1) * 64]
                    for lev in range(4):
                        C.mm(pO, pO[:, h * 64:(h + 1) * 64], AM[:, lev * 128:(lev + 1) * 128], vs, [AM, vT], start=(lev == 0), stop=False)
                    C.mm(pO, pO[:, h * 64:(h + 1) * 64], qS[pr, :], Sst[hp][pr, :], [qS, Sst[hp]], start=False, stop=True)
                C.mm(pD, pD[:, 0:128], kh[:], vT[:, t * 256 + hp * 128: t * 256 + (hp + 1) * 128], [kh, vT])
                for hh in range(2):
                    pr = slice(hh * 64, (hh + 1) * 64)
                    C.i("dve", "scalar_tensor_tensor", Sst[hp][pr, :], Sst[hp][pr, :], EX[pr, 128:129], pD[pr, hh * 64:(hh + 1) * 64],
                        ALU.mult, ALU.add, R=[Sst[hp], EX, pD], Wr=[Sst[hp]])
            if d == 0:
                C.i("act", "copy", OF[:, t * 256:(t + 1) * 256], pO[:, 0:256], R=[pO], Wr=[OF])
            else:
                C.dma("sp", gt[:], ZT[s, t * 128:(t + 1) * 128, 768:1024], [ZT], [gt])
                C.i("dve", "tensor_tensor", yy[:], pO[:, 0:256], OF[:, t * 256:(t + 1) * 256], ALU.add, R=[pO, OF], Wr=[yy])
                C.i("dve", "tensor_tensor", y2[:], yy[:], yy[:], ALU.mult, R=[yy], Wr=[y2])
                C.i("dve", "tensor_reduce", sm[:, 0:4], y2[:].rearrange("p (h e) -> p h e", h=4), AX.X, ALU.add, R=[y2], Wr=[sm])
                C.i("act", "activation", sm[:, 0:4], sm[:, 0:4], AF.Ln, bias=EPS, scale=1.0 / 64.0, R=[sm], Wr=[sm])
                C.i("act", "activation", sm[:, 0:4], sm[:, 0:4], AF.Exp, scale=-0.5, R=[sm], Wr=[sm])
                C.i("act", "activation", gt[:], gt[:], AF.Silu, R=[gt], Wr=[gt])
                C.i("dve", "tensor_tensor", yy[:].rearrange("p (h e) -> p h e", h=4), yy[:].rearrange("p (h e) -> p h e", h=4),
                    sm[:, 0:4].unsqueeze(2).to_broadcast([128, 4, 64]), ALU.mult, R=[yy, sm], Wr=[yy])
                C.i("dve", "tensor_tensor", yy[:], yy[:], nw[:], ALU.mult, R=[yy, nw], Wr=[yy])
                C.i("dve", "tensor_tensor", y2[:], yy[:], gt[:], ALU.mult, R=[yy, gt], Wr=[y2])
                C.dma("sp", O[s, t * 128:(t + 1) * 128, 0:256], y2[:], [y2], [O])
    A.release(m0)


def make_gdn_consts():
    t = np.arange(128)
    I_, J_ = t[:, None], t[None, :]
    c = {}
    for sfx, valid in (("_f", J_ < I_), ("_b", J_ > I_)):
        c["poss4" + sfx] = np.tile(np.where(valid, 0.0, 30000.0).astype(np.float32), (1, 4))
    bm = lambda b: (I_ // b == J_ // b)
    c["bm32"] = bm(32).astype(np.float32)
    c["be64"] = (bm(64) & ~bm(32)).astype(np.float32)
    c["be128"] = (~bm(64)).astype(np.float32)
    c["blk64"] = bm(64).astype(np.float32)
    return c


def gdn_mixer(C, A, PS, K, cfg, l, s, ZT, ZF, O, W_):
    TC, TT, NT = cfg.TC, cfg.T, cfg.NT
    NTC = TC // 128
    m0 = A.mark()
    prm = A.alloc("gd_prm", 32)
    C.dma("sp", prm[:, 0:8], W_["gdn_dt_bias"][l:l + 1, :].partition_broadcast(128), [W_["gdn_dt_bias"]], [prm])
    C.dma("sp", prm[:, 8:16], W_["gdn_a_log"][l:l + 1, :].partition_broadcast(128), [W_["gdn_a_log"]], [prm])
    C.i("act", "activation", prm[:, 8:16], prm[:, 8:16], AF.Exp, R=[prm], Wr=[prm])
    C.i("dve", "tensor_scalar", prm[:, 8:16], prm[:, 8:16], -1.0, None, ALU.mult, R=[prm], Wr=[prm])
    nw = A.alloc("gd_nw", 256)
    C.dma("sp", nw[:], W_["gdn_norm_w"][l:l + 1, :].partition_broadcast(128), [W_["gdn_norm_w"]], [nw])
    cw = A.alloc("gd_cw", 6 * 5)
    cv = [A.alloc("gd_cv%d" % g, TT) for g in range(6)]
    raw = A.alloc("gd_raw", TT)
    for g in range(6):
        C.dma("sp", cw[:, g * 5:g * 5 + 5], W_["gdn_cwT"][l, g * 128:(g + 1) * 128, :], [W_["gdn_cwT"]], [cw])
        C.dma("sp", raw[:], ZF[s, (6 + g) * 128:(7 + g) * 128, :], [ZF], [raw])
        acc = cv[g]
        C.i("dve", "tensor_scalar", acc[:], raw[:], cw[:, g * 5 + 2:g * 5 + 3], None, ALU.mult, R=[raw, cw], Wr=[acc])
        for (a, b) in ((0, TC), (TC, TT)):
            for k in (0, 1, 3, 4):
                sh = k - 2
                lo, hi = max(a, a - sh), min(b, b - sh)
                C.i("dve", "scalar_tensor_tensor", acc[:, lo:hi], raw[:, lo + sh:hi + sh], cw[:, g * 5 + k:g * 5 + k + 1], acc[:, lo:hi],
                    ALU.mult, ALU.add, R=[raw, cw, acc], Wr=[acc])
        C.i("act", "activation", acc[:], acc[:], AF.Silu, R=[acc], Wr=[acc])
    sq = A.alloc("gd_sq", 512)
    for g in range(4):
        for c0 in range(0, TT, 512):
            n = min(512, TT - c0)
            ps = PS[g % 2]
            C.i("act", "activation", sq[:, 0:n], cv[g][:, c0:c0 + n], AF.Square, R=[cv[g]], Wr=[sq])
            C.mm(ps, ps[:, 0:n], K("blk64"), sq[:, 0:n], [K.t, sq])
            C.i("act", "activation", sq[:, 0:n], ps[:, 0:n], AF.Ln, bias=EPS, R=[ps], Wr=[sq])
            C.i("act", "activation", sq[:, 0:n], sq[:, 0:n], AF.Exp, scale=-0.5, R=[sq], Wr=[sq])
            if g < 2:
                C.i("dve", "scalar_tensor_tensor", cv[g][:, c0:c0 + n], cv[g][:, c0:c0 + n], 0.125, sq[:, 0:n], ALU.mult, ALU.mult,
                    R=[cv[g], sq], Wr=[cv[g]])
            else:
                C.i("dve", "tensor_tensor", cv[g][:, c0:c0 + n], cv[g][:, c0:c0 + n], sq[:, 0:n], ALU.mult, R=[cv[g], sq], Wr=[cv[g]])
    laA = A.alloc("gd_la", NT * 8)
    btA = A.alloc("gd_bt", NT * 8)
    C.dma("sp", laA[:].rearrange("p (t c) -> p t c", c=8), ZT[s, :, 1280:1288].rearrange("(t p) c -> p t c", p=128), [ZT], [laA])
    C.dma("sp", btA[:].rearrange("p (t c) -> p t c", c=8), ZT[s, :, 1288:1296].rearrange("(t p) c -> p t c", p=128), [ZT], [btA])
    la3 = laA[:].rearrange("p (t c) -> p t c", c=8)
    C.i("dve", "tensor_tensor", la3, la3, prm[:, 0:8].unsqueeze(1).to_broadcast([128, NT, 8]), ALU.add, R=[laA, prm], Wr=[laA])
    C.i("act", "activation", laA[:], laA[:], AF.Exp, R=[laA], Wr=[laA])
    C.i("act", "activation", laA[:], laA[:], AF.Ln, bias=1.0, R=[laA], Wr=[laA])
    C.i("dve", "tensor_tensor", la3, la3, prm[:, 8:16].unsqueeze(1).to_broadcast([128, NT, 8]), ALU.mult, R=[laA, prm], Wr=[laA])
    C.i("act", "activation", btA[:], btA[:], AF.Sigmoid, R=[btA], Wr=[btA])
    OF = A.alloc("gd_of", NT * 256)
    Sst = [A.alloc("gd_S%d" % i, 64) for i in range(2)]
    R4, LT, LL, Eg = A.alloc("gd_R4", 512), A.alloc("gd_LT", 512), A.alloc("gd_L", 512), A.alloc("gd_Eg", 512)
    ktok, vtok = A.alloc("gd_ktok", 256), A.alloc("gd_vtok", 256)
    bv, bk, kd = A.alloc("gd_bv", 256), A.alloc("gd_bk", 256), A.alloc("gd_kd", 256)
    qd = A.alloc("gd_qd", 256)
    sm = A.alloc("gd_sm", 32)
    Am, AmT, QKL = A.alloc("gd_A", 128), A.alloc("gd_AT", 128), A.alloc("gd_QKL", 128)
    P_, PT_, N_, NT_ = A.alloc("gd_P", 128), A.alloc("gd_PT", 128), A.alloc("gd_N", 128), A.alloc("gd_NT", 128)
    P2, PT2, Y1, Y2, E1, E1T = (A.alloc("gd_P2", 128), A.alloc("gd_PT2", 128), A.alloc("gd_Y1", 128), A.alloc("gd_Y2", 128),
                                A.alloc("gd_E1", 128), A.alloc("gd_E1T", 128))
    nwT = A.alloc("gd_nwT", 128)
    X1 = A.alloc("gd_X1", 256)
    vn = [A.alloc("gd_vn%d" % i, 128) for i in range(2)]
    yy, y2, gt = A.alloc("gd_y", 256), A.alloc("gd_y2", 256), A.alloc("gd_g", 256)
    pG, pG2, pU, pM, pA, pB, pV, pO = PS
    ident = K("ident")
    GS = cfg.gdn_stage
    for d in range(2):
        if GS < 2:
            break
        sfx = "_f" if d == 0 else "_b"
        tri, tris, neg4, poss4 = K("tri" + sfx), K("tris" + sfx), K("neg4" + sfx), K("poss4" + sfx)
        last = 127 if d == 0 else 0
        order = list(range(NT)) if d == 0 else (list(range(NTC - 1, -1, -1)) + list(range(NT - 1, NTC - 1, -1)))
        for hp in range(2):
            C.i("dve", "memset", Sst[hp][:], 0.0, Wr=[Sst[hp]])
        for t in order:
            ts = slice(t * 128, (t + 1) * 128)
            la = laA[:, t * 8 + d * 4: t * 8 + d * 4 + 4]
            bt = btA[:, t * 8 + d * 4: t * 8 + d * 4 + 4]
            for c in range(2):
                C.tr(pV, pV[:, 256 + c * 128: 384 + c * 128], cv[2 + c][:, ts], K.ident, [cv[2 + c]])
            C.i("act", "copy", ktok[:], pV[:, 256:512], R=[pV], Wr=[ktok])
            for c in range(2):
                C.tr(pV, pV[:, 256 + c * 128: 384 + c * 128], cv[4 + c][:, ts], K.ident, [cv[4 + c]])
            C.i("act", "copy", vtok[:], pV[:, 256:512], R=[pV], Wr=[vtok])
            C.i("dve", "tensor_tensor", R4[:].rearrange("p (h i) -> p h i", h=4), tri.unsqueeze(1).to_broadcast([128, 4, 128]),
                la.unsqueeze(2).to_broadcast([128, 4, 128]), ALU.mult, R=[K.t, laA], Wr=[R4])
            C.mm(pG, pG[:, :], K("ones"), R4[:], [K.t, R4], start=True, stop=False)
            C.mm(pG, pG[:, :], ident, neg4, [K.t], start=False, stop=True)
            C.mm(pG2, pG2[:, :], K("ones"), R4[:], [K.t, R4], start=True, stop=False)
            C.mm(pG2, pG2[:, :], ident, poss4, [K.t], start=False, stop=True)
            C.mm(pU, pU[:, :], K("ones"), R4[:], [K.t, R4])
            C.mm(pM, pM[:, 0:4], tri, la, [K.t, laA])
            C.mm(pM, pM[:, 4:8], tris, la, [K.t, laA])
            C.i("dve", "tensor_scalar", sm[:, 0:4], pM[:, 0:4], -1.0, None, ALU.mult, R=[pM], Wr=[sm])
            C.i("dve", "tensor_copy", sm[:, 4:8], pM[:, 0:4], R=[pM], Wr=[sm])
            C.i("act", "activation", sm[:, 8:12], pM[:, 4:8], AF.Exp, R=[pM], Wr=[sm])
            C.i("act", "activation", sm[:, 12:16], pM[:, 0:4], AF.Exp, R=[pM], Wr=[sm])
            C.i("dve", "tensor_tensor", sm[:, 16:20], sm[:, 12:16], bt, ALU.mult, R=[sm, btA], Wr=[sm])
            for h in range(4):
                C.i("act", "activation", LT[:, h * 128:(h + 1) * 128], pG[:, h * 128:(h + 1) * 128], AF.Exp, bias=sm[:, h:h + 1], R=[pG, sm], Wr=[LT])
                C.i("act", "activation", LL[:, h * 128:(h + 1) * 128], pG2[:, h * 128:(h + 1) * 128], AF.Exp, bias=sm[:, 4 + h:5 + h], scale=-1.0,
                    R=[pG2, sm], Wr=[LL])
            C.i("act", "activation", Eg[:], pU[:, :], AF.Exp, R=[pU], Wr=[Eg])
            C.i("dve", "tensor_tensor", bv[:].rearrange("p (h e) -> p h e", h=4), vtok[:].rearrange("p (h e) -> p h e", h=4),
                bt.unsqueeze(2).to_broadcast([128, 4, 64]), ALU.mult, R=[vtok, btA], Wr=[bv])
            C.i("dve", "tensor_tensor", bk[:].rearrange("p (h e) -> p h e", h=4), ktok[:].rearrange("p (h e) -> p h e", h=4),
                sm[:, 16:20].unsqueeze(2).to_broadcast([128, 4, 64]), ALU.mult, R=[ktok, sm], Wr=[bk])
            C.i("dve", "tensor_tensor", kd[:].rearrange("p (h e) -> p h e", h=4), ktok[:].rearrange("p (h e) -> p h e", h=4),
                sm[:, 8:12].unsqueeze(2).to_broadcast([128, 4, 64]), ALU.mult, R=[ktok, sm], Wr=[kd])
            if GS < 3:
                continue
            for hp in range(2):
                for hh in range(2):
                    h = hp * 2 + hh
                    pr = slice(hh * 64, (hh + 1) * 64)
                    kTs = cv[2 + hp][pr, ts]
                    qTs = cv[hp][pr, ts]
                    C.mm(pM, pM[:, 128:256], kTs, kTs, [cv[2 + hp]])
                    C.mm(pM, pM[:, 256:384], kTs, qTs, [cv[2 + hp], cv[hp]])
                    C.i("dve", "scalar_tensor_tensor", Am[:], pM[:, 128:256], bt[:, h:h + 1], LL[:, h * 128:(h + 1) * 128], ALU.mult, ALU.mult,
                        R=[pM, btA, LL], Wr=[Am])
                    C.i("dve", "tensor_tensor", QKL[:], pM[:, 256:384], LT[:, h * 128:(h + 1) * 128], ALU.mult, R=[pM, LT], Wr=[QKL])
                    C.tr(pM, pM[:, 384:512], Am[:], K.ident, [Am])
                    C.i("act", "copy", AmT[:], pM[:, 384:512], R=[pM], Wr=[AmT])
                    C.i("dve", "tensor_tensor", qd[pr, hp * 128:(hp + 1) * 128], qTs, Eg[pr, h * 128:(h + 1) * 128], ALU.mult, R=[cv[hp], Eg], Wr=[qd])
                    if GS < 4.1:
                        continue
                    C.i("dve", "tensor_tensor", P_[:], Am[:], K("bm32"), ALU.mult, R=[Am, K.t], Wr=[P_])
                    C.i("dve", "tensor_tensor", PT_[:], AmT[:], K("bm32"), ALU.mult, R=[AmT, K.t], Wr=[PT_])
                    C.i("dve", "tensor_tensor", N_[:], ident, P_[:], ALU.subtract, R=[K.t, P_], Wr=[N_])
                    C.i("dve", "tensor_tensor", NT_[:], ident, PT_[:], ALU.subtract, R=[K.t, PT_], Wr=[NT_])
                    C.i("dve", "tensor_tensor", E1[:], Am[:], K("be64"), ALU.mult, R=[Am, K.t], Wr=[E1])
                    C.i("dve", "tensor_tensor", E1T[:], AmT[:], K("be64"), ALU.mult, R=[AmT, K.t], Wr=[E1T])
                    cp, cpt = P_, PT_
                    if GS < 4.15:
                        continue
                    for lev in range(4 if GS >= 4.4 else 1):
                        np_, npt = (P2, PT2) if lev % 2 == 0 else (P_, PT_)
                        C.mm(pA, pA[:, 0:128], cpt[:], cp[:], [cpt, cp])
                        if GS >= 4.17:
                            C.mm(pA, pA[:, 128:256], cp[:], cpt[:], [cpt, cp])
                        if GS >= 4.18:
                            C.i("act", "copy", np_[:], pA[:, 0:128], R=[pA], Wr=[np_])
                        if GS >= 4.19:
                            C.i("act", "copy", npt[:], pA[:, 128:256], R=[pA], Wr=[npt])
                        if GS < 4.3:
                            continue
                        C.mm(pB, pB[:, 0:128], npt[:], N_[:], [npt, N_])
                        C.mm(pB, pB[:, 128:256], np_[:], NT_[:], [np_, NT_])
                        C.i("act", "copy", X1[:], pB[:, 0:256], R=[pB], Wr=[X1])
                        C.i("dve", "tensor_tensor", N_[:], N_[:], X1[:, 0:128], ALU.add, R=[N_, X1], Wr=[N_])
                        C.i("dve", "tensor_tensor", NT_[:], NT_[:], X1[:, 128:256], ALU.add, R=[NT_, X1], Wr=[NT_])
                        cp, cpt = np_, npt
                    if GS < 4.6:
                        continue
                    C.mm(pA, pA[:, 256:384], E1T[:], N_[:], [E1T, N_])
                    C.mm(pA, pA[:, 384:512], E1[:], NT_[:], [E1, NT_])
                    C.i("act", "copy", Y1[:], pA[:, 256:384], R=[pA], Wr=[Y1])
                    C.i("act", "copy", Y2[:], pA[:, 384:512], R=[pA], Wr=[Y2])
                    C.mm(pB, pB[:, 256:384], NT_[:], Y1[:], [NT_, Y1])
                    C.mm(pB, pB[:, 384:512], N_[:], Y2[:], [N_, Y2])
                    C.i("act", "copy", X1[:], pB[:, 256:512], R=[pB], Wr=[X1])
                    C.i("dve", "tensor_tensor", N_[:], N_[:], X1[:, 0:128], ALU.subtract, R=[N_, X1], Wr=[N_])
                    C.i("dve", "tensor_tensor", NT_[:], NT_[:], X1[:, 128:256], ALU.subtract, R=[NT_, X1], Wr=[NT_])
                    if GS < 4.8:
                        continue
                    C.i("dve", "tensor_tensor", E1[:], Am[:], K("be128"), ALU.mult, R=[Am, K.t], Wr=[E1])
                    C.mm(pA, pA[:, 0:128], E1[:], NT_[:], [E1, NT_])
                    C.i("act", "copy", Y2[:], pA[:, 0:128], R=[pA], Wr=[Y2])
                    C.mm(pB, pB[:, 0:128], N_[:], Y2[:], [N_, Y2])
                    C.i("act", "copy", X1[:, 0:128], pB[:, 0:128], R=[pB], Wr=[X1])
                    C.i("dve", "tensor_tensor", NT_[:], NT_[:], X1[:, 0:128], ALU.subtract, R=[NT_, X1], Wr=[NT_])
                    if GS < 5:
                        continue
                    C.mm(pV, pV[:, 128:256], bk[:, hp * 128:(hp + 1) * 128], NT_[:], [bk, NT_])
                    C.i("dve", "tensor_scalar", nwT[pr, :], pV[pr, 128:256], -1.0, None, ALU.mult, R=[pV], Wr=[nwT])
                    C.mm(pV, pV[:, 0:64], NT_[:], bv[:, h * 64:(h + 1) * 64], [NT_, bv], start=True, stop=False)
                    C.mm(pV, pV[:, 0:64], nwT[pr, :], Sst[hp][pr, :], [nwT, Sst[hp]], start=False, stop=True)
                    C.i("act", "copy", vn[hp][:, hh * 64:(hh + 1) * 64], pV[:, 0:64], R=[pV], Wr=[vn[hp]])
                    C.mm(pO, pO[:, h * 64:(h + 1) * 64], qd[pr, hp * 128:(hp + 1) * 128], Sst[hp][pr, :], [qd, Sst[hp]], start=True, stop=False)
                    C.mm(pO, pO[:, h * 64:(h + 1) * 64], QKL[:], vn[hp][:, hh * 64:(hh + 1) * 64], [QKL, vn[hp]], start=False, stop=True)
                if GS < 6:
                    continue
                C.mm(pO, pO[:, 256:384], kd[:, hp * 128:(hp + 1) * 128], vn[hp][:], [kd, vn[hp]])
                for hh in range(2):
                    h = hp * 2 + hh
                    pr = slice(hh * 64, (hh + 1) * 64)
                    C.i("dve", "scalar_tensor_tensor", Sst[hp][pr, :], Sst[hp][pr, :], Eg[pr, h * 128 + last: h * 128 + last + 1],
                        pO[pr, 256 + hh * 64: 256 + (hh + 1) * 64], ALU.mult, ALU.add, R=[Sst[hp], Eg, pO], Wr=[Sst[hp]])
            if GS < 6.5:
                continue
            if d == 0:
                C.i("act", "copy", OF[:, t * 256:(t + 1) * 256], pO[:, 0:256], R=[pO], Wr=[OF])
            else:
                if GS < 6.6:
                    continue
                C.dma("sp", gt[:], ZT[s, t * 128:(t + 1) * 128, 1024:1280], [ZT], [gt])
                if GS < 6.7:
                    continue
                C.i("act", "copy", yy[:], pO[:, 0:256], R=[pO], Wr=[yy])
                if GS < 6.8:
                    continue
                C.i("dve", "tensor_tensor", yy[:], yy[:], OF[:, t * 256:(t + 1) * 256], ALU.add, R=[yy, OF], Wr=[yy])
                C.i("dve", "tensor_tensor", y2[:], yy[:], yy[:], ALU.mult, R=[yy], Wr=[y2])
                C.i("dve", "tensor_reduce", sm[:, 24:28], y2[:].rearrange("p (h e) -> p h e", h=4), AX.X, ALU.add, R=[y2], Wr=[sm])
                C.i("act", "activation", sm[:, 24:28], sm[:, 24:28], AF.Ln, bias=EPS, scale=1.0 / 64.0, R=[sm], Wr=[sm])
                C.i("act", "activation", sm[:, 24:28], sm[:, 24:28], AF.Exp, scale=-0.5, R=[sm], Wr=[sm])
                C.i("act", "activation", gt[:], gt[:], AF.Silu, R=[gt], Wr=[gt])
                C.i("dve", "tensor_tensor", yy[:].rearrange("p (h e) -> p h e", h=4), yy[:].rearrange("p (h e) -> p h e", h=4),
                    sm[:, 24:28].unsqueeze(2).to_broadcast([128, 4, 64]), ALU.mult, R=[yy, sm], Wr=[yy])
                C.i("dve", "tensor_tensor", yy[:], yy[:], nw[:], ALU.mult, R=[yy, nw], Wr=[yy])
                C.i("dve", "tensor_tensor", y2[:], yy[:], gt[:], ALU.mult, R=[yy, gt], Wr=[y2])
                C.dma("sp", O[s, t * 128:(t + 1) * 128, 256:512], y2[:], [y2], [O])
    A.release(m0)


def token_groups(cfg, with_ctx):
    NTC, NT = cfg.TC // 128, cfg.NT
    gs = []
    if with_ctx:
        for a in range(0, NTC, 4):
            gs.append((a, min(4, NTC - a), True))
    for a in range(NTC, NT, 4):
        gs.append((a, min(4, NT - a), False))
    return gs


def merge_phase(C, A, PS, K, cfg, l, s, X, HT, O, MOD, W_):
    NS, TC, TT, NT, DEPTH = cfg.NS, cfg.TC, cfg.T, cfg.NT, cfg.DEPTH
    alpha = (2.0 * DEPTH) ** 0.25
    with_ctx = l < DEPTH - 1
    m0 = A.mark()
    wg = A.alloc("mg_wg", 8192)
    wb = A.alloc("mg_wb", 2048)
    wo = A.alloc("mg_wo", 8192)
    hT = A.alloc("mg_hT", 4096)
    oT = A.alloc("mg_oT", 4096)
    mT = A.alloc("mg_mT", 4096)
    bgT = A.alloc("mg_bg", 32)
    g1B = [A.alloc("mg_g1_%d" % i, 1024) for i in range(2)]
    lnG, lnB = A.alloc("mg_lnG", 1024), A.alloc("mg_lnB", 1024)
    ot = A.alloc("mg_ot", 1024)
    xt, rr, xn = A.alloc("mg_xt", 1024), A.alloc("mg_r", 1024), A.alloc("mg_xn", 1024)
    sig, tmp = A.alloc("mg_sig", 512), A.alloc("mg_tmp", 512)
    st = A.alloc("mg_st", 8)
    C.dma("sp", bgT[:], W_["b_gateT"][l, :, :], [W_["b_gateT"]], [bgT])
    C.dma("sp", g1B[0][:], MOD[l, s:s + 1, 2048:3072].partition_broadcast(128), [MOD], [g1B[0]])
    C.dma("sp", g1B[1][:], MOD[l, NS:NS + 1, 2048:3072].partition_broadcast(128), [MOD], [g1B[1]])
    C.dma("sp", lnG[:], W_["ln1_g"][l:l + 1, :].partition_broadcast(128), [W_["ln1_g"]], [lnG])
    C.dma("sp", lnB[:], W_["ln1_b"][l:l + 1, :].partition_broadcast(128), [W_["ln1_b"]], [lnB])
    C.dma("sp", wo[:].rearrange("p (k n) -> p k n", k=8), W_["w_out"][l, :, :].rearrange("(k p) n -> p k n", p=128), [W_["w_out"]], [wo])
    for (t0, ntile, is_ctx) in token_groups(cfg, with_ctx):
        ntok = ntile * 128
        c0 = t0 * 128
        C.dma("sp", hT[:].rearrange("p (k n) -> p k n", k=8)[:, :, 0:ntok], HT[s, :, :, c0:c0 + ntok].rearrange("k p n -> p k n"), [HT], [hT])
        for ti in range(ntile):
            C.dma("sp", ot[:], O[s, c0 + ti * 128: c0 + (ti + 1) * 128, :], [O], [ot])
            for half in range(2):
                ps = PS[half]
                for kk in range(4):
                    C.tr(ps, ps[:, kk * 128:(kk + 1) * 128], ot[:, (half * 4 + kk) * 128:(half * 4 + kk + 1) * 128], K.ident, [ot])
                for kk in range(4):
                    k = half * 4 + kk
                    C.i("act" if kk % 2 else "dve", "copy" if kk % 2 else "tensor_copy",
                        oT[:, k * 512 + ti * 128: k * 512 + (ti + 1) * 128], ps[:, kk * 128:(kk + 1) * 128], R=[ps], Wr=[oT])
        for g in range(4):
            C.dma("sp", wg[:].rearrange("p (k n) -> p k n", k=8), W_["w_gate"][l, g, :, :].rearrange("(k p) n -> p k n", p=128), [W_["w_gate"]], [wg])
            C.dma("sp", wb[:].rearrange("p (k n) -> p k n", k=2), W_["w_branch"][l, g, :, :].rearrange("(k p) n -> p k n", p=128), [W_["w_branch"]], [wb])
            for n in range(8):
                pg, pb = PS[2 + n % 2], PS[4 + n % 2]
                for k in range(8):
                    C.mm(pg, pg[:, 0:ntok], wg[:, k * 1024 + n * 128: k * 1024 + (n + 1) * 128], hT[:, k * 512: k * 512 + ntok], [wg, hT],
                         start=(k == 0), stop=(k == 7))
                for k in range(2):
                    C.mm(pb, pb[:, 0:ntok], wb[:, k * 1024 + n * 128: k * 1024 + (n + 1) * 128], oT[:, (2 * g + k) * 512: (2 * g + k) * 512 + ntok], [wb, oT],
                         start=(k == 0), stop=(k == 1))
                C.i("act", "activation", sig[:, 0:ntok], pg[:, 0:ntok], AF.Sigmoid, bias=bgT[:, g * 8 + n: g * 8 + n + 1], R=[pg, bgT], Wr=[sig])
                if g == 0:
                    C.i("dve", "tensor_tensor", mT[:, n * 512: n * 512 + ntok], sig[:, 0:ntok], pb[:, 0:ntok], ALU.mult, R=[sig, pb], Wr=[mT])
                else:
                    C.i("dve", "tensor_tensor", tmp[:, 0:ntok], sig[:, 0:ntok], pb[:, 0:ntok], ALU.mult, R=[sig, pb], Wr=[tmp])
                    C.i("dve", "tensor_tensor", mT[:, n * 512: n * 512 + ntok], mT[:, n * 512: n * 512 + ntok], tmp[:, 0:ntok], ALU.add, R=[mT, tmp], Wr=[mT])
        gB = g1B[1] if is_ctx else g1B[0]
        for ti in range(ntile):
            r0 = c0 + ti * 128
            C.dma("sp", xt[:], X[s, r0:r0 + 128, :], [X], [xt])
            for half in range(2):
                py = PS[6 + half]
                hs = slice(half * 512, (half + 1) * 512)
                for k in range(8):
                    C.mm(py, py[:, :], mT[:, k * 512 + ti * 128: k * 512 + (ti + 1) * 128], wo[:, k * 1024 + half * 512: k * 1024 + (half + 1) * 512],
                         [mT, wo], start=(k == 0), stop=(k == 7))
                C.i("dve", "tensor_tensor", rr[:, hs], py[:, :], gB[:, hs], ALU.mult, R=[py, gB], Wr=[rr])
                C.i("dve", "scalar_tensor_tensor", rr[:, hs], xt[:, hs], alpha, rr[:, hs], ALU.mult, ALU.add, R=[xt, rr], Wr=[rr])
            layernorm(C, rr, xn, st)
            C.i("dve", "tensor_tensor", xn[:], xn[:], lnG[:], ALU.mult, R=[xn, lnG], Wr=[xn])
            C.i("dve", "tensor_tensor", xn[:], xn[:], lnB[:], ALU.add, R=[xn, lnB], Wr=[xn])
            C.dma("sp", X[s, r0:r0 + 128, :], xn[:], [xn], [X])
    A.release(m0)


def make_moe_consts():
    c = {}
    p = np.arange(128, dtype=np.float32)
    c["tcol"] = (p[:, None] + 128.0 * np.arange(16, dtype=np.float32)[None, :]).astype(np.float32)
    c["iota512"] = np.tile(np.arange(512, dtype=np.float32)[None, :], (128, 1))
    return c


def moe_phase(C, A, PS, K, cfg, l, s, X, H2, YE, IDX, MOD, W_):
    NS, TC, TT, NT, DEPTH = cfg.NS, cfg.TC, cfg.T, cfg.NT, cfg.DEPTH
    alpha = (2.0 * DEPTH) ** 0.25
    with_ctx = l < DEPTH - 1
    streams = ([(0, TC, NS)] if with_ctx else []) + [(TC, TT, s)]
    m0 = A.mark()
    lnG, lnB = A.alloc("mo_lnG", 1024), A.alloc("mo_lnB", 1024)
    C.dma("sp", lnG[:], W_["ln2_g"][l:l + 1, :].partition_broadcast(128), [W_["ln2_g"]], [lnG])
    C.dma("sp", lnB[:], W_["ln2_b"][l:l + 1, :].partition_broadcast(128), [W_["ln2_b"]], [lnB])
    for (a, b, ms) in streams:
        Tn = b - a
        NTs = Tn // 128
        cap = 2 * Tn // E
        ncc = (cap + 127) // 128
        rows = min(128, cap)
        m1 = A.mark()
        modB = A.alloc("mo_modB", 3 * 1024)
        C.dma("sp", modB[:], MOD[l, ms:ms + 1, 3072:6144].partition_broadcast(128), [MOD], [modB])
        C.i("dve", "tensor_scalar", modB[:, 1024:2048], modB[:, 1024:2048], 1.0, None, ALU.add, R=[modB], Wr=[modB])
        AFF = A.alloc("mo_aff", Tn)
        wv, ixf = A.alloc("mo_wv", cap), A.alloc("mo_ixf", cap)
        idxT, wT = A.alloc("mo_idxT", ncc * 16), A.alloc("mo_wT", ncc * 16)
        m2 = A.mark()
        wr = A.alloc("mo_wr", 8 * 16)
        C.dma("sp", wr[:].rearrange("p (k e) -> p k e", k=8), W_["w_router"][l, :, :].rearrange("(k p) e -> p k e", p=128), [W_["w_router"]], [wr])
        xts = [A.alloc("mo_xt", 1024) for _ in range(2)]
        xns = [A.alloc("mo_xn", 1024) for _ in range(2)]
        h2T = A.alloc("mo_h2T", 1024)
        sts = [A.alloc("mo_st", 8) for _ in range(2)]
        lg, sm = A.alloc("mo_lg", 16), A.alloc("mo_sm", 8)
        for tt in range(NTs):
            r0 = a + tt * 128
            xt, xn, st = xts[tt % 2], xns[tt % 2], sts[tt % 2]
            C.dma("sp", xt[:], X[s, r0:r0 + 128, :], [X], [xt])
            layernorm(C, xt, xn, st)
            C.i("dve", "tensor_tensor", xn[:], xn[:], modB[:, 1024:2048], ALU.mult, R=[xn, modB], Wr=[xn])
            C.i("dve", "tensor_tensor", xn[:], xn[:], modB[:, 0:1024], ALU.add, R=[xn, modB], Wr=[xn])
            C.dma("sp", H2[s, r0:r0 + 128, :], xn[:], [xn], [H2])
            for half in range(2):
                ps = PS[half]
                for kk in range(4):
                    C.tr(ps, ps[:, kk * 128:(kk + 1) * 128], xn[:, (half * 4 + kk) * 128:(half * 4 + kk + 1) * 128], K.ident, [xn])
                C.i("act" if half else "dve", "copy" if half else "tensor_copy", h2T[:, half * 512:(half + 1) * 512], ps[:, :], R=[ps], Wr=[h2T])
            pl = PS[2]
            for k in range(8):
                C.mm(pl, pl[:, 0:16], h2T[:, k * 128:(k + 1) * 128], wr[:, k * 16:(k + 1) * 16], [h2T, wr], start=(k == 0), stop=(k == 7))
            C.i("dve", "reduce_max", sm[:, 0:1], pl[:, 0:16], AX.X, R=[pl], Wr=[sm])
            C.i("dve", "tensor_scalar", sm[:, 1:2], sm[:, 0:1], -1.0, None, ALU.mult, R=[sm], Wr=[sm])
            C.i("act", "activation", lg[:], pl[:, 0:16], AF.Exp, bias=sm[:, 1:2], accum_out=sm[:, 2:3], R=[pl, sm], Wr=[lg, sm])
            C.i("dve", "reciprocal", sm[:, 3:4], sm[:, 2:3], R=[sm], Wr=[sm])
            C.i("dve", "tensor_scalar", lg[:], lg[:], sm[:, 3:4], None, ALU.mult, R=[lg, sm], Wr=[lg])
            C.S.op("pe", lambda e, lg=lg, tt=tt: e.transpose(PS[3][0:16, (tt % 4) * 128:(tt % 4 + 1) * 128], lg[:, 0:16], K("ident")),
                   [lg.res, K.t.res], [PS[3].res])
            C.i("act", "copy", AFF[0:16, tt * 128:(tt + 1) * 128], PS[3][0:16, (tt % 4) * 128:(tt % 4 + 1) * 128], R=[PS[3]], Wr=[AFF])
        ix = A.alloc("mo_ix", cap)
        ixu = T(ix.ap.bitcast(U32), "ixu")
        ixu.res = ix.res
        for r in range(cap // 8):
            C.i("dve", "max", out=wv[0:16, r * 8:(r + 1) * 8], in_=AFF[0:16, :], R=[AFF], Wr=[wv])
            C.i("dve", "max_index", out=ixu[0:16, r * 8:(r + 1) * 8], in_max=wv[0:16, r * 8:(r + 1) * 8], in_values=AFF[0:16, :], R=[AFF, wv], Wr=[ix])
            C.i("dve", "match_replace", out=AFF[0:16, :], in_to_replace=wv[0:16, r * 8:(r + 1) * 8], in_values=AFF[0:16, :], imm_value=-1.0,
                R=[AFF, wv], Wr=[AFF])
        C.i("dve", "tensor_copy", ixf[0:16, :], ixu[0:16, :], R=[ix], Wr=[ixf])
        C.dma("sp", IDX[0:16, 0:cap], ixf[0:16, :], [ixf], [IDX])
        for cc in range(ncc):
            for src, dst in ((ixf, idxT), (wv, wT)):
                C.S.op("pe", lambda e, src=src, cc=cc, rows=rows: e.transpose(PS[3][0:rows, 0:16], src[0:16, cc * 128: cc * 128 + rows], K("ident")[0:16, 0:16]),
                       [src.res, K.t.res], [PS[3].res])
                C.i("dve", "tensor_copy", dst[0:rows, cc * 16:(cc + 1) * 16], PS[3][0:rows, 0:16], R=[PS[3]], Wr=[dst])
        A.release(m2)
        m3 = A.mark()
        Wg, Wu, Wd = A.alloc("mo_Wg", 8192), A.alloc("mo_Wu", 8192), A.alloc("mo_Wd", 8192)
        XeT, hidT, ye = A.alloc("mo_XeT", 8 * cap), A.alloc("mo_hid", 8 * cap), A.alloc("mo_ye", ncc * 1024)
        h2s = [A.alloc("mo_h2", 1024) for _ in range(2)]
        Pm = [A.alloc("mo_P", cap) for _ in range(2)]
        idr, tmp = A.alloc("mo_idr", cap), A.alloc("mo_tmp", cap)
        xe = A.alloc("mo_xe", ncc * 1024)
        for e_ in range(E):
            C.dma("sp", idr[:], IDX[e_:e_ + 1, 0:cap].partition_broadcast(128), [IDX], [idr])
            for wt, nm in ((Wg, "w_ff_gate"), (Wu, "w_ff_up"), (Wd, "w_ff_down")):
                C.dma("sp", wt[:].rearrange("p (k n) -> p k n", k=8), W_[nm][l, e_, :, :].rearrange("(k p) n -> p k n", p=128), [W_[nm]], [wt])
            for tt in range(NTs):
                h2t, P = h2s[tt % 2], Pm[tt % 2]
                C.dma("sp", h2t[:], H2[s, a + tt * 128: a + (tt + 1) * 128, :], [H2], [h2t])
                C.i("dve", "tensor_scalar", P[:], idr[:], K("tcol")[:, tt:tt + 1], None, ALU.is_equal, R=[idr, K.t], Wr=[P])
                for cc in range(ncc):
                    for half in range(2):
                        pb = PS[cc * 2 + half]
                        C.mm(pb, pb[0:rows, :], P[:, cc * 128: cc * 128 + rows], h2t[:, half * 512:(half + 1) * 512], [h2t, P],
                             start=(tt == 0), stop=(tt == NTs - 1))
            for cc in range(ncc):
                for half in range(2):
                    pb = PS[cc * 2 + half]
                    C.i("act" if half else "dve", "copy" if half else "tensor_copy", xe[0:rows, cc * 1024 + half * 512: cc * 1024 + (half + 1) * 512],
                        pb[0:rows, :], R=[pb], Wr=[xe])
            for cc in range(ncc):
                for half in range(2):
                    pb = PS[4 + half]
                    for kk in range(4):
                        k = half * 4 + kk
                        C.S.op("pe", lambda e, pb=pb, kk=kk, k=k, cc=cc, rows=rows, xe=xe: e.transpose(pb[:, kk * 128: kk * 128 + rows], xe[0:rows, cc * 1024 + k * 128: cc * 1024 + (k + 1) * 128],
                                                                                       K("ident")[0:rows, 0:rows]), [xe.res, K.t.res], [pb.res])
                    for kk in range(4):
                        k = half * 4 + kk
                        C.i("act" if kk % 2 else "dve", "copy" if kk % 2 else "tensor_copy", XeT[:, k * cap + cc * 128: k * cap + cc * 128 + rows],
                            pb[:, kk * 128: kk * 128 + rows], R=[pb], Wr=[XeT])
            for f in range(8):
                pg, pu = PS[6], PS[7]
                for k in range(8):
                    C.mm(pg, pg[:, 0:cap], Wg[:, k * 1024 + f * 128: k * 1024 + (f + 1) * 128], XeT[:, k * cap:(k + 1) * cap], [Wg, XeT], start=(k == 0), stop=(k == 7))
                for k in range(8):
                    C.mm(pu, pu[:, 0:cap], Wu[:, k * 1024 + f * 128: k * 1024 + (f + 1) * 128], XeT[:, k * cap:(k + 1) * cap], [Wu, XeT], start=(k == 0), stop=(k == 7))
                C.i("act", "activation", tmp[:], pg[:, 0:cap], AF.Silu, R=[pg], Wr=[tmp])
                C.i("dve", "tensor_tensor", hidT[:, f * cap:(f + 1) * cap], tmp[:], pu[:, 0:cap], ALU.mult, R=[tmp, pu], Wr=[hidT])
            for cc in range(ncc):
                for half in range(2):
                    pd = PS[half]
                    for f in range(8):
                        C.mm(pd, pd[0:rows, :], hidT[:, f * cap + cc * 128: f * cap + cc * 128 + rows], Wd[:, f * 1024 + half * 512: f * 1024 + (half + 1) * 512],
                             [hidT, Wd], start=(f == 0), stop=(f == 7))
                    C.i("act" if half else "dve", "copy" if half else "tensor_copy", ye[0:rows, cc * 1024 + half * 512: cc * 1024 + (half + 1) * 512],
                        pd[0:rows, :], R=[pd], Wr=[ye])
                C.dma("sp", YE[e_, cc * 128: cc * 128 + rows, :], ye[0:rows, cc * 1024:(cc + 1) * 1024], [ye], [YE])
        A.release(m3)
        yes_ = [A.alloc("mo_yeB", ncc * 1024) for _ in range(2)]
        PwT = [A.alloc("mo_PwT", ncc * 512) for _ in range(2)]
        idxs = A.alloc("mo_idxs", ncc * 16)
        xt, rr, xn, st = A.alloc("mo_xtB", 1024), A.alloc("mo_rB", 1024), A.alloc("mo_xnB", 1024), A.alloc("mo_stB", 8)
        for g0 in range(0, NTs, 4):
            ntile = min(4, NTs - g0)
            ntok = ntile * 128
            C.i("dve", "tensor_scalar", idxs[0:rows, :], idxT[0:rows, :], float(-g0 * 128), None, ALU.add, R=[idxT], Wr=[idxs])
            for e_ in range(E):
                yb, pw = yes_[e_ % 2], PwT[e_ % 2]
                C.dma("sp", yb[0:rows, :].rearrange("p (c n) -> p c n", c=ncc), YE[e_, 0:ncc * rows, :].rearrange("(c p) n -> p c n", p=rows), [YE], [yb])
                for cc in range(ncc):
                    C.i("dve", "tensor_scalar", pw[0:rows, cc * 512: cc * 512 + ntok], K("iota512")[0:rows, 0:ntok], idxs[0:rows, cc * 16 + e_: cc * 16 + e_ + 1],
                        wT[0:rows, cc * 16 + e_: cc * 16 + e_ + 1], ALU.is_equal, ALU.mult, R=[K.t, idxs, wT], Wr=[pw])
                for ti in range(ntile):
                    for half in range(2):
                        py = PS[ti * 2 + half]
                        for cc in range(ncc):
                            C.mm(py, py[:, :], pw[0:rows, cc * 512 + ti * 128: cc * 512 + (ti + 1) * 128], yb[0:rows, cc * 1024 + half * 512: cc * 1024 + (half + 1) * 512],
                                 [pw, yb], start=(e_ == 0 and cc == 0), stop=(e_ == E - 1 and cc == ncc - 1))
            for ti in range(ntile):
                r0 = a + (g0 + ti) * 128
                C.dma("sp", xt[:], X[s, r0:r0 + 128, :], [X], [xt])
                for half in range(2):
                    py = PS[ti * 2 + half]
                    hs = slice(half * 512, (half + 1) * 512)
                    C.i("dve", "tensor_tensor", rr[:, hs], py[:, :], modB[:, 2048 + half * 512: 2048 + (half + 1) * 512], ALU.mult, R=[py, modB], Wr=[rr])
                    C.i("dve", "scalar_tensor_tensor", rr[:, hs], xt[:, hs], alpha, rr[:, hs], ALU.mult, ALU.add, R=[xt, rr], Wr=[rr])
                layernorm(C, rr, xn, st)
                C.i("dve", "tensor_tensor", xn[:], xn[:], lnG[:], ALU.mult, R=[xn, lnG], Wr=[xn])
                C.i("dve", "tensor_tensor", xn[:], xn[:], lnB[:], ALU.add, R=[xn, lnB], Wr=[xn])
                C.dma("sp", X[s, r0:r0 + 128, :], xn[:], [xn], [X])
        A.release(m1)
    A.release(m0)


CONSTS_NP, CONST_OFFS = make_consts()


def build(cfg, debug_out=(), dbg_cols=(768,)):
    NS, TC, TL, DEPTH, TT, NT, NST = cfg.NS, cfg.TC, cfg.TL, cfg.DEPTH, cfg.T, cfg.NT, cfg.NST
    nc = bass.Bass("TRN2", target_bir_lowering=False)
    es = ExitStack()
    C = Ctx(nc, es)
    S = C.S
    A = Arena(C, 52000)
    PS = [C.ps("ps%d" % i) for i in range(8)]

    def ein(name, shape):
        return C.dram(name, shape, kind="ExternalInput")

    x_in = ein("x", [NS, TL, D])
    ctx_in = ein("ctx", [NS, TC, D])
    cT_in = ein("cT", [128, 8, NST])
    pos_in = ein("pos", [TL, D])
    consts_in = ein("consts", list(CONSTS_NP.shape))
    W_ = {}
    for nm, shp in (("ssd_dt_bias", [DEPTH, 8]), ("ssd_a_log", [DEPTH, 8]), ("ssd_d", [DEPTH, 4]), ("ssd_norm_w", [DEPTH, 256]),
                    ("ssd_cwT", [DEPTH, 512, 5]), ("ssd_cbT", [DEPTH, 512, 1]),
                    ("hgrn_lb_logits", [2, DEPTH, 256]), ("hgrn_norm_w", [DEPTH, 256]),
                    ("gdn_dt_bias", [DEPTH, 8]), ("gdn_a_log", [DEPTH, 8]), ("gdn_norm_w", [DEPTH, 256]), ("gdn_cwT", [DEPTH, 768, 5]),
                    ("w_gate", [DEPTH, 4, D, D]), ("b_gateT", [DEPTH, 128, 32]), ("w_branch", [DEPTH, 4, W, D]), ("w_out", [DEPTH, D, D]),
                    ("ln1_g", [DEPTH, D]), ("ln1_b", [DEPTH, D]),
                    ("w_router", [DEPTH, D, E]), ("w_ff_gate", [DEPTH, E, D, FF]), ("w_ff_up", [DEPTH, E, D, FF]), ("w_ff_down", [DEPTH, E, FF, D]),
                    ("ln2_g", [DEPTH, D]), ("ln2_b", [DEPTH, D]),
                    ("bd64", [2, 2, 128, 256]), ("dft_ctx", [2, TC // 128, 128, TC // 128, 128]),
                    ("dft_lat", [2, TL // 128, 128, TL // 128, 128])):
        W_[nm] = ein(nm, shp)
    ada_w = ein("ada_w", [DEPTH, D, 6 * D])
    ada_b = ein("ada_b", [DEPTH, 6 * D])
    w_in = ein("w_in", [DEPTH, D, IN_COLS])
    out = C.dram("out", [NS, TL, D], kind="ExternalOutput")
    dbg = {k: C.dram(k, shp, kind="ExternalOutput") for k, shp in debug_out}

    X = C.dram("X", [NS, TT, D])
    MOD = C.dram("MOD", [DEPTH, NST, 6 * D])
    ZT_COLS = 1560
    ZT = C.dram("ZT", [NS, TT, ZT_COLS])
    ZF = C.dram("ZF", [NS, 18 * 128, TT])
    HT = C.dram("HT", [NS, 8, 128, TT])
    H2 = C.dram("H2", [NS, TT, D])
    YE = C.dram("YE", [E, 2 * TL // E, D])
    IDX = C.dram("IDX", [E, 2 * TL // E])

    O = C.dram("O", [NS, TT, D])
    KT = A.alloc("consts", CONSTS_NP.shape[1])
    C.dma("sp", KT[:], consts_in[:, :], [consts_in], [KT])

    def K(name):
        o, n = CONST_OFFS[name]
        return KT[:, o:o + n]
    K.t = KT
    ident = T(K("ident"), "ident")
    ident.res = KT.res
    K.ident = ident

    m0 = A.mark()
    xts = [A.alloc("xt", D) for _ in range(2)]
    pts = [A.alloc("pt", D) for _ in range(2)]
    for s in range(NS):
        C.dma("sp", X[s, 0:TC, :], ctx_in[s, :, :], [ctx_in], [X])
        for t in range(TL // 128):
            xt = xts[t % 2]
            pt = pts[t % 2]
            C.dma("sp", xt[:], x_in[s, t * 128:(t + 1) * 128, :], [x_in], [xt])
            C.dma("sp", pt[:], pos_in[t * 128:(t + 1) * 128, :], [pos_in], [pt])
            C.v("dve", lambda e, xt=xt, pt=pt: e.tensor_tensor(xt[:], xt[:], pt[:], ALU.add), [xt, pt], [xt])
            C.dma("sp", X[s, TC + t * 128:TC + (t + 1) * 128, :], xt[:], [xt], [X])
    A.release(m0)

    scT = A.alloc("scT", 8 * NST)
    C.dma("sp", scT[:], cT_in[:, :, :].rearrange("p k s -> p (k s)"), [cT_in], [scT])
    C.v("act", lambda e: e.activation(scT[:], scT[:], AF.Silu), [scT], [scT])
    m1 = A.mark()
    for l in range(DEPTH):
        abB = A.alloc("abB", 6 * D, parts=NST)
        modrow = A.alloc("modrow", 6 * D, parts=NST)
        C.dma("sp", abB[:], ada_b[l:l + 1, :].partition_broadcast(NST), [ada_b], [abB])
        aws = [A.alloc("aw", 8 * 512) for _ in range(2)]
        for ng in range(12):
            aw = aws[ng % 2]
            C.dma("sp", aw[:].rearrange("p (k n) -> p k n", k=8),
                  ada_w[l, :, ng * 512:(ng + 1) * 512].rearrange("(k p) n -> p k n", p=128), [ada_w], [aw])
            ps = PS[ng % 2]
            for k in range(8):
                C.mm(ps, ps[0:NST, :], scT[:, k * NST:(k + 1) * NST], aw[:, k * 512:(k + 1) * 512], [scT, aw],
                     start=(k == 0), stop=(k == 7))
            C.v("dve", lambda e, ps=ps, ng=ng, modrow=modrow, abB=abB: e.tensor_tensor(
                modrow[:, ng * 512:(ng + 1) * 512], ps[0:NST, :], abB[:, ng * 512:(ng + 1) * 512], ALU.add),
                [ps, abB], [modrow])
            if ng % 4 == 3:
                pass
        C.dma("sp", MOD[l, :, :], modrow[:], [modrow], [MOD])
        A.release(m1)

    for l in range(DEPTH):
        mL = A.mark()
        modF = A.alloc("modF", 4 * 8 * NST)
        mm_ = A.mark()
        modrow = A.alloc("modrow2", 6 * D, parts=NST)
        C.dma("sp", modrow[:], MOD[l, :, :], [MOD], [modrow])
        secs = (0, 1, 3, 4)
        ps = PS[0]
        for si, sec in enumerate(secs):
            for k in range(8):
                col = (si * 8 + k) * NST
                S.op("pe", lambda e, col=col, sec=sec, k=k, ps=ps, modrow=modrow: e.transpose(
                    ps[:, col:col + NST], modrow[0:NST, sec * D + k * 128: sec * D + (k + 1) * 128], ident[0:NST, 0:NST]),
                    [modrow.res, ident.res], [ps.res])
        C.v("dve", lambda e, ps=ps, modF=modF: e.tensor_copy(modF[:], ps[:, 0:32 * NST]), [ps], [modF])
        for si in (1, 3):
            C.v("dve", lambda e, si=si, modF=modF: e.tensor_scalar(
                modF[:, si * 8 * NST:(si + 1) * 8 * NST], modF[:, si * 8 * NST:(si + 1) * 8 * NST], 1.0, None, ALU.add),
                [modF], [modF])
        A.release(mm_)

        def mf(si, k, ms):
            c0 = (si * 8 + k) * NST + ms
            return modF[:, c0:c0 + 1]

        mAB = A.mark()
        wI = A.alloc("wI", 8 * IN_COLS)
        for k in range(8):
            C.dma("sp", wI[:, k * IN_COLS:(k + 1) * IN_COLS], w_in[l, k * 128:(k + 1) * 128, :], [w_in], [wI])
        o_aq, o_ff, o_fb, o_av, o_ag = 0, 256, 512, 768, 1024
        o_qkv, o_bg, o_ba, o_bb = 1280, 2048, 2304, 2312
        o_cu, o_xbc, o_dz, o_dt = 2320, 2576, 3088, 3344
        tm_groups = [(o_ff, 512, 0), (o_av, 512, 512), (o_bg, 272, 1024), (o_dz, 264, 1296)]
        fm_cols = [o_aq, o_aq + 128, o_ff, o_ff + 128, o_fb, o_fb + 128] + [o_qkv + i * 128 for i in range(6)] + \
                  [o_cu, o_cu + 128] + [o_xbc + i * 128 for i in range(4)]
        sts = [A.alloc("st", 8) for _ in range(2)]
        hTs = [A.alloc("hT", 8 * 512) for _ in range(1)]
        xts = [A.alloc("xt", D) for _ in range(2)]
        xns = [A.alloc("xn", D) for _ in range(2)]
        zts = [A.alloc("zt", ZT_COLS) for _ in range(2)]
        zfs = [A.alloc("zf", 512) for _ in range(2)]
        gcnt = 0
        for s in range(NS):
            for tg in range(0, NT, 4):
                ntile = min(4, NT - tg)
                ntok = ntile * 128
                hT = hTs[0]
                gcnt += 1
                for ti in range(ntile):
                    t = tg + ti
                    ms = NS if t * 128 < TC else s
                    xt = xts[t % 2]
                    xn = xns[t % 2]
                    st = sts[t % 2]
                    C.dma("sp", xt[:], X[s, t * 128:(t + 1) * 128, :], [X], [xt])
                    layernorm(C, xt, xn, st)
                    for half in range(2):
                        ps = PS[2 + half]
                        for kk in range(4):
                            k = half * 4 + kk
                            C.tr(ps, ps[:, kk * 128:(kk + 1) * 128], xn[:, k * 128:(k + 1) * 128], ident, [xn])
                        for kk in range(4):
                            k = half * 4 + kk
                            C.v("act", lambda e, ps=ps, kk=kk, k=k, ti=ti, ms=ms, hT=hT: e.activation(
                                hT[:, k * 512 + ti * 128: k * 512 + (ti + 1) * 128], ps[:, kk * 128:(kk + 1) * 128],
                                AF.Identity, bias=mf(0, k, ms), scale=mf(1, k, ms)), [ps, modF], [hT])
                C.dma("sp", HT[s, :, :, tg * 128: tg * 128 + ntok].rearrange("k p n -> p k n"),
                      hT[:].rearrange("p (k n) -> p k n", k=8)[:, :, 0:ntok], [hT], [HT])
                for ti in range(ntile):
                    t = tg + ti
                    zt = zts[t % 2]
                    for gi, (c0, wd, d0) in enumerate(tm_groups):
                        ps = PS[4 + gi % 2]
                        for k in range(8):
                            C.mm(ps, ps[:, 0:wd], hT[:, k * 512 + ti * 128: k * 512 + (ti + 1) * 128],
                                 wI[:, k * IN_COLS + c0: k * IN_COLS + c0 + wd], [hT, wI], start=(k == 0), stop=(k == 7))
                        C.v("dve" if gi % 2 == 0 else "act",
                            (lambda e, ps=ps, zt=zt, wd=wd, d0=d0: e.tensor_copy(zt[:, d0:d0 + wd], ps[:, 0:wd])) if gi % 2 == 0 else
                            (lambda e, ps=ps, zt=zt, wd=wd, d0=d0: e.copy(zt[:, d0:d0 + wd], ps[:, 0:wd])), [ps], [zt])
                    C.dma("sp", ZT[s, t * 128:(t + 1) * 128, :], zt[:], [zt], [ZT])
                for ci, c0 in enumerate(fm_cols):
                    ps = PS[6 + ci % 2]
                    zf = zfs[ci % 2]
                    for k in range(8):
                        C.mm(ps, ps[:, 0:ntok], wI[:, k * IN_COLS + c0: k * IN_COLS + c0 + 128], hT[:, k * 512: k * 512 + ntok],
                             [hT, wI], start=(k == 0), stop=(k == 7))
                    C.v("dve" if ci % 2 == 0 else "act",
                        (lambda e, ps=ps, zf=zf, ntok=ntok: e.tensor_copy(zf[:, 0:ntok], ps[:, 0:ntok])) if ci % 2 == 0 else
                        (lambda e, ps=ps, zf=zf, ntok=ntok: e.copy(zf[:, 0:ntok], ps[:, 0:ntok])), [ps], [zf])
                    C.dma("sp", ZF[s, ci * 128:(ci + 1) * 128, tg * 128: tg * 128 + ntok], zf[:, 0:ntok], [zf], [ZF])
        A.release(mAB)
        for s in range(NS):
            if "ssd" in cfg.parts:
                ssd_mixer(C, A, PS, K, cfg, l, s, ZT, ZF, O, W_)
            if "hgrn" in cfg.parts:
                hgrn_mixer(C, A, PS, K, cfg, l, s, ZT, ZF, O, W_)
            if "gdn" in cfg.parts:
                gdn_mixer(C, A, PS, K, cfg, l, s, ZT, ZF, O, W_)
            if "fnet" in cfg.parts:
                fnet_mixer(C, A, PS, K, cfg, l, s, ZF, O, W_)
        if "merge" in cfg.parts:
            for s in range(NS):
                merge_phase(C, A, PS, K, cfg, l, s, X, HT, O, MOD, W_)
        if "moe" in cfg.parts:
            for s in range(NS):
                moe_phase(C, A, PS, K, cfg, l, s, X, H2, YE, IDX, MOD, W_)
        A.release(mL)

    for k, t in dbg.items():
        src = {"dbg_X": X, "dbg_MOD": MOD, "dbg_ZT": ZT, "dbg_ZF": ZF, "dbg_O": O}[k]
        sap = src.ap
        if k == "dbg_O":
            c0 = dbg_cols[0]
            sap = src.ap[:, :, c0:c0 + t.ap.shape[2]]
        C.dma("sp", t.ap, sap, [src], [t])
    for s in range(NS):
        C.dma("sp", out[s, :, :], X[s, TC:, :], [X], [out])
    S.emit(nc, es)
    return nc, es


def _host_weights(inp, depth):
    f = lambda a: np.ascontiguousarray(np.asarray(a, dtype=np.float32))
    return {
        "ada_w": f(inp["ada_w"][:depth]), "ada_b": f(inp["ada_b"][:depth]), "w_in": f(inp["w_in"][:depth]),
        "consts": CONSTS_NP,
        "ssd_dt_bias": f(np.asarray(inp["ssd_dt_bias"])[:depth].reshape(depth, 8)),
        "ssd_a_log": f(np.asarray(inp["ssd_a_log"])[:depth].reshape(depth, 8)),
        "ssd_d": f(inp["ssd_d"][:depth]), "ssd_norm_w": f(inp["ssd_norm_w"][:depth]),
        "ssd_cwT": f(np.asarray(inp["ssd_conv_w"])[:depth].transpose(0, 2, 1)),
        "ssd_cbT": f(np.asarray(inp["ssd_conv_b"])[:depth][:, :, None]),
        "bd64": make_bd64(),
        "w_router": f(inp["w_router"][:depth]), "w_ff_gate": f(inp["w_ff_gate"][:depth]), "w_ff_up": f(inp["w_ff_up"][:depth]),
        "w_ff_down": f(inp["w_ff_down"][:depth]), "ln2_g": f(inp["ln2_g"][:depth]), "ln2_b": f(inp["ln2_b"][:depth]),
        "w_gate": f(inp["w_gate"][:depth]), "w_branch": f(inp["w_branch"][:depth]), "w_out": f(inp["w_out"][:depth]),
        "b_gateT": f(np.asarray(inp["b_gate"])[:depth].reshape(depth, 4, 8, 128).transpose(0, 3, 1, 2).reshape(depth, 128, 32)),
        "ln1_g": f(inp["ln1_g"][:depth]), "ln1_b": f(inp["ln1_b"][:depth]),
        "gdn_dt_bias": f(np.asarray(inp["gdn_dt_bias"])[:depth].reshape(depth, 8)),
        "gdn_a_log": f(np.asarray(inp["gdn_a_log"])[:depth].reshape(depth, 8)),
        "gdn_norm_w": f(inp["gdn_norm_w"][:depth]),
        "gdn_cwT": f(np.asarray(inp["gdn_conv_w"])[:depth].transpose(0, 2, 1)),
        "hgrn_lb_logits": f(np.asarray(inp["hgrn_lb_logits"])[:, :depth]), "hgrn_norm_w": f(inp["hgrn_norm_w"][:depth]),
    }


def _prepare(inp, n_cores, cfg=None):
    x = np.asarray(inp["x"], dtype=np.float32)
    ctx = np.asarray(inp["ctx"], dtype=np.float32)
    c = np.asarray(inp["c"], dtype=np.float32)
    c_ctx = np.asarray(inp["c_ctx"], dtype=np.float32)
    B, TL, _ = x.shape
    TC = ctx.shape[1]
    NS = B // n_cores
    depth = int(np.asarray(inp["ada_w"]).shape[0])
    if cfg is None:
        cfg = Cfg(NS, TC, TL, depth)
    wts = _host_weights(inp, depth)
    wts["dft_ctx"] = make_dft(TC)
    wts["dft_lat"] = make_dft(TL)
    pos = sincos_grid_np(TL // 64, 64, D)
    in_maps = []
    for i in range(n_cores):
        sl = slice(i * NS, (i + 1) * NS)
        call = np.concatenate([c[sl], c_ctx[None]], 0)
        cT = np.ascontiguousarray(call.reshape(NS + 1, 8, 128).transpose(2, 1, 0))
        m = {"x": np.ascontiguousarray(x[sl]), "ctx": np.ascontiguousarray(ctx[sl]), "cT": cT, "pos": pos}
        m.update(wts)
        in_maps.append(m)
    return cfg, in_maps


def kernel_sim(inp, cfg, simrun):
    cfg, in_maps = _prepare(inp, 1, cfg)
    nc, es = build(cfg)
    res = simrun(nc, in_maps)
    return np.concatenate([np.asarray(r["out"], dtype=np.float32) for r in res], axis=0)


def kernel(**inp):
    n_cores = 8
    cfg, in_maps = _prepare(inp, n_cores)
    nc, es = build(cfg)
    res = run_bass_kernel_spmd(nc, in_maps, core_ids=list(range(n_cores)))
    es.close()
    return np.concatenate([np.asarray(r["out"], dtype=np.float32) for r in res.results], axis=0)
```

```python
import math
from contextlib import ExitStack
import numpy as np
import concourse.bass as bass
import concourse.mybir as mybir
from concourse.bass_utils import run_bass_kernel_spmd

F32 = mybir.dt.float32
I32 = mybir.dt.int32
U32 = mybir.dt.uint32
AF = mybir.ActivationFunctionType
ALU = mybir.AluOpType
AX = mybir.AxisListType

D = 1024
W = 256
H = 4
HD = 64
E = 16
FF = 1024
IN_COLS = 3352
EPS = 1e-6
NDMA_SEMS = 40


class Res:
    __slots__ = ("name", "w", "r", "psum")

    def __init__(self, name):
        self.name = name
        self.w = None
        self.r = {}
        self.psum = False


class Sched:
    ENGS = ("pe", "dve", "act", "pool", "sp")

    def __init__(self):
        self.q = {e: [] for e in self.ENGS}
        self.cnt = {e: 0 for e in self.ENGS}
        self.seen = {e: {} for e in self.ENGS}
        self.dcnt = [0] * NDMA_SEMS
        self.dnext = 0
        self.n = 0

    def _waits(self, eng, reads, writes, extra=()):
        waits = {}

        def need(kv):
            if kv is None:
                return
            k, v = kv
            if v > waits.get(k, 0):
                waits[k] = v
        for R in reads:
            need(R.w)
            if R.psum:
                for k, v in R.r.items():
                    if k != eng:
                        need((k, v))
        for Wr in writes:
            need(Wr.w)
            for k, v in Wr.r.items():
                need((k, v))
        for kv in extra:
            need(kv)
        out = []
        seen = self.seen[eng]
        for k, v in waits.items():
            if seen.get(k, 0) < v:
                seen[k] = v
                out.append((k, v))
        return out

    def op(self, eng, fn, reads=(), writes=()):
        waits = self._waits(eng, reads, writes)
        if eng == "pe":
            waits = [kv for kv in waits if kv[0] != "pe"]
        self.cnt[eng] += 1
        c = self.cnt[eng]
        self.q[eng].append((waits, fn, (eng, 1)))
        for R in reads:
            if R.r.get(eng, 0) < c:
                R.r[eng] = c
        for Wr in writes:
            Wr.w = (eng, c)
            Wr.r = {}
        self.n += 1

    def dma(self, eng, out_ap, in_ap, reads=(), writes=(), **kw):
        s = self.dnext
        self.dnext = (self.dnext + 1) % NDMA_SEMS
        key = ("d", s)
        prev = self.dcnt[s]
        waits = self._waits(eng, reads, writes, extra=((key, prev),) if prev else ())
        self.dcnt[s] += 16
        c = self.dcnt[s]
        self.q[eng].append((waits, lambda e: e.dma_start(out=out_ap, in_=in_ap, **kw), (key, 16)))
        for R in reads:
            R.r[key] = c
        for Wr in writes:
            Wr.w = (key, c)
            Wr.r = {}
        self.n += 1

    def barrier(self):
        for eng in self.ENGS:
            waits = []
            seen = self.seen[eng]
            for o in self.ENGS:
                if o != eng and self.cnt[o] > seen.get(o, 0):
                    seen[o] = self.cnt[o]
                    waits.append((o, self.cnt[o]))
            for s in range(NDMA_SEMS):
                k = ("d", s)
                if self.dcnt[s] > seen.get(k, 0):
                    seen[k] = self.dcnt[s]
                    waits.append((k, self.dcnt[s]))
            if waits:
                self.q[eng].append((waits, None, None))

    def emit(self, nc, es):
        sems = {e: es.enter_context(nc.semaphore("s_" + e)) for e in self.ENGS}
        for s in range(NDMA_SEMS):
            sems[("d", s)] = es.enter_context(nc.semaphore("sd%d" % s))
        self.barrier()
        block = es.enter_context(nc.Block())

        def run(e, lst):
            for waits, fn, inc in lst:
                for k, v in waits:
                    e.wait_ge(sems[k], v)
                if fn is not None:
                    fn(e).then_inc(sems[inc[0]], inc[1])

        @block.tensor
        def _(e):
            run(e, self.q["pe"])

        @block.vector
        def _(e):
            run(e, self.q["dve"])

        @block.scalar
        def _(e):
            run(e, self.q["act"])

        @block.gpsimd
        def _(e):
            run(e, self.q["pool"])

        @block.sync
        def _(e):
            run(e, self.q["sp"])


class T:
    def __init__(self, ap, name):
        self.ap = ap
        self.res = Res(name)

    def __getitem__(self, k):
        return self.ap[k]


class Ctx:
    def __init__(self, nc, es):
        self.nc = nc
        self.es = es
        self.S = Sched()
        self.uid = 0

    def sb(self, name, shape, dt=F32):
        self.uid += 1
        t = self.es.enter_context(self.nc.sbuf_tensor("%s_%d" % (name, self.uid), list(shape), dt))
        return T(t, name)

    def ps(self, name, shape=(128, 512), dt=F32):
        self.uid += 1
        t = self.es.enter_context(self.nc.psum_tensor("%s_%d" % (name, self.uid), list(shape), dt))
        r = T(t, name)
        r.res.psum = True
        return r

    def dram(self, name, shape, dt=F32, kind="Internal"):
        t = self.nc.dram_tensor(name, list(shape), dt, kind=kind)
        return T(t.ap(), name)

    def mm(self, out_t, out_ap, lhsT, rhs, reads, start=True, stop=True):
        rd = [x.res for x in reads]
        wr = [out_t.res]
        if not start:
            rd = rd + [out_t.res]
        self.S.op("pe", lambda e: e.matmul(out_ap, lhsT, rhs, start=start, stop=stop), rd, wr)

    def tr(self, out_t, out_ap, in_ap, ident, reads):
        self.S.op("pe", lambda e: e.transpose(out_ap, in_ap, ident.ap[:]), [x.res for x in reads] + [ident.res], [out_t.res])

    def v(self, eng, fn, reads, writes):
        self.S.op(eng, fn, [x.res for x in reads], [x.res for x in writes])

    def dma(self, eng, out_ap, in_ap, reads, writes, **kw):
        self.S.dma(eng, out_ap, in_ap, [x.res for x in reads], [x.res for x in writes], **kw)

    def i(self, eng, name, *args, R=(), Wr=(), **kw):
        self.S.op(eng, lambda e: getattr(e, name)(*args, **kw), [x.res for x in R], [x.res for x in Wr])


def sincos_grid_np(rows, cols, dim):
    quarter = dim // 4
    omega = (1.0 / (10000.0 ** (np.arange(quarter, dtype=np.float32) / np.float32(quarter)))).astype(np.float32)
    er = np.arange(rows, dtype=np.float32)[:, None] * omega
    ec = np.arange(cols, dtype=np.float32)[:, None] * omega
    er = np.concatenate([np.sin(er), np.cos(er)], axis=-1)
    ec = np.concatenate([np.sin(ec), np.cos(ec)], axis=-1)
    emb = np.concatenate([np.broadcast_to(er[:, None, :], (rows, cols, dim // 2)),
                          np.broadcast_to(ec[None, :, :], (rows, cols, dim // 2))], axis=-1)
    return np.ascontiguousarray(emb.reshape(rows * cols, dim).astype(np.float32))


class Cfg:
    def __init__(self, NS=2, TC=256, TL=2048, DEPTH=4):
        self.NS, self.TC, self.TL, self.DEPTH = NS, TC, TL, DEPTH
        self.T = TC + TL
        self.NT = self.T // 128
        self.NST = NS + 1
        self.parts = ("ssd", "fnet", "hgrn", "gdn", "merge", "moe")
        self.gdn_stage = 9


WNAMES = [("ada_w", (D, 6 * D)), ("ada_b", (6 * D,)), ("w_in", (D, IN_COLS))]


def layernorm(C, xt, out_t, st):
    nd = float(xt.ap.shape[-1])
    C.v("dve", lambda e: e.reduce_sum(st[:, 0:1], xt[:], AX.X), [xt], [st])
    C.v("act", lambda e: e.activation(out_t[:], xt[:], AF.Square, accum_out=st[:, 1:2]), [xt, st], [out_t, st])
    C.v("dve", lambda e: e.tensor_scalar(st[:, 2:3], st[:, 0:1], 1.0 / nd, None, ALU.mult), [st], [st])
    C.v("dve", lambda e: e.tensor_tensor(st[:, 3:4], st[:, 2:3], st[:, 2:3], ALU.mult), [st], [st])
    C.v("dve", lambda e: e.scalar_tensor_tensor(st[:, 4:5], st[:, 1:2], 1.0 / nd, st[:, 3:4], ALU.mult, ALU.subtract), [st], [st])
    C.v("act", lambda e: e.activation(st[:, 5:6], st[:, 4:5], AF.Ln, bias=EPS), [st], [st])
    C.v("act", lambda e: e.activation(st[:, 5:6], st[:, 5:6], AF.Exp, scale=-0.5), [st], [st])
    C.v("dve", lambda e: e.scalar_tensor_tensor(st[:, 6:7], st[:, 2:3], -1.0, st[:, 5:6], ALU.mult, ALU.mult), [st], [st])
    C.v("act", lambda e: e.activation(out_t[:], xt[:], AF.Identity, bias=st[:, 6:7], scale=st[:, 5:6]), [xt, st], [out_t])


class Arena:
    def __init__(self, C, cols):
        self.C = C
        self.t = C.es.enter_context(C.nc.sbuf_tensor("arena", [128, cols], F32))
        self.cols = cols
        self.off = 0

    def alloc(self, name, cols, parts=128):
        assert self.off + cols <= self.cols, "SBUF arena overflow at %s (%d + %d)" % (name, self.off, cols)
        t = T(self.t[0:parts, self.off:self.off + cols], name)
        self.off += cols
        return t

    def mark(self):
        return self.off

    def release(self, mark):
        self.C.S.barrier()
        self.off = mark


def make_consts():
    t = np.arange(128)
    c = {}
    c["ident"] = np.eye(128, dtype=np.float32)
    c["ones"] = np.ones((128, 128), np.float32)
    c["tri_f"] = (t[:, None] <= t[None, :]).astype(np.float32)
    c["tri_b"] = (t[:, None] >= t[None, :]).astype(np.float32)
    c["tris_f"] = (t[:, None] > t[None, :]).astype(np.float32)
    c["tris_b"] = (t[:, None] < t[None, :]).astype(np.float32)
    NEG = -30000.0
    negf = np.where(t[None, :] >= t[:, None], 0.0, NEG).astype(np.float32)
    negb = np.where(t[None, :] <= t[:, None], 0.0, NEG).astype(np.float32)
    c["neg4_f"] = np.tile(negf, (1, 4))
    c["neg4_b"] = np.tile(negb, (1, 4))
    c.update(make_gla_consts())
    c.update(make_gdn_consts())
    c.update(make_moe_consts())
    names = list(c.keys())
    offs = {}
    o = 0
    for n in names:
        offs[n] = (o, c[n].shape[1])
        o += c[n].shape[1]
    return np.ascontiguousarray(np.concatenate([c[n] for n in names], axis=1)), offs


def ssd_mixer(C, A, PS, K, cfg, l, s, ZT, ZF, O, W_):
    TC, TT, NT = cfg.TC, cfg.T, cfg.NT
    NTC = TC // 128
    m0 = A.mark()
    prm = A.alloc("ssd_prm", 64)
    C.dma("sp", prm[:, 0:8], W_["ssd_dt_bias"][l:l + 1, :].partition_broadcast(128), [W_["ssd_dt_bias"]], [prm])
    C.dma("sp", prm[:, 8:16], W_["ssd_a_log"][l:l + 1, :].partition_broadcast(128), [W_["ssd_a_log"]], [prm])
    C.dma("sp", prm[:, 16:20], W_["ssd_d"][l:l + 1, :].partition_broadcast(128), [W_["ssd_d"]], [prm])
    C.v("act", lambda e: e.activation(prm[:, 8:16], prm[:, 8:16], AF.Exp), [prm], [prm])
    C.v("dve", lambda e: e.tensor_scalar(prm[:, 8:16], prm[:, 8:16], -1.0, None, ALU.mult), [prm], [prm])
    nw = A.alloc("ssd_nw", 256)
    C.dma("sp", nw[:], W_["ssd_norm_w"][l:l + 1, :].partition_broadcast(128), [W_["ssd_norm_w"]], [nw])
    cw = A.alloc("ssd_cw", 6 * 6)
    grp = [(0, 128), (128, 128), (256, 64), (320, 64), (384, 64), (448, 64)]
    cv = [A.alloc("ssd_cv%d" % g, TT) for g in range(6)]
    raw = A.alloc("ssd_raw", TT)
    for g, (c0, n) in enumerate(grp):
        C.dma("sp", cw[0:n, g * 6:g * 6 + 5], W_["ssd_cwT"][l, c0:c0 + n, :], [W_["ssd_cwT"]], [cw])
        C.dma("sp", cw[0:n, g * 6 + 5:g * 6 + 6], W_["ssd_cbT"][l, c0:c0 + n, :], [W_["ssd_cbT"]], [cw])
        C.dma("sp", raw[0:n, :], ZF[s, 14 * 128 + c0: 14 * 128 + c0 + n, :], [ZF], [raw])
        acc = cv[g]
        C.v("dve", lambda e, acc=acc, n=n, g=g: e.tensor_scalar(acc[0:n, :], raw[0:n, :], cw[0:n, g * 6 + 2:g * 6 + 3], None, ALU.mult),
            [raw, cw], [acc])
        for (a, b) in ((0, TC), (TC, TT)):
            for k in (0, 1, 3, 4):
                sh = k - 2
                lo, hi = max(a, a - sh), min(b, b - sh)
                C.v("dve", lambda e, acc=acc, n=n, g=g, k=k, lo=lo, hi=hi, sh=sh: e.scalar_tensor_tensor(
                    acc[0:n, lo:hi], raw[0:n, lo + sh:hi + sh], cw[0:n, g * 6 + k:g * 6 + k + 1], acc[0:n, lo:hi], ALU.mult, ALU.add),
                    [raw, cw, acc], [acc])
        C.v("act", lambda e, acc=acc, n=n, g=g: e.activation(acc[0:n, :], acc[0:n, :], AF.Silu, bias=cw[0:n, g * 6 + 5:g * 6 + 6]),
            [acc, cw], [acc])
    dtA = A.alloc("ssd_dt", NT * 8)
    laA = A.alloc("ssd_la", NT * 8)
    C.dma("sp", dtA[:].rearrange("p (t c) -> p t c", c=8), ZT[s, :, 1552:1560].rearrange("(t p) c -> p t c", p=128), [ZT], [dtA])
    dt3 = dtA[:].rearrange("p (t c) -> p t c", c=8)
    la3 = laA[:].rearrange("p (t c) -> p t c", c=8)
    C.v("dve", lambda e: e.tensor_tensor(dt3, dt3, prm[:, 0:8].unsqueeze(1).to_broadcast([128, NT, 8]), ALU.add), [dtA, prm], [dtA])
    C.v("act", lambda e: e.activation(dtA[:], dtA[:], AF.Exp), [dtA], [dtA])
    C.v("act", lambda e: e.activation(dtA[:], dtA[:], AF.Ln, bias=1.0), [dtA], [dtA])
    C.v("dve", lambda e: e.tensor_tensor(la3, dt3, prm[:, 8:16].unsqueeze(1).to_broadcast([128, NT, 8]), ALU.mult), [dtA, prm], [laA])
    OF = A.alloc("ssd_of", NT * 256)
    XS = A.alloc("ssd_xs", NT * 256)
    Sst = A.alloc("ssd_S", 256)
    R4 = A.alloc("ssd_R4", 512)
    LT = A.alloc("ssd_LT", 512)
    Eg = A.alloc("ssd_Eg", 512)
    AT = A.alloc("ssd_AT", 512)
    qd = A.alloc("ssd_qd", 512)
    vv = A.alloc("ssd_v", 256)
    kd = A.alloc("ssd_kd", 256)
    bst = A.alloc("ssd_bst", 128)
    sm = A.alloc("ssd_sm", 16)
    zt = A.alloc("ssd_z", 256)
    yy = A.alloc("ssd_y", 256)
    y2 = A.alloc("ssd_y2", 256)
    pG, pU, pS, pO, pT, pD = PS[0], PS[1], PS[2], PS[3], PS[4], PS[5]
    for d in range(2):
        sfx = "_f" if d == 0 else "_b"
        tri, tris, neg4 = K("tri" + sfx), K("tris" + sfx), K("neg4" + sfx)
        last = 127 if d == 0 else 0
        order = list(range(NT)) if d == 0 else (list(range(NTC - 1, -1, -1)) + list(range(NT - 1, NTC - 1, -1)))
        C.v("dve", lambda e: e.memset(Sst[0:64, :], 0.0), [], [Sst])
        for t in order:
            ts = slice(t * 128, (t + 1) * 128)
            la = laA[:, t * 8 + d * 4: t * 8 + d * 4 + 4]
            dt = dtA[:, t * 8 + d * 4: t * 8 + d * 4 + 4]
            if d == 0:
                for c in range(2):
                    C.tr(pT, pT[:, c * 128:(c + 1) * 128], cv[c][:, ts], K.ident, [cv[c]])
                C.v("act", lambda e, t=t: e.copy(XS[:, t * 256:(t + 1) * 256], pT[:, 0:256]), [pT], [XS])
            xs = XS[:, t * 256:(t + 1) * 256]
            for g in range(2):
                C.S.op("pe", lambda e, g=g, ts=ts: e.transpose(pT[:, 256 + g * 64: 256 + (g + 1) * 64], cv[2 + g][0:64, ts], K.ident[0:64, 0:64]),
                       [cv[2 + g].res, K.t.res], [pT.res])
            C.v("act", lambda e: e.copy(bst[:], pT[:, 256:384]), [pT], [bst])
            C.v("dve", lambda e, xs=xs, dt=dt: e.tensor_tensor(vv[:].rearrange("p (h e) -> p h e", h=4), xs.rearrange("p (h e) -> p h e", h=4),
                                                               dt.unsqueeze(2).to_broadcast([128, 4, 64]), ALU.mult), [XS, dtA], [vv])
            C.v("dve", lambda e, la=la, tri=tri: e.tensor_tensor(R4[:].rearrange("p (h i) -> p h i", h=4), tri.unsqueeze(1).to_broadcast([128, 4, 128]),
                                                                 la.unsqueeze(2).to_broadcast([128, 4, 128]), ALU.mult), [K.t, laA], [R4])
            C.mm(pG, pG[:, :], K("ones"), R4[:], [K.t, R4], start=True, stop=False)
            C.mm(pG, pG[:, :], K("ident"), neg4, [K.t], start=False, stop=True)
            C.mm(pU, pU[:, :], K("ones"), R4[:], [K.t, R4])
            C.mm(pS, pS[:, 0:4], tri, la, [K.t, laA], start=True, stop=True)
            C.mm(pS, pS[:, 4:8], tris, la, [K.t, laA], start=True, stop=True)
            C.v("dve", lambda e: e.tensor_scalar(sm[:, 0:4], pS[:, 0:4], -1.0, None, ALU.mult), [pS], [sm])
            C.v("act", lambda e: e.activation(sm[:, 4:8], pS[:, 4:8], AF.Exp), [pS], [sm])
            for h in range(4):
                C.v("act", lambda e, h=h: e.activation(LT[:, h * 128:(h + 1) * 128], pG[:, h * 128:(h + 1) * 128], AF.Exp, bias=sm[:, h:h + 1]),
                    [pG, sm], [LT])
            C.v("act", lambda e: e.activation(Eg[0:64, :], pU[0:64, :], AF.Exp), [pU], [Eg])
            for g in range(2):
                C.mm(pS, pS[:, 128 + g * 128: 256 + g * 128], cv[2 + g][0:64, ts], cv[4 + g][0:64, ts], [cv[2 + g], cv[4 + g]])
            C.v("dve", lambda e: e.tensor_tensor(AT[:].rearrange("p (g a i) -> p g a i", g=2, a=2),
                                                 pS[:, 128:384].rearrange("p (g i) -> p g i", g=2).unsqueeze(2).to_broadcast([128, 2, 2, 128]),
                                                 LT[:].rearrange("p (g a i) -> p g a i", g=2, a=2), ALU.mult), [pS, LT], [AT])
            for h in range(4):
                C.v("dve", lambda e, h=h, ts=ts: e.tensor_tensor(
                    qd[0:64, h * 128:(h + 1) * 128], cv[4 + h // 2][0:64, ts], Eg[0:64, h * 128:(h + 1) * 128], ALU.mult),
                    [cv[4 + h // 2], Eg], [qd])
            C.v("dve", lambda e: e.tensor_tensor(kd[:].rearrange("p (g a n) -> p g a n", g=2, a=2),
                                                 bst[:].rearrange("p (g n) -> p g n", g=2).unsqueeze(2).to_broadcast([128, 2, 2, 64]),
                                                 sm[:, 4:8].rearrange("p (g a) -> p g a", g=2).unsqueeze(3).to_broadcast([128, 2, 2, 64]), ALU.mult),
                [bst, sm], [kd])
            for h in range(4):
                C.mm(pO, pO[:, h * 64:(h + 1) * 64], AT[:, h * 128:(h + 1) * 128], vv[:, h * 64:(h + 1) * 64], [AT, vv], start=True, stop=False)
                C.mm(pO, pO[:, h * 64:(h + 1) * 64], qd[0:64, h * 128:(h + 1) * 128], Sst[0:64, h * 64:(h + 1) * 64], [qd, Sst], start=False, stop=True)
            for h in range(4):
                C.mm(pD, pD[0:64, h * 64:(h + 1) * 64], kd[:, h * 64:(h + 1) * 64], vv[:, h * 64:(h + 1) * 64], [kd, vv])
            C.v("dve", lambda e, last=last: e.tensor_tensor(Sst[0:64, :].rearrange("p (h e) -> p h e", h=4), Sst[0:64, :].rearrange("p (h e) -> p h e", h=4),
                                                            Eg[0:64, :].rearrange("p (h i) -> p h i", h=4)[:, :, last:last + 1].to_broadcast([64, 4, 64]), ALU.mult),
                [Sst, Eg], [Sst])
            C.v("dve", lambda e: e.tensor_tensor(Sst[0:64, :], Sst[0:64, :], pD[0:64, 0:256], ALU.add), [Sst, pD], [Sst])
            if d == 0:
                C.v("act", lambda e, t=t: e.copy(OF[:, t * 256:(t + 1) * 256], pO[:, 0:256]), [pO], [OF])
            else:
                C.dma("sp", zt[:], ZT[s, t * 128:(t + 1) * 128, 1296:1552], [ZT], [zt])
                C.v("dve", lambda e, t=t: e.tensor_tensor(yy[:], pO[:, 0:256], OF[:, t * 256:(t + 1) * 256], ALU.add), [pO, OF], [yy])
                C.v("dve", lambda e, xs=xs: e.tensor_tensor(y2[:].rearrange("p (h e) -> p h e", h=4), xs.rearrange("p (h e) -> p h e", h=4),
                                                            prm[:, 16:20].unsqueeze(2).to_broadcast([128, 4, 64]), ALU.mult), [XS, prm], [y2])
                C.v("dve", lambda e: e.tensor_tensor(yy[:], yy[:], y2[:], ALU.add), [yy, y2], [yy])
                C.v("act", lambda e: e.activation(zt[:], zt[:], AF.Silu), [zt], [zt])
                C.v("dve", lambda e: e.tensor_tensor(yy[:], yy[:], zt[:], ALU.mult), [yy, zt], [yy])
                C.v("act", lambda e: e.activation(y2[:], yy[:], AF.Square, accum_out=sm[:, 8:9]), [yy, sm], [y2, sm])
                C.v("act", lambda e: e.activation(sm[:, 9:10], sm[:, 8:9], AF.Ln, bias=EPS, scale=1.0 / 256.0), [sm], [sm])
                C.v("act", lambda e: e.activation(sm[:, 9:10], sm[:, 9:10], AF.Exp, scale=-0.5), [sm], [sm])
                C.v("dve", lambda e: e.scalar_tensor_tensor(y2[:], yy[:], sm[:, 9:10], nw[:], ALU.mult, ALU.mult), [yy, sm, nw], [y2])
                C.dma("sp", O[s, t * 128:(t + 1) * 128, 768:1024], y2[:], [y2], [O])
    A.release(m0)


def make_dft(Tn):
    NTs = Tn // 128
    t = np.arange(Tn, dtype=np.float64)
    ang = 2.0 * np.pi * ((t[:, None] * t[None, :]) % Tn) / Tn
    sc = 1.0 / math.sqrt(64.0 * Tn)
    m = np.stack([np.cos(ang) * sc, -np.sin(ang) * sc]).astype(np.float32)
    m = m.reshape(2, NTs, 128, NTs, 128).transpose(0, 3, 2, 1, 4)
    return np.ascontiguousarray(m)


def make_bd64():
    ch = np.arange(256)
    same = (ch[:, None] // 64) == (ch[None, :] // 64)
    ang = 2.0 * np.pi * (((ch[:, None] % 64) * (ch[None, :] % 64)) % 64) / 64.0
    m = np.stack([np.where(same, np.cos(ang), 0.0), np.where(same, np.sin(ang), 0.0)]).astype(np.float32)
    return np.ascontiguousarray(m.reshape(2, 2, 128, 256))


def fnet_mixer(C, A, PS, K, cfg, l, s, ZF, O, W_):
    TC, TT = cfg.TC, cfg.T
    m0 = A.mark()
    bd = A.alloc("fn_bd", 1024)
    bd4 = bd[:].rearrange("p (a k n) -> p a k n", a=2, k=2)
    C.dma("sp", bd4, W_["bd64"][:, :, :, :].rearrange("a k p n -> p a k n"), [W_["bd64"]], [bd])
    yos = [A.alloc("fn_yo", 256) for _ in range(2)]
    m1 = A.mark()
    for name, a, b in (("ctx", 0, TC), ("lat", TC, TT)):
        Tn = b - a
        NTs = Tn // 128
        cu = A.alloc("fn_cu", 2 * Tn)
        cu3 = cu[:].rearrange("p (k t) -> p k t", k=2)
        for kc in range(2):
            C.dma("sp", cu3[:, kc, :], ZF[s, (12 + kc) * 128:(13 + kc) * 128, a:b], [ZF], [cu])
        XCS = A.alloc("fn_xcs", NTs * 512)
        X4 = XCS[:].rearrange("p (t a n) -> p t a n", t=NTs, a=2)
        for tt in range(NTs):
            for cs in range(2):
                ps = PS[cs]
                for kc in range(2):
                    C.mm(ps, ps[:, 0:256], cu3[:, kc, tt * 128:(tt + 1) * 128], bd4[:, cs, kc, :], [cu, bd], start=(kc == 0), stop=(kc == 1))
                C.v("dve" if cs == 0 else "act",
                    (lambda e, ps=ps, tt=tt, cs=cs, X4=X4: e.tensor_copy(X4[:, tt, cs, :], ps[:, 0:256])) if cs == 0 else
                    (lambda e, ps=ps, tt=tt, cs=cs, X4=X4: e.copy(X4[:, tt, cs, :], ps[:, 0:256])), [ps], [XCS])
        dms = [A.alloc("fn_dm", 2 * NTs * 128) for _ in range(2)]
        for kt in range(NTs):
            dm = dms[kt % 2]
            dm4 = dm[:].rearrange("p (a t k) -> p a t k", a=2, t=NTs)
            C.dma("sp", dm4, W_["dft_" + name][:, kt, :, :, :].rearrange("a p t k -> p a t k"), [W_["dft_" + name]], [dm])
            ps = PS[2 + kt % 2]
            n = 0
            for cs in range(2):
                for tt in range(NTs):
                    C.mm(ps, ps[:, 0:256], dm4[:, cs, tt, :], X4[:, tt, cs, :], [dm, XCS], start=(n == 0), stop=(n == 2 * NTs - 1))
                    n += 1
            yo = yos[kt % 2]
            C.v("dve", lambda e, ps=ps, yo=yo: e.tensor_copy(yo[:], ps[:, 0:256]), [ps], [yo])
            C.dma("sp", O[s, a + kt * 128: a + (kt + 1) * 128, 512:768], yo[:], [yo], [O])
        A.release(m1)
    A.release(m0)


def make_gla_consts():
    t = np.arange(128)
    WQ = np.zeros((128, 5, 128), np.float32)
    WK = np.zeros((128, 4, 128), np.float32)
    MK = np.zeros((128, 4, 128), np.float32)
    T_, I_ = t[:, None], t[None, :]
    r = 16 * (I_ // 16)
    WQ[:, 0, :] = ((T_ >= r) & (T_ <= I_))
    WK[:, 0, :] = -((T_ >= r) & (T_ <= I_)).astype(np.float32)
    MK[:, 0, :] = ((T_ // 16 == I_ // 16) & (T_ <= I_))
    for lev, B in ((1, 32), (2, 64), (3, 128)):
        m = B * (I_ // B) + B // 2
        WQ[:, lev, :] = ((I_ >= m) & (T_ >= m) & (T_ <= I_))
        WK[:, lev, :] = ((I_ < m) & (T_ > I_) & (T_ < m))
        mj = B * (T_ // B) + B // 2
        MK[:, lev, :] = ((T_ // B == I_ // B) & (T_ < mj) & (I_ >= mj))
    WQ[:, 4, :] = (T_ <= I_)
    out = {}
    for sfx, rev in (("_f", False), ("_b", True)):
        wq, wk, mk = (WQ[::-1, :, ::-1], WK[::-1, :, ::-1], MK[::-1, :, ::-1]) if rev else (WQ, WK, MK)
        out["wq" + sfx] = np.ascontiguousarray(np.concatenate([wq.reshape(128, 640), np.ones((128, 1), np.float32)], 1))
        out["wk" + sfx] = np.ascontiguousarray(wk.reshape(128, 512))
        out["mk" + sfx] = np.ascontiguousarray(mk.reshape(128, 512))
    return out


def hgrn_mixer(C, A, PS, K, cfg, l, s, ZT, ZF, O, W_):
    TC, TT, NT, DEPTH = cfg.TC, cfg.T, cfg.NT, cfg.DEPTH
    NTC = TC // 128
    m0 = A.mark()
    nw = A.alloc("hg_nw", 256)
    C.dma("sp", nw[:], W_["hgrn_norm_w"][l:l + 1, :].partition_broadcast(128), [W_["hgrn_norm_w"]], [nw])
    qT = A.alloc("hg_qT", 2 * TT)
    for c in range(2):
        C.dma("sp", qT[:, c * TT:(c + 1) * TT], ZF[s, c * 128:(c + 1) * 128, :], [ZF], [qT])
    C.i("act", "activation", qT[:], qT[:], AF.Silu, R=[qT], Wr=[qT])
    vT = A.alloc("hg_v", NT * 256)
    C.dma("sp", vT[:].rearrange("p (t c) -> p t c", c=256), ZT[s, :, 512:768].rearrange("(t p) c -> p t c", p=128), [ZT], [vT])
    OF = A.alloc("hg_of", NT * 256)
    kT = A.alloc("hg_kT", 2 * TT)
    lf = A.alloc("hg_lf", NT * 256)
    kt = A.alloc("hg_kt", NT * 256)
    lbB = A.alloc("hg_lb", 256)
    omB = A.alloc("hg_om", 256)
    lg = A.alloc("hg_lg", DEPTH * 256)
    omc = A.alloc("hg_omc", 4)
    Sst = [A.alloc("hg_S%d" % i, 64) for i in range(2)]
    EQ = A.alloc("hg_EQ", 512)
    EX = A.alloc("hg_EX", 129)
    EK = A.alloc("hg_EK", 512)
    qS = A.alloc("hg_qS", 128)
    kh = A.alloc("hg_kh", 128)
    AM = A.alloc("hg_AM", 512)
    yy = A.alloc("hg_y", 256)
    y2 = A.alloc("hg_y2", 256)
    gt = A.alloc("hg_g", 256)
    sm = A.alloc("hg_sm", 8)
    pEq, pEx, pEk, pKh, pAt, pO, pD, pT = PS
    for d in range(2):
        sfx = "_f" if d == 0 else "_b"
        wq, wk, mk, tris = K("wq" + sfx), K("wk" + sfx), K("mk" + sfx), K("tris" + sfx)
        C.dma("sp", lg[:].rearrange("p (j c) -> p j c", j=DEPTH), W_["hgrn_lb_logits"][d:d + 1, :, :].partition_broadcast(128),
              [W_["hgrn_lb_logits"]], [lg])
        C.i("act", "activation", lg[:], lg[:], AF.Exp, R=[lg], Wr=[lg])
        C.i("dve", "tensor_copy", omB[:], lg[:, 0:256], R=[lg], Wr=[omB])
        for j in range(1, DEPTH):
            C.i("dve", "tensor_tensor", omB[:], omB[:], lg[:, j * 256:(j + 1) * 256], ALU.add, R=[lg, omB], Wr=[omB])
        C.i("dve", "reciprocal", omB[:], omB[:], R=[omB], Wr=[omB])
        C.i("dve", "memset", lbB[:], 0.0, Wr=[lbB])
        for j in range(1, l + 1):
            C.i("dve", "tensor_tensor", lbB[:], lbB[:], lg[:, j * 256:(j + 1) * 256], ALU.add, R=[lg, lbB], Wr=[lbB])
        C.i("dve", "tensor_tensor", lbB[:], lbB[:], omB[:], ALU.mult, R=[lbB, omB], Wr=[lbB])
        C.i("dve", "tensor_scalar", omB[:], lbB[:], -1.0, 1.0, ALU.mult, ALU.add, R=[lbB], Wr=[omB])
        for c in range(2):
            C.tr(pT, pT[:, 0:128], omB[:, c * 128:(c + 1) * 128], K.ident, [omB])
            C.i("dve", "tensor_copy", omc[:, c:c + 1], pT[:, 0:1], R=[pT], Wr=[omc])
            C.i("dve", "tensor_scalar", omc[:, 2 + c:3 + c], pT[:, 0:1], -1.0, None, ALU.mult, R=[pT], Wr=[omc])
        for c in range(2):
            C.dma("sp", kT[:, c * TT:(c + 1) * TT], ZF[s, (2 + 2 * d + c) * 128:(3 + 2 * d + c) * 128, :], [ZF], [kT])
            C.i("act", "activation", kT[:, c * TT:(c + 1) * TT], kT[:, c * TT:(c + 1) * TT], AF.Sigmoid, R=[kT], Wr=[kT])
            C.i("dve", "tensor_scalar", kT[:, c * TT:(c + 1) * TT], kT[:, c * TT:(c + 1) * TT], omc[:, 2 + c:3 + c], omc[:, c:c + 1],
                ALU.mult, ALU.add, R=[kT, omc], Wr=[kT])
        C.dma("sp", lf[:].rearrange("p (t c) -> p t c", c=256), ZT[s, :, d * 256:(d + 1) * 256].rearrange("(t p) c -> p t c", p=128), [ZT], [lf])
        C.i("act", "activation", lf[:], lf[:], AF.Sigmoid, R=[lf], Wr=[lf])
        lf3 = lf[:].rearrange("p (t c) -> p t c", c=256)
        kt3 = kt[:].rearrange("p (t c) -> p t c", c=256)
        omB3 = omB[:].unsqueeze(1).to_broadcast([128, NT, 256])
        C.i("dve", "tensor_tensor", lf3, lf3, omB3, ALU.mult, R=[lf, omB], Wr=[lf])
        C.i("dve", "tensor_tensor", kt3, omB3, lf3, ALU.subtract, R=[lf, omB], Wr=[kt])
        C.i("dve", "tensor_tensor", lf3, lf3, lbB[:].unsqueeze(1).to_broadcast([128, NT, 256]), ALU.add, R=[lf, lbB], Wr=[lf])
        C.i("act", "activation", lf[:], lf[:], AF.Ln, R=[lf], Wr=[lf])
        order = list(range(NT)) if d == 0 else (list(range(NTC - 1, -1, -1)) + list(range(NT - 1, NTC - 1, -1)))
        for hp in range(2):
            C.i("dve", "memset", Sst[hp][:], 0.0, Wr=[Sst[hp]])
        for t in order:
            ts = slice(t * 128, (t + 1) * 128)
            for hp in range(2):
                lfs = lf[:, t * 256 + hp * 128: t * 256 + (hp + 1) * 128]
                qts = qT[:, hp * TT + t * 128: hp * TT + (t + 1) * 128]
                kts = kT[:, hp * TT + t * 128: hp * TT + (t + 1) * 128]
                C.mm(pEq, pEq[:, :], lfs, wq[:, 0:512], [lf, K.t])
                C.mm(pEx, pEx[:, 0:129], lfs, wq[:, 512:641], [lf, K.t])
                C.mm(pEk, pEk[:, :], lfs, wk, [lf, K.t])
                C.mm(pKh, pKh[:, 0:128], tris, lfs, [lf, K.t])
                C.i("act", "activation", EQ[:], pEq[:, :], AF.Exp, R=[pEq], Wr=[EQ])
                C.i("act", "activation", EX[:], pEx[:, 0:129], AF.Exp, R=[pEx], Wr=[EX])
                C.i("act", "activation", EK[:], pEk[:, :], AF.Exp, R=[pEk], Wr=[EK])
                C.i("act", "activation", kh[:], pKh[:, 0:128], AF.Exp, R=[pKh], Wr=[kh])
                C.i("dve", "tensor_tensor", EQ[:].rearrange("p (l i) -> p l i", l=4), EQ[:].rearrange("p (l i) -> p l i", l=4),
                    qts.unsqueeze(1).to_broadcast([128, 4, 128]), ALU.mult, R=[EQ, qT], Wr=[EQ])
                C.i("dve", "tensor_tensor", EK[:].rearrange("p (l i) -> p l i", l=4), EK[:].rearrange("p (l i) -> p l i", l=4),
                    kts.unsqueeze(1).to_broadcast([128, 4, 128]), ALU.mult, R=[EK, kT], Wr=[EK])
                C.i("dve", "tensor_tensor", qS[:], EX[:, 0:128], qts, ALU.mult, R=[EX, qT], Wr=[qS])
                C.i("dve", "tensor_tensor", kh[:], kh[:], kt[:, t * 256 + hp * 128: t * 256 + (hp + 1) * 128], ALU.mult, R=[kh, kt], Wr=[kh])
                for hh in range(2):
                    pr = slice(hh * 64, (hh + 1) * 64)
                    h = hp * 2 + hh
                    for lev in range(4):
                        C.mm(pAt, pAt[:, lev * 128:(lev + 1) * 128], EK[pr, lev * 128:(lev + 1) * 128], EQ[pr, lev * 128:(lev + 1) * 128], [EK, EQ])
                    C.i("dve", "tensor_tensor", AM[:], pAt[:, :], mk, ALU.mult, R=[pAt, K.t], Wr=[AM])
                    vs = vT[:, t * 256 + h * 64: t * 256 + (h + 1) * 64]
                    for lev in range(4):
                        C.mm(pO, pO[:, h * 64:(h + 1) * 64], AM[:, lev * 128:(lev + 1) * 128], vs, [AM, vT], start=(lev == 0), stop=False)
                    C.mm(pO, pO[:, h * 64:(h + 1) * 64], qS[pr, :], Sst[hp][pr, :], [qS, Sst[hp]], start=False, stop=True)
                C.mm(pD, pD[:, 0:128], kh[:], vT[:, t * 256 + hp * 128: t * 256 + (hp + 1) * 128], [kh, vT])
                for hh in range(2):
                    pr = slice(hh * 64, (hh + 1) * 64)
                    C.i("dve", "scalar_tensor_tensor", Sst[hp][pr, :], Sst[hp][pr, :], EX[pr, 128:129], pD[pr, hh * 64:(hh + 1) * 64],
                        ALU.mult, ALU.add, R=[Sst[hp], EX, pD], Wr=[Sst[hp]])
            if d == 0:
                C.i("act", "copy", OF[:, t * 256:(t + 1) * 256], pO[:, 0:256], R=[pO], Wr=[OF])
            else:
                C.dma("sp", gt[:], ZT[s, t * 128:(t + 1) * 128, 768:1024], [ZT], [gt])
                C.i("dve", "tensor_tensor", yy[:], pO[:, 0:256], OF[:, t * 256:(t + 1) * 256], ALU.add, R=[pO, OF], Wr=[yy])
                C.i("dve", "tensor_tensor", y2[:], yy[:], yy[:], ALU.mult, R=[yy], Wr=[y2])
                C.i("dve", "tensor_reduce", sm[:, 0:4], y2[:].rearrange("p (h e) -> p h e", h=4), AX.X, ALU.add, R=[y2], Wr=[sm])
                C.i("act", "activation", sm[:, 0:4], sm[:, 0:4], AF.Ln, bias=EPS, scale=1.0 / 64.0, R=[sm], Wr=[sm])
                C.i("act", "activation", sm[:, 0:4], sm[:, 0:4], AF.Exp, scale=-0.5, R=[sm], Wr=[sm])
                C.i("act", "activation", gt[:], gt[:], AF.Silu, R=[gt], Wr=[gt])
                C.i("dve", "tensor_tensor", yy[:].rearrange("p (h e) -> p h e", h=4), yy[:].rearrange("p (h e) -> p h e", h=4),
                    sm[:, 0:4].unsqueeze(2).to_broadcast([128, 4, 64]), ALU.mult, R=[yy, sm], Wr=[yy])
                C.i("dve", "tensor_tensor", yy[:], yy[:], nw[:], ALU.mult, R=[yy, nw], Wr=[yy])
                C.i("dve", "tensor_tensor", y2[:], yy[:], gt[:], ALU.mult, R=[yy, gt], Wr=[y2])
                C.dma("sp", O[s, t * 128:(t + 1) * 128, 0:256], y2[:], [y2], [O])
    A.release(m0)


def make_gdn_consts():
    t = np.arange(128)
    I_, J_ = t[:, None], t[None, :]
    c = {}
    for sfx, valid in (("_f", J_ < I_), ("_b", J_ > I_)):
        c["poss4" + sfx] = np.tile(np.where(valid, 0.0, 30000.0).astype(np.float32), (1, 4))
    bm = lambda b: (I_ // b == J_ // b)
    c["bm32"] = bm(32).astype(np.float32)
    c["be64"] = (bm(64) & ~bm(32)).astype(np.float32)
    c["be128"] = (~bm(64)).astype(np.float32)
    c["blk64"] = bm(64).astype(np.float32)
    return c


def gdn_mixer(C, A, PS, K, cfg, l, s, ZT, ZF, O, W_):
    TC, TT, NT = cfg.TC, cfg.T, cfg.NT
    NTC = TC // 128
    m0 = A.mark()
    prm = A.alloc("gd_prm", 32)
    C.dma("sp", prm[:, 0:8], W_["gdn_dt_bias"][l:l + 1, :].partition_broadcast(128), [W_["gdn_dt_bias"]], [prm])
    C.dma("sp", prm[:, 8:16], W_["gdn_a_log"][l:l + 1, :].partition_broadcast(128), [W_["gdn_a_log"]], [prm])
    C.i("act", "activation", prm[:, 8:16], prm[:, 8:16], AF.Exp, R=[prm], Wr=[prm])
    C.i("dve", "tensor_scalar", prm[:, 8:16], prm[:, 8:16], -1.0, None, ALU.mult, R=[prm], Wr=[prm])
    nw = A.alloc("gd_nw", 256)
    C.dma("sp", nw[:], W_["gdn_norm_w"][l:l + 1, :].partition_broadcast(128), [W_["gdn_norm_w"]], [nw])
    cw = A.alloc("gd_cw", 6 * 5)
    cv = [A.alloc("gd_cv%d" % g, TT) for g in range(6)]
    raw = A.alloc("gd_raw", TT)
    for g in range(6):
        C.dma("sp", cw[:, g * 5:g * 5 + 5], W_["gdn_cwT"][l, g * 128:(g + 1) * 128, :], [W_["gdn_cwT"]], [cw])
        C.dma("sp", raw[:], ZF[s, (6 + g) * 128:(7 + g) * 128, :], [ZF], [raw])
        acc = cv[g]
        C.i("dve", "tensor_scalar", acc[:], raw[:], cw[:, g * 5 + 2:g * 5 + 3], None, ALU.mult, R=[raw, cw], Wr=[acc])
        for (a, b) in ((0, TC), (TC, TT)):
            for k in (0, 1, 3, 4):
                sh = k - 2
                lo, hi = max(a, a - sh), min(b, b - sh)
                C.i("dve", "scalar_tensor_tensor", acc[:, lo:hi], raw[:, lo + sh:hi + sh], cw[:, g * 5 + k:g * 5 + k + 1], acc[:, lo:hi],
                    ALU.mult, ALU.add, R=[raw, cw, acc], Wr=[acc])
        C.i("act", "activation", acc[:], acc[:], AF.Silu, R=[acc], Wr=[acc])
    sq = A.alloc("gd_sq", 512)
    for g in range(4):
        for c0 in range(0, TT, 512):
            n = min(512, TT - c0)
            ps = PS[g % 2]
            C.i("act", "activation", sq[:, 0:n], cv[g][:, c0:c0 + n], AF.Square, R=[cv[g]], Wr=[sq])
            C.mm(ps, ps[:, 0:n], K("blk64"), sq[:, 0:n], [K.t, sq])
            C.i("act", "activation", sq[:, 0:n], ps[:, 0:n], AF.Ln, bias=EPS, R=[ps], Wr=[sq])
            C.i("act", "activation", sq[:, 0:n], sq[:, 0:n], AF.Exp, scale=-0.5, R=[sq], Wr=[sq])
            if g < 2:
                C.i("dve", "scalar_tensor_tensor", cv[g][:, c0:c0 + n], cv[g][:, c0:c0 + n], 0.125, sq[:, 0:n], ALU.mult, ALU.mult,
                    R=[cv[g], sq], Wr=[cv[g]])
            else:
                C.i("dve", "tensor_tensor", cv[g][:, c0:c0 + n], cv[g][:, c0:c0 + n], sq[:, 0:n], ALU.mult, R=[cv[g], sq], Wr=[cv[g]])
    laA = A.alloc("gd_la", NT * 8)
    btA = A.alloc("gd_bt", NT * 8)
    C.dma("sp", laA[:].rearrange("p (t c) -> p t c", c=8), ZT[s, :, 1280:1288].rearrange("(t p) c -> p t c", p=128), [ZT], [laA])
    C.dma("sp", btA[:].rearrange("p (t c) -> p t c", c=8), ZT[s, :, 1288:1296].rearrange("(t p) c -> p t c", p=128), [ZT], [btA])
    la3 = laA[:].rearrange("p (t c) -> p t c", c=8)
    C.i("dve", "tensor_tensor", la3, la3, prm[:, 0:8].unsqueeze(1).to_broadcast([128, NT, 8]), ALU.add, R=[laA, prm], Wr=[laA])
    C.i("act", "activation", laA[:], laA[:], AF.Exp, R=[laA], Wr=[laA])
    C.i("act", "activation", laA[:], laA[:], AF.Ln, bias=1.0, R=[laA], Wr=[laA])
    C.i("dve", "tensor_tensor", la3, la3, prm[:, 8:16].unsqueeze(1).to_broadcast([128, NT, 8]), ALU.mult, R=[laA, prm], Wr=[laA])
    C.i("act", "activation", btA[:], btA[:], AF.Sigmoid, R=[btA], Wr=[btA])
    OF = A.alloc("gd_of", NT * 256)
    Sst = [A.alloc("gd_S%d" % i, 64) for i in range(2)]
    R4, LT, LL, Eg = A.alloc("gd_R4", 512), A.alloc("gd_LT", 512), A.alloc("gd_L", 512), A.alloc("gd_Eg", 512)
    ktok, vtok = A.alloc("gd_ktok", 256), A.alloc("gd_vtok", 256)
    bv, bk, kd = A.alloc("gd_bv", 256), A.alloc("gd_bk", 256), A.alloc("gd_kd", 256)
    qd = A.alloc("gd_qd", 256)
    sm = A.alloc("gd_sm", 32)
    Am, AmT, QKL = A.alloc("gd_A", 128), A.alloc("gd_AT", 128), A.alloc("gd_QKL", 128)
    P_, PT_, N_, NT_ = A.alloc("gd_P", 128), A.alloc("gd_PT", 128), A.alloc("gd_N", 128), A.alloc("gd_NT", 128)
    P2, PT2, Y1, Y2, E1, E1T = (A.alloc("gd_P2", 128), A.alloc("gd_PT2", 128), A.alloc("gd_Y1", 128), A.alloc("gd_Y2", 128),
                                A.alloc("gd_E1", 128), A.alloc("gd_E1T", 128))
    nwT = A.alloc("gd_nwT", 128)
    X1 = A.alloc("gd_X1", 256)
    vn = [A.alloc("gd_vn%d" % i, 128) for i in range(2)]
    yy, y2, gt = A.alloc("gd_y", 256), A.alloc("gd_y2", 256), A.alloc("gd_g", 256)
    pG, pG2, pU, pM, pA, pB, pV, pO = PS
    ident = K("ident")
    GS = cfg.gdn_stage
    for d in range(2):
        if GS < 2:
            break
        sfx = "_f" if d == 0 else "_b"
        tri, tris, neg4, poss4 = K("tri" + sfx), K("tris" + sfx), K("neg4" + sfx), K("poss4" + sfx)
        last = 127 if d == 0 else 0
        order = list(range(NT)) if d == 0 else (list(range(NTC - 1, -1, -1)) + list(range(NT - 1, NTC - 1, -1)))
        for hp in range(2):
            C.i("dve", "memset", Sst[hp][:], 0.0, Wr=[Sst[hp]])
        for t in order:
            ts = slice(t * 128, (t + 1) * 128)
            la = laA[:, t * 8 + d * 4: t * 8 + d * 4 + 4]
            bt = btA[:, t * 8 + d * 4: t * 8 + d * 4 + 4]
            for c in range(2):
                C.tr(pV, pV[:, 256 + c * 128: 384 + c * 128], cv[2 + c][:, ts], K.ident, [cv[2 + c]])
            C.i("act", "copy", ktok[:], pV[:, 256:512], R=[pV], Wr=[ktok])
            for c in range(2):
                C.tr(pV, pV[:, 256 + c * 128: 384 + c * 128], cv[4 + c][:, ts], K.ident, [cv[4 + c]])
            C.i("act", "copy", vtok[:], pV[:, 256:512], R=[pV], Wr=[vtok])
            C.i("dve", "tensor_tensor", R4[:].rearrange("p (h i) -> p h i", h=4), tri.unsqueeze(1).to_broadcast([128, 4, 128]),
                la.unsqueeze(2).to_broadcast([128, 4, 128]), ALU.mult, R=[K.t, laA], Wr=[R4])
            C.mm(pG, pG[:, :], K("ones"), R4[:], [K.t, R4], start=True, stop=False)
            C.mm(pG, pG[:, :], ident, neg4, [K.t], start=False, stop=True)
            C.mm(pG2, pG2[:, :], K("ones"), R4[:], [K.t, R4], start=True, stop=False)
            C.mm(pG2, pG2[:, :], ident, poss4, [K.t], start=False, stop=True)
            C.mm(pU, pU[:, :], K("ones"), R4[:], [K.t, R4])
            C.mm(pM, pM[:, 0:4], tri, la, [K.t, laA])
            C.mm(pM, pM[:, 4:8], tris, la, [K.t, laA])
            C.i("dve", "tensor_scalar", sm[:, 0:4], pM[:, 0:4], -1.0, None, ALU.mult, R=[pM], Wr=[sm])
            C.i("dve", "tensor_copy", sm[:, 4:8], pM[:, 0:4], R=[pM], Wr=[sm])
            C.i("act", "activation", sm[:, 8:12], pM[:, 4:8], AF.Exp, R=[pM], Wr=[sm])
            C.i("act", "activation", sm[:, 12:16], pM[:, 0:4], AF.Exp, R=[pM], Wr=[sm])
            C.i("dve", "tensor_tensor", sm[:, 16:20], sm[:, 12:16], bt, ALU.mult, R=[sm, btA], Wr=[sm])
            for h in range(4):
                C.i("act", "activation", LT[:, h * 128:(h + 1) * 128], pG[:, h * 128:(h + 1) * 128], AF.Exp, bias=sm[:, h:h + 1], R=[pG, sm], Wr=[LT])
                C.i("act", "activation", LL[:, h * 128:(h + 1) * 128], pG2[:, h * 128:(h + 1) * 128], AF.Exp, bias=sm[:, 4 + h:5 + h], scale=-1.0,
                    R=[pG2, sm], Wr=[LL])
            C.i("act", "activation", Eg[:], pU[:, :], AF.Exp, R=[pU], Wr=[Eg])
            C.i("dve", "tensor_tensor", bv[:].rearrange("p (h e) -> p h e", h=4), vtok[:].rearrange("p (h e) -> p h e", h=4),
                bt.unsqueeze(2).to_broadcast([128, 4, 64]), ALU.mult, R=[vtok, btA], Wr=[bv])
            C.i("dve", "tensor_tensor", bk[:].rearrange("p (h e) -> p h e", h=4), ktok[:].rearrange("p (h e) -> p h e", h=4),
                sm[:, 16:20].unsqueeze(2).to_broadcast([128, 4, 64]), ALU.mult, R=[ktok, sm], Wr=[bk])
            C.i("dve", "tensor_tensor", kd[:].rearrange("p (h e) -> p h e", h=4), ktok[:].rearrange("p (h e) -> p h e", h=4),
                sm[:, 8:12].unsqueeze(2).to_broadcast([128, 4, 64]), ALU.mult, R=[ktok, sm], Wr=[kd])
            if GS < 3:
                continue
            for hp in range(2):
                for hh in range(2):
                    h = hp * 2 + hh
                    pr = slice(hh * 64, (hh + 1) * 64)
                    kTs = cv[2 + hp][pr, ts]
                    qTs = cv[hp][pr, ts]
                    C.mm(pM, pM[:, 128:256], kTs, kTs, [cv[2 + hp]])
                    C.mm(pM, pM[:, 256:384], kTs, qTs, [cv[2 + hp], cv[hp]])
                    C.i("dve", "scalar_tensor_tensor", Am[:], pM[:, 128:256], bt[:, h:h + 1], LL[:, h * 128:(h + 1) * 128], ALU.mult, ALU.mult,
                        R=[pM, btA, LL], Wr=[Am])
                    C.i("dve", "tensor_tensor", QKL[:], pM[:, 256:384], LT[:, h * 128:(h + 1) * 128], ALU.mult, R=[pM, LT], Wr=[QKL])
                    C.tr(pM, pM[:, 384:512], Am[:], K.ident, [Am])
                    C.i("act", "copy", AmT[:], pM[:, 384:512], R=[pM], Wr=[AmT])
                    C.i("dve", "tensor_tensor", qd[pr, hp * 128:(hp + 1) * 128], qTs, Eg[pr, h * 128:(h + 1) * 128], ALU.mult, R=[cv[hp], Eg], Wr=[qd])
                    if GS < 4.1:
                        continue
                    C.i("dve", "tensor_tensor", P_[:], Am[:], K("bm32"), ALU.mult, R=[Am, K.t], Wr=[P_])
                    C.i("dve", "tensor_tensor", PT_[:], AmT[:], K("bm32"), ALU.mult, R=[AmT, K.t], Wr=[PT_])
                    C.i("dve", "tensor_tensor", N_[:], ident, P_[:], ALU.subtract, R=[K.t, P_], Wr=[N_])
                    C.i("dve", "tensor_tensor", NT_[:], ident, PT_[:], ALU.subtract, R=[K.t, PT_], Wr=[NT_])
                    C.i("dve", "tensor_tensor", E1[:], Am[:], K("be64"), ALU.mult, R=[Am, K.t], Wr=[E1])
                    C.i("dve", "tensor_tensor", E1T[:], AmT[:], K("be64"), ALU.mult, R=[AmT, K.t], Wr=[E1T])
                    cp, cpt = P_, PT_
                    if GS < 4.15:
                        continue
                    for lev in range(4 if GS >= 4.4 else 1):
                        np_, npt = (P2, PT2) if lev % 2 == 0 else (P_, PT_)
                        C.mm(pA, pA[:, 0:128], cpt[:], cp[:], [cpt, cp])
                        if GS >= 4.17:
                            C.mm(pA, pA[:, 128:256], cp[:], cpt[:], [cpt, cp])
                        if GS >= 4.18:
                            C.i("act", "copy", np_[:], pA[:, 0:128], R=[pA], Wr=[np_])
                        if GS >= 4.19:
                            C.i("act", "copy", npt[:], pA[:, 128:256], R=[pA], Wr=[npt])
                        if GS < 4.3:
                            continue
                        C.mm(pB, pB[:, 0:128], npt[:], N_[:], [npt, N_])
                        C.mm(pB, pB[:, 128:256], np_[:], NT_[:], [np_, NT_])
                        C.i("act", "copy", X1[:], pB[:, 0:256], R=[pB], Wr=[X1])
                        C.i("dve", "tensor_tensor", N_[:], N_[:], X1[:, 0:128], ALU.add, R=[N_, X1], Wr=[N_])
                        C.i("dve", "tensor_tensor", NT_[:], NT_[:], X1[:, 128:256], ALU.add, R=[NT_, X1], Wr=[NT_])
                        cp, cpt = np_, npt
                    if GS < 4.6:
                        continue
                    C.mm(pA, pA[:, 256:384], E1T[:], N_[:], [E1T, N_])
                    C.mm(pA, pA[:, 384:512], E1[:], NT_[:], [E1, NT_])
                    C.i("act", "copy", Y1[:], pA[:, 256:384], R=[pA], Wr=[Y1])
                    C.i("act", "copy", Y2[:], pA[:, 384:512], R=[pA], Wr=[Y2])
                    C.mm(pB, pB[:, 256:384], NT_[:], Y1[:], [NT_, Y1])
                    C.mm(pB, pB[:, 384:512], N_[:], Y2[:], [N_, Y2])
                    C.i("act", "copy", X1[:], pB[:, 256:512], R=[pB], Wr=[X1])
                    C.i("dve", "tensor_tensor", N_[:], N_[:], X1[:, 0:128], ALU.subtract, R=[N_, X1], Wr=[N_])
                    C.i("dve", "tensor_tensor", NT_[:], NT_[:], X1[:, 128:256], ALU.subtract, R=[NT_, X1], Wr=[NT_])
                    if GS < 4.8:
                        continue
                    C.i("dve", "tensor_tensor", E1[:], Am[:], K("be128"), ALU.mult, R=[Am, K.t], Wr=[E1])
                    C.mm(pA, pA[:, 0:128], E1[:], NT_[:], [E1, NT_])
                    C.i("act", "copy", Y2[:], pA[:, 0:128], R=[pA], Wr=[Y2])
                    C.mm(pB, pB[:, 0:128], N_[:], Y2[:], [N_, Y2])
                    C.i("act", "copy", X1[:, 0:128], pB[:, 0:128], R=[pB], Wr=[X1])
                    C.i("dve", "tensor_tensor", NT_[:], NT_[:], X1[:, 0:128], ALU.subtract, R=[NT_, X1], Wr=[NT_])
                    if GS < 5:
                        continue
                    C.mm(pV, pV[:, 128:256], bk[:, hp * 128:(hp + 1) * 128], NT_[:], [bk, NT_])
                    C.i("dve", "tensor_scalar", nwT[pr, :], pV[pr, 128:256], -1.0, None, ALU.mult, R=[pV], Wr=[nwT])
                    C.mm(pV, pV[:, 0:64], NT_[:], bv[:, h * 64:(h + 1) * 64], [NT_, bv], start=True, stop=False)
                    C.mm(pV, pV[:, 0:64], nwT[pr, :], Sst[hp][pr, :], [nwT, Sst[hp]], start=False, stop=True)
                    C.i("act", "copy", vn[hp][:, hh * 64:(hh + 1) * 64], pV[:, 0:64], R=[pV], Wr=[vn[hp]])
                    C.mm(pO, pO[:, h * 64:(h + 1) * 64], qd[pr, hp * 128:(hp + 1) * 128], Sst[hp][pr, :], [qd, Sst[hp]], start=True, stop=False)
                    C.mm(pO, pO[:, h * 64:(h + 1) * 64], QKL[:], vn[hp][:, hh * 64:(hh + 1) * 64], [QKL, vn[hp]], start=False, stop=True)
                if GS < 6:
                    continue
                C.mm(pO, pO[:, 256:384], kd[:, hp * 128:(hp + 1) * 128], vn[hp][:], [kd, vn[hp]])
                for hh in range(2):
                    h = hp * 2 + hh
                    pr = slice(hh * 64, (hh + 1) * 64)
                    C.i("dve", "scalar_tensor_tensor", Sst[hp][pr, :], Sst[hp][pr, :], Eg[pr, h * 128 + last: h * 128 + last + 1],
                        pO[pr, 256 + hh * 64: 256 + (hh + 1) * 64], ALU.mult, ALU.add, R=[Sst[hp], Eg, pO], Wr=[Sst[hp]])
            if GS < 6.5:
                continue
            if d == 0:
                C.i("act", "copy", OF[:, t * 256:(t + 1) * 256], pO[:, 0:256], R=[pO], Wr=[OF])
            else:
                if GS < 6.6:
                    continue
                C.dma("sp", gt[:], ZT[s, t * 128:(t + 1) * 128, 1024:1280], [ZT], [gt])
                if GS < 6.7:
                    continue
                C.i("act", "copy", yy[:], pO[:, 0:256], R=[pO], Wr=[yy])
                if GS < 6.8:
                    continue
                C.i("dve", "tensor_tensor", yy[:], yy[:], OF[:, t * 256:(t + 1) * 256], ALU.add, R=[yy, OF], Wr=[yy])
                C.i("dve", "tensor_tensor", y2[:], yy[:], yy[:], ALU.mult, R=[yy], Wr=[y2])
                C.i("dve", "tensor_reduce", sm[:, 24:28], y2[:].rearrange("p (h e) -> p h e", h=4), AX.X, ALU.add, R=[y2], Wr=[sm])
                C.i("act", "activation", sm[:, 24:28], sm[:, 24:28], AF.Ln, bias=EPS, scale=1.0 / 64.0, R=[sm], Wr=[sm])
                C.i("act", "activation", sm[:, 24:28], sm[:, 24:28], AF.Exp, scale=-0.5, R=[sm], Wr=[sm])
                C.i("act", "activation", gt[:], gt[:], AF.Silu, R=[gt], Wr=[gt])
                C.i("dve", "tensor_tensor", yy[:].rearrange("p (h e) -> p h e", h=4), yy[:].rearrange("p (h e) -> p h e", h=4),
                    sm[:, 24:28].unsqueeze(2).to_broadcast([128, 4, 64]), ALU.mult, R=[yy, sm], Wr=[yy])
                C.i("dve", "tensor_tensor", yy[:], yy[:], nw[:], ALU.mult, R=[yy, nw], Wr=[yy])
                C.i("dve", "tensor_tensor", y2[:], yy[:], gt[:], ALU.mult, R=[yy, gt], Wr=[y2])
                C.dma("sp", O[s, t * 128:(t + 1) * 128, 256:512], y2[:], [y2], [O])
    A.release(m0)


def token_groups(cfg, with_ctx):
    NTC, NT = cfg.TC // 128, cfg.NT
    gs = []
    if with_ctx:
        for a in range(0, NTC, 4):
            gs.append((a, min(4, NTC - a), True))
    for a in range(NTC, NT, 4):
        gs.append((a, min(4, NT - a), False))
    return gs


def merge_phase(C, A, PS, K, cfg, l, s, X, HT, O, MOD, W_):
    NS, TC, TT, NT, DEPTH = cfg.NS, cfg.TC, cfg.T, cfg.NT, cfg.DEPTH
    alpha = (2.0 * DEPTH) ** 0.25
    with_ctx = l < DEPTH - 1
    m0 = A.mark()
    wgs = [A.alloc("mg_wg", 4096) for _ in range(2)]
    wbs = [A.alloc("mg_wb", 1024) for _ in range(2)]
    wcnt = 0
    wo = A.alloc("mg_wo", 8192)
    hT = A.alloc("mg_hT", 4096)
    oT = A.alloc("mg_oT", 4096)
    mT = A.alloc("mg_mT", 4096)
    bgT = A.alloc("mg_bg", 32)
    g1B = [A.alloc("mg_g1_%d" % i, 1024) for i in range(2)]
    lnG, lnB = A.alloc("mg_lnG", 1024), A.alloc("mg_lnB", 1024)
    ot = A.alloc("mg_ot", 1024)
    xt, rr, xn = A.alloc("mg_xt", 1024), A.alloc("mg_r", 1024), A.alloc("mg_xn", 1024)
    sig, tmp = A.alloc("mg_sig", 512), A.alloc("mg_tmp", 512)
    st = A.alloc("mg_st", 8)
    C.dma("sp", bgT[:], W_["b_gateT"][l, :, :], [W_["b_gateT"]], [bgT])
    C.dma("sp", g1B[0][:], MOD[l, s:s + 1, 2048:3072].partition_broadcast(128), [MOD], [g1B[0]])
    C.dma("sp", g1B[1][:], MOD[l, NS:NS + 1, 2048:3072].partition_broadcast(128), [MOD], [g1B[1]])
    C.dma("sp", lnG[:], W_["ln1_g"][l:l + 1, :].partition_broadcast(128), [W_["ln1_g"]], [lnG])
    C.dma("sp", lnB[:], W_["ln1_b"][l:l + 1, :].partition_broadcast(128), [W_["ln1_b"]], [lnB])
    C.dma("sp", wo[:].rearrange("p (k n) -> p k n", k=8), W_["w_out"][l, :, :].rearrange("(k p) n -> p k n", p=128), [W_["w_out"]], [wo])
    for (t0, ntile, is_ctx) in token_groups(cfg, with_ctx):
        ntok = ntile * 128
        c0 = t0 * 128
        C.dma("sp", hT[:].rearrange("p (k n) -> p k n", k=8)[:, :, 0:ntok], HT[s, :, :, c0:c0 + ntok].rearrange("k p n -> p k n"), [HT], [hT])
        for ti in range(ntile):
            C.dma("sp", ot[:], O[s, c0 + ti * 128: c0 + (ti + 1) * 128, :], [O], [ot])
            for half in range(2):
                ps = PS[half]
                for kk in range(4):
                    C.tr(ps, ps[:, kk * 128:(kk + 1) * 128], ot[:, (half * 4 + kk) * 128:(half * 4 + kk + 1) * 128], K.ident, [ot])
                for kk in range(4):
                    k = half * 4 + kk
                    C.i("act" if kk % 2 else "dve", "copy" if kk % 2 else "tensor_copy",
                        oT[:, k * 512 + ti * 128: k * 512 + (ti + 1) * 128], ps[:, kk * 128:(kk + 1) * 128], R=[ps], Wr=[oT])
        for g in range(4):
          for nh in range(2):
            wg, wb = wgs[wcnt % 2], wbs[wcnt % 2]
            wcnt += 1
            C.dma("sp", wg[:].rearrange("p (k n) -> p k n", k=8), W_["w_gate"][l, g, :, nh * 512:(nh + 1) * 512].rearrange("(k p) n -> p k n", p=128),
                  [W_["w_gate"]], [wg])
            C.dma("sp", wb[:].rearrange("p (k n) -> p k n", k=2), W_["w_branch"][l, g, :, nh * 512:(nh + 1) * 512].rearrange("(k p) n -> p k n", p=128),
                  [W_["w_branch"]], [wb])
            for n in range(nh * 4, nh * 4 + 4):
                nn = n - nh * 4
                pg, pb = PS[2 + n % 2], PS[4 + n % 2]
                for k in range(8):
                    C.mm(pg, pg[:, 0:ntok], wg[:, k * 512 + nn * 128: k * 512 + (nn + 1) * 128], hT[:, k * 512: k * 512 + ntok], [wg, hT],
                         start=(k == 0), stop=(k == 7))
                for k in range(2):
                    C.mm(pb, pb[:, 0:ntok], wb[:, k * 512 + nn * 128: k * 512 + (nn + 1) * 128], oT[:, (2 * g + k) * 512: (2 * g + k) * 512 + ntok], [wb, oT],
                         start=(k == 0), stop=(k == 1))
                C.i("act", "activation", sig[:, 0:ntok], pg[:, 0:ntok], AF.Sigmoid, bias=bgT[:, g * 8 + n: g * 8 + n + 1], R=[pg, bgT], Wr=[sig])
                if g == 0:
                    C.i("dve", "tensor_tensor", mT[:, n * 512: n * 512 + ntok], sig[:, 0:ntok], pb[:, 0:ntok], ALU.mult, R=[sig, pb], Wr=[mT])
                else:
                    C.i("dve", "tensor_tensor", tmp[:, 0:ntok], sig[:, 0:ntok], pb[:, 0:ntok], ALU.mult, R=[sig, pb], Wr=[tmp])
                    C.i("dve", "tensor_tensor", mT[:, n * 512: n * 512 + ntok], mT[:, n * 512: n * 512 + ntok], tmp[:, 0:ntok], ALU.add, R=[mT, tmp], Wr=[mT])
        gB = g1B[1] if is_ctx else g1B[0]
        for ti in range(ntile):
            r0 = c0 + ti * 128
            C.dma("sp", xt[:], X[s, r0:r0 + 128, :], [X], [xt])
            for half in range(2):
                py = PS[6 + half]
                hs = slice(half * 512, (half + 1) * 512)
                for k in range(8):
                    C.mm(py, py[:, :], mT[:, k * 512 + ti * 128: k * 512 + (ti + 1) * 128], wo[:, k * 1024 + half * 512: k * 1024 + (half + 1) * 512],
                         [mT, wo], start=(k == 0), stop=(k == 7))
                C.i("dve", "tensor_tensor", rr[:, hs], py[:, :], gB[:, hs], ALU.mult, R=[py, gB], Wr=[rr])
                C.i("dve", "scalar_tensor_tensor", rr[:, hs], xt[:, hs], alpha, rr[:, hs], ALU.mult, ALU.add, R=[xt, rr], Wr=[rr])
            layernorm(C, rr, xn, st)
            C.i("dve", "tensor_tensor", xn[:], xn[:], lnG[:], ALU.mult, R=[xn, lnG], Wr=[xn])
            C.i("dve", "tensor_tensor", xn[:], xn[:], lnB[:], ALU.add, R=[xn, lnB], Wr=[xn])
            C.dma("sp", X[s, r0:r0 + 128, :], xn[:], [xn], [X])
    A.release(m0)


def make_moe_consts():
    c = {}
    p = np.arange(128, dtype=np.float32)
    c["tcol"] = (p[:, None] + 128.0 * np.arange(16, dtype=np.float32)[None, :]).astype(np.float32)
    c["iota512"] = np.tile(np.arange(512, dtype=np.float32)[None, :], (128, 1))
    return c


def moe_phase(C, A, PS, K, cfg, l, s, X, H2, YE, IDX, MOD, W_):
    NS, TC, TT, NT, DEPTH = cfg.NS, cfg.TC, cfg.T, cfg.NT, cfg.DEPTH
    alpha = (2.0 * DEPTH) ** 0.25
    with_ctx = l < DEPTH - 1
    streams = ([(0, TC, NS)] if with_ctx else []) + [(TC, TT, s)]
    m0 = A.mark()
    lnG, lnB = A.alloc("mo_lnG", 1024), A.alloc("mo_lnB", 1024)
    C.dma("sp", lnG[:], W_["ln2_g"][l:l + 1, :].partition_broadcast(128), [W_["ln2_g"]], [lnG])
    C.dma("sp", lnB[:], W_["ln2_b"][l:l + 1, :].partition_broadcast(128), [W_["ln2_b"]], [lnB])
    for (a, b, ms) in streams:
        Tn = b - a
        NTs = Tn // 128
        cap = 2 * Tn // E
        ncc = (cap + 127) // 128
        rows = min(128, cap)
        m1 = A.mark()
        modB = A.alloc("mo_modB", 3 * 1024)
        C.dma("sp", modB[:], MOD[l, ms:ms + 1, 3072:6144].partition_broadcast(128), [MOD], [modB])
        C.i("dve", "tensor_scalar", modB[:, 1024:2048], modB[:, 1024:2048], 1.0, None, ALU.add, R=[modB], Wr=[modB])
        AFF = A.alloc("mo_aff", Tn)
        wv, ixf = A.alloc("mo_wv", cap), A.alloc("mo_ixf", cap)
        idxT, wT = A.alloc("mo_idxT", ncc * 16), A.alloc("mo_wT", ncc * 16)
        m2 = A.mark()
        wr = A.alloc("mo_wr", 8 * 16)
        C.dma("sp", wr[:].rearrange("p (k e) -> p k e", k=8), W_["w_router"][l, :, :].rearrange("(k p) e -> p k e", p=128), [W_["w_router"]], [wr])
        xts = [A.alloc("mo_xt", 1024) for _ in range(2)]
        xns = [A.alloc("mo_xn", 1024) for _ in range(2)]
        h2T = A.alloc("mo_h2T", 1024)
        sts = [A.alloc("mo_st", 8) for _ in range(2)]
        lg, sm = A.alloc("mo_lg", 16), A.alloc("mo_sm", 8)
        for tt in range(NTs):
            r0 = a + tt * 128
            xt, xn, st = xts[tt % 2], xns[tt % 2], sts[tt % 2]
            C.dma("sp", xt[:], X[s, r0:r0 + 128, :], [X], [xt])
            layernorm(C, xt, xn, st)
            C.i("dve", "tensor_tensor", xn[:], xn[:], modB[:, 1024:2048], ALU.mult, R=[xn, modB], Wr=[xn])
            C.i("dve", "tensor_tensor", xn[:], xn[:], modB[:, 0:1024], ALU.add, R=[xn, modB], Wr=[xn])
            C.dma("sp", H2[s, r0:r0 + 128, :], xn[:], [xn], [H2])
            for half in range(2):
                ps = PS[half]
                for kk in range(4):
                    C.tr(ps, ps[:, kk * 128:(kk + 1) * 128], xn[:, (half * 4 + kk) * 128:(half * 4 + kk + 1) * 128], K.ident, [xn])
                C.i("act" if half else "dve", "copy" if half else "tensor_copy", h2T[:, half * 512:(half + 1) * 512], ps[:, :], R=[ps], Wr=[h2T])
            pl = PS[2]
            for k in range(8):
                C.mm(pl, pl[:, 0:16], h2T[:, k * 128:(k + 1) * 128], wr[:, k * 16:(k + 1) * 16], [h2T, wr], start=(k == 0), stop=(k == 7))
            C.i("dve", "reduce_max", sm[:, 0:1], pl[:, 0:16], AX.X, R=[pl], Wr=[sm])
            C.i("dve", "tensor_scalar", sm[:, 1:2], sm[:, 0:1], -1.0, None, ALU.mult, R=[sm], Wr=[sm])
            C.i("act", "activation", lg[:], pl[:, 0:16], AF.Exp, bias=sm[:, 1:2], accum_out=sm[:, 2:3], R=[pl, sm], Wr=[lg, sm])
            C.i("dve", "reciprocal", sm[:, 3:4], sm[:, 2:3], R=[sm], Wr=[sm])
            C.i("dve", "tensor_scalar", lg[:], lg[:], sm[:, 3:4], None, ALU.mult, R=[lg, sm], Wr=[lg])
            C.S.op("pe", lambda e, lg=lg, tt=tt: e.transpose(PS[3][0:16, (tt % 4) * 128:(tt % 4 + 1) * 128], lg[:, 0:16], K("ident")),
                   [lg.res, K.t.res], [PS[3].res])
            C.i("act", "copy", AFF[0:16, tt * 128:(tt + 1) * 128], PS[3][0:16, (tt % 4) * 128:(tt % 4 + 1) * 128], R=[PS[3]], Wr=[AFF])
        ix = A.alloc("mo_ix", cap)
        ixu = T(ix.ap.bitcast(U32), "ixu")
        ixu.res = ix.res
        for r in range(cap // 8):
            C.i("dve", "max", out=wv[0:16, r * 8:(r + 1) * 8], in_=AFF[0:16, :], R=[AFF], Wr=[wv])
            C.i("dve", "max_index", out=ixu[0:16, r * 8:(r + 1) * 8], in_max=wv[0:16, r * 8:(r + 1) * 8], in_values=AFF[0:16, :], R=[AFF, wv], Wr=[ix])
            C.i("dve", "match_replace", out=AFF[0:16, :], in_to_replace=wv[0:16, r * 8:(r + 1) * 8], in_values=AFF[0:16, :], imm_value=-1.0,
                R=[AFF, wv], Wr=[AFF])
        C.i("dve", "tensor_copy", ixf[0:16, :], ixu[0:16, :], R=[ix], Wr=[ixf])
        C.dma("sp", IDX[0:16, 0:cap], ixf[0:16, :], [ixf], [IDX])
        for cc in range(ncc):
            for src, dst in ((ixf, idxT), (wv, wT)):
                C.S.op("pe", lambda e, src=src, cc=cc, rows=rows: e.transpose(PS[3][0:rows, 0:16], src[0:16, cc * 128: cc * 128 + rows], K("ident")[0:16, 0:16]),
                       [src.res, K.t.res], [PS[3].res])
                C.i("dve", "tensor_copy", dst[0:rows, cc * 16:(cc + 1) * 16], PS[3][0:rows, 0:16], R=[PS[3]], Wr=[dst])
        A.release(m2)
        m3 = A.mark()
        Wg, Wu, Wd = A.alloc("mo_Wg", 8192), A.alloc("mo_Wu", 8192), A.alloc("mo_Wd", 8192)
        XeT, hidT, ye = A.alloc("mo_XeT", 8 * cap), A.alloc("mo_hid", 8 * cap), A.alloc("mo_ye", ncc * 1024)
        h2s = [A.alloc("mo_h2", 1024) for _ in range(2)]
        Pm = [A.alloc("mo_P", cap) for _ in range(2)]
        idr, tmp = A.alloc("mo_idr", cap), A.alloc("mo_tmp", cap)
        xe = A.alloc("mo_xe", ncc * 1024)
        def load_w(wt, nm, ee):
            C.dma("sp", wt[:].rearrange("p (k n) -> p k n", k=8), W_[nm][l, ee, :, :].rearrange("(k p) n -> p k n", p=128), [W_[nm]], [wt])

        idrs = [idr, A.alloc("mo_idr2", cap)]
        C.dma("sp", idrs[0][:], IDX[0:1, 0:cap].partition_broadcast(128), [IDX], [idrs[0]])
        for wt, nm in ((Wg, "w_ff_gate"), (Wu, "w_ff_up"), (Wd, "w_ff_down")):
            load_w(wt, nm, 0)
        for e_ in range(E):
            idr = idrs[e_ % 2]
            if e_ + 1 < E:
                C.dma("sp", idrs[(e_ + 1) % 2][:], IDX[e_ + 1:e_ + 2, 0:cap].partition_broadcast(128), [IDX], [idrs[(e_ + 1) % 2]])
            for tt in range(NTs):
                h2t, P = h2s[tt % 2], Pm[tt % 2]
                C.dma("sp", h2t[:], H2[s, a + tt * 128: a + (tt + 1) * 128, :], [H2], [h2t])
                C.i("dve", "tensor_scalar", P[:], idr[:], K("tcol")[:, tt:tt + 1], None, ALU.is_equal, R=[idr, K.t], Wr=[P])
                for cc in range(ncc):
                    for half in range(2):
                        pb = PS[cc * 2 + half]
                        C.mm(pb, pb[0:rows, :], P[:, cc * 128: cc * 128 + rows], h2t[:, half * 512:(half + 1) * 512], [h2t, P],
                             start=(tt == 0), stop=(tt == NTs - 1))
            for cc in range(ncc):
                for half in range(2):
                    pb = PS[cc * 2 + half]
                    C.i("act" if half else "dve", "copy" if half else "tensor_copy", xe[0:rows, cc * 1024 + half * 512: cc * 1024 + (half + 1) * 512],
                        pb[0:rows, :], R=[pb], Wr=[xe])
            for cc in range(ncc):
                for half in range(2):
                    pb = PS[4 + half]
                    for kk in range(4):
                        k = half * 4 + kk
                        C.S.op("pe", lambda e, pb=pb, kk=kk, k=k, cc=cc, rows=rows, xe=xe: e.transpose(pb[:, kk * 128: kk * 128 + rows], xe[0:rows, cc * 1024 + k * 128: cc * 1024 + (k + 1) * 128],
                                                                                       K("ident")[0:rows, 0:rows]), [xe.res, K.t.res], [pb.res])
                    for kk in range(4):
                        k = half * 4 + kk
                        C.i("act" if kk % 2 else "dve", "copy" if kk % 2 else "tensor_copy", XeT[:, k * cap + cc * 128: k * cap + cc * 128 + rows],
                            pb[:, kk * 128: kk * 128 + rows], R=[pb], Wr=[XeT])
            for f in range(8):
                pg, pu = PS[6], PS[7]
                for k in range(8):
                    C.mm(pg, pg[:, 0:cap], Wg[:, k * 1024 + f * 128: k * 1024 + (f + 1) * 128], XeT[:, k * cap:(k + 1) * cap], [Wg, XeT], start=(k == 0), stop=(k == 7))
                for k in range(8):
                    C.mm(pu, pu[:, 0:cap], Wu[:, k * 1024 + f * 128: k * 1024 + (f + 1) * 128], XeT[:, k * cap:(k + 1) * cap], [Wu, XeT], start=(k == 0), stop=(k == 7))
                C.i("act", "activation", tmp[:], pg[:, 0:cap], AF.Silu, R=[pg], Wr=[tmp])
                C.i("dve", "tensor_tensor", hidT[:, f * cap:(f + 1) * cap], tmp[:], pu[:, 0:cap], ALU.mult, R=[tmp, pu], Wr=[hidT])
            if e_ + 1 < E:
                load_w(Wg, "w_ff_gate", e_ + 1)
                load_w(Wu, "w_ff_up", e_ + 1)
            for cc in range(ncc):
                for half in range(2):
                    pd = PS[half]
                    for f in range(8):
                        C.mm(pd, pd[0:rows, :], hidT[:, f * cap + cc * 128: f * cap + cc * 128 + rows], Wd[:, f * 1024 + half * 512: f * 1024 + (half + 1) * 512],
                             [hidT, Wd], start=(f == 0), stop=(f == 7))
                    C.i("act" if half else "dve", "copy" if half else "tensor_copy", ye[0:rows, cc * 1024 + half * 512: cc * 1024 + (half + 1) * 512],
                        pd[0:rows, :], R=[pd], Wr=[ye])
                C.dma("sp", YE[e_, cc * 128: cc * 128 + rows, :], ye[0:rows, cc * 1024:(cc + 1) * 1024], [ye], [YE])
            if e_ + 1 < E:
                load_w(Wd, "w_ff_down", e_ + 1)
        A.release(m3)
        yes_ = [A.alloc("mo_yeB", ncc * 1024) for _ in range(2)]
        PwT = [A.alloc("mo_PwT", ncc * 512) for _ in range(2)]
        idxs = A.alloc("mo_idxs", ncc * 16)
        xt, rr, xn, st = A.alloc("mo_xtB", 1024), A.alloc("mo_rB", 1024), A.alloc("mo_xnB", 1024), A.alloc("mo_stB", 8)
        for g0 in range(0, NTs, 4):
            ntile = min(4, NTs - g0)
            ntok = ntile * 128
            C.i("dve", "tensor_scalar", idxs[0:rows, :], idxT[0:rows, :], float(-g0 * 128), None, ALU.add, R=[idxT], Wr=[idxs])
            for e_ in range(E):
                yb, pw = yes_[e_ % 2], PwT[e_ % 2]
                C.dma("sp", yb[0:rows, :].rearrange("p (c n) -> p c n", c=ncc), YE[e_, 0:ncc * rows, :].rearrange("(c p) n -> p c n", p=rows), [YE], [yb])
                for cc in range(ncc):
                    C.i("dve", "tensor_scalar", pw[0:rows, cc * 512: cc * 512 + ntok], K("iota512")[0:rows, 0:ntok], idxs[0:rows, cc * 16 + e_: cc * 16 + e_ + 1],
                        wT[0:rows, cc * 16 + e_: cc * 16 + e_ + 1], ALU.is_equal, ALU.mult, R=[K.t, idxs, wT], Wr=[pw])
                for ti in range(ntile):
                    for half in range(2):
                        py = PS[ti * 2 + half]
                        for cc in range(ncc):
                            C.mm(py, py[:, :], pw[0:rows, cc * 512 + ti * 128: cc * 512 + (ti + 1) * 128], yb[0:rows, cc * 1024 + half * 512: cc * 1024 + (half + 1) * 512],
                                 [pw, yb], start=(e_ == 0 and cc == 0), stop=(e_ == E - 1 and cc == ncc - 1))
            for ti in range(ntile):
                r0 = a + (g0 + ti) * 128
                C.dma("sp", xt[:], X[s, r0:r0 + 128, :], [X], [xt])
                for half in range(2):
                    py = PS[ti * 2 + half]
                    hs = slice(half * 512, (half + 1) * 512)
                    C.i("dve", "tensor_tensor", rr[:, hs], py[:, :], modB[:, 2048 + half * 512: 2048 + (half + 1) * 512], ALU.mult, R=[py, modB], Wr=[rr])
                    C.i("dve", "scalar_tensor_tensor", rr[:, hs], xt[:, hs], alpha, rr[:, hs], ALU.mult, ALU.add, R=[xt, rr], Wr=[rr])
                layernorm(C, rr, xn, st)
                C.i("dve", "tensor_tensor", xn[:], xn[:], lnG[:], ALU.mult, R=[xn, lnG], Wr=[xn])
                C.i("dve", "tensor_tensor", xn[:], xn[:], lnB[:], ALU.add, R=[xn, lnB], Wr=[xn])
                C.dma("sp", X[s, r0:r0 + 128, :], xn[:], [xn], [X])
        A.release(m1)
    A.release(m0)


CONSTS_NP, CONST_OFFS = make_consts()


def build(cfg, debug_out=(), dbg_cols=(768,)):
    NS, TC, TL, DEPTH, TT, NT, NST = cfg.NS, cfg.TC, cfg.TL, cfg.DEPTH, cfg.T, cfg.NT, cfg.NST
    nc = bass.Bass("TRN2", target_bir_lowering=False)
    es = ExitStack()
    C = Ctx(nc, es)
    S = C.S
    A = Arena(C, 52000)
    PS = [C.ps("ps%d" % i) for i in range(8)]

    def ein(name, shape):
        return C.dram(name, shape, kind="ExternalInput")

    x_in = ein("x", [NS, TL, D])
    ctx_in = ein("ctx", [NS, TC, D])
    cT_in = ein("cT", [128, 8, NST])
    pos_in = ein("pos", [TL, D])
    consts_in = ein("consts", list(CONSTS_NP.shape))
    W_ = {}
    for nm, shp in (("ssd_dt_bias", [DEPTH, 8]), ("ssd_a_log", [DEPTH, 8]), ("ssd_d", [DEPTH, 4]), ("ssd_norm_w", [DEPTH, 256]),
                    ("ssd_cwT", [DEPTH, 512, 5]), ("ssd_cbT", [DEPTH, 512, 1]),
                    ("hgrn_lb_logits", [2, DEPTH, 256]), ("hgrn_norm_w", [DEPTH, 256]),
                    ("gdn_dt_bias", [DEPTH, 8]), ("gdn_a_log", [DEPTH, 8]), ("gdn_norm_w", [DEPTH, 256]), ("gdn_cwT", [DEPTH, 768, 5]),
                    ("w_gate", [DEPTH, 4, D, D]), ("b_gateT", [DEPTH, 128, 32]), ("w_branch", [DEPTH, 4, W, D]), ("w_out", [DEPTH, D, D]),
                    ("ln1_g", [DEPTH, D]), ("ln1_b", [DEPTH, D]),
                    ("w_router", [DEPTH, D, E]), ("w_ff_gate", [DEPTH, E, D, FF]), ("w_ff_up", [DEPTH, E, D, FF]), ("w_ff_down", [DEPTH, E, FF, D]),
                    ("ln2_g", [DEPTH, D]), ("ln2_b", [DEPTH, D]),
                    ("bd64", [2, 2, 128, 256]), ("dft_ctx", [2, TC // 128, 128, TC // 128, 128]),
                    ("dft_lat", [2, TL // 128, 128, TL // 128, 128])):
        W_[nm] = ein(nm, shp)
    ada_w = ein("ada_w", [DEPTH, D, 6 * D])
    ada_b = ein("ada_b", [DEPTH, 6 * D])
    w_in = ein("w_in", [DEPTH, D, IN_COLS])
    out = C.dram("out", [NS, TL, D], kind="ExternalOutput")
    dbg = {k: C.dram(k, shp, kind="ExternalOutput") for k, shp in debug_out}

    X = C.dram("X", [NS, TT, D])
    MOD = C.dram("MOD", [DEPTH, NST, 6 * D])
    ZT_COLS = 1560
    ZT = C.dram("ZT", [NS, TT, ZT_COLS])
    ZF = C.dram("ZF", [NS, 18 * 128, TT])
    HT = C.dram("HT", [NS, 8, 128, TT])
    H2 = C.dram("H2", [NS, TT, D])
    YE = C.dram("YE", [E, 2 * TL // E, D])
    IDX = C.dram("IDX", [E, 2 * TL // E])

    O = C.dram("O", [NS, TT, D])
    KT = A.alloc("consts", CONSTS_NP.shape[1])
    C.dma("sp", KT[:], consts_in[:, :], [consts_in], [KT])

    def K(name):
        o, n = CONST_OFFS[name]
        return KT[:, o:o + n]
    K.t = KT
    ident = T(K("ident"), "ident")
    ident.res = KT.res
    K.ident = ident

    m0 = A.mark()
    xts = [A.alloc("xt", D) for _ in range(2)]
    pts = [A.alloc("pt", D) for _ in range(2)]
    for s in range(NS):
        C.dma("sp", X[s, 0:TC, :], ctx_in[s, :, :], [ctx_in], [X])
        for t in range(TL // 128):
            xt = xts[t % 2]
            pt = pts[t % 2]
            C.dma("sp", xt[:], x_in[s, t * 128:(t + 1) * 128, :], [x_in], [xt])
            C.dma("sp", pt[:], pos_in[t * 128:(t + 1) * 128, :], [pos_in], [pt])
            C.v("dve", lambda e, xt=xt, pt=pt: e.tensor_tensor(xt[:], xt[:], pt[:], ALU.add), [xt, pt], [xt])
            C.dma("sp", X[s, TC + t * 128:TC + (t + 1) * 128, :], xt[:], [xt], [X])
    A.release(m0)

    scT = A.alloc("scT", 8 * NST)
    C.dma("sp", scT[:], cT_in[:, :, :].rearrange("p k s -> p (k s)"), [cT_in], [scT])
    C.v("act", lambda e: e.activation(scT[:], scT[:], AF.Silu), [scT], [scT])
    m1 = A.mark()
    for l in range(DEPTH):
        abB = A.alloc("abB", 6 * D, parts=NST)
        modrow = A.alloc("modrow", 6 * D, parts=NST)
        C.dma("sp", abB[:], ada_b[l:l + 1, :].partition_broadcast(NST), [ada_b], [abB])
        aws = [A.alloc("aw", 8 * 512) for _ in range(2)]
        for ng in range(12):
            aw = aws[ng % 2]
            C.dma("sp", aw[:].rearrange("p (k n) -> p k n", k=8),
                  ada_w[l, :, ng * 512:(ng + 1) * 512].rearrange("(k p) n -> p k n", p=128), [ada_w], [aw])
            ps = PS[ng % 2]
            for k in range(8):
                C.mm(ps, ps[0:NST, :], scT[:, k * NST:(k + 1) * NST], aw[:, k * 512:(k + 1) * 512], [scT, aw],
                     start=(k == 0), stop=(k == 7))
            C.v("dve", lambda e, ps=ps, ng=ng, modrow=modrow, abB=abB: e.tensor_tensor(
                modrow[:, ng * 512:(ng + 1) * 512], ps[0:NST, :], abB[:, ng * 512:(ng + 1) * 512], ALU.add),
                [ps, abB], [modrow])
            if ng % 4 == 3:
                pass
        C.dma("sp", MOD[l, :, :], modrow[:], [modrow], [MOD])
        A.release(m1)

    for l in range(DEPTH):
        mL = A.mark()
        modF = A.alloc("modF", 4 * 8 * NST)
        mm_ = A.mark()
        modrow = A.alloc("modrow2", 6 * D, parts=NST)
        C.dma("sp", modrow[:], MOD[l, :, :], [MOD], [modrow])
        secs = (0, 1, 3, 4)
        ps = PS[0]
        for si, sec in enumerate(secs):
            for k in range(8):
                col = (si * 8 + k) * NST
                S.op("pe", lambda e, col=col, sec=sec, k=k, ps=ps, modrow=modrow: e.transpose(
                    ps[:, col:col + NST], modrow[0:NST, sec * D + k * 128: sec * D + (k + 1) * 128], ident[0:NST, 0:NST]),
                    [modrow.res, ident.res], [ps.res])
        C.v("dve", lambda e, ps=ps, modF=modF: e.tensor_copy(modF[:], ps[:, 0:32 * NST]), [ps], [modF])
        for si in (1, 3):
            C.v("dve", lambda e, si=si, modF=modF: e.tensor_scalar(
                modF[:, si * 8 * NST:(si + 1) * 8 * NST], modF[:, si * 8 * NST:(si + 1) * 8 * NST], 1.0, None, ALU.add),
                [modF], [modF])
        A.release(mm_)

        def mf(si, k, ms):
            c0 = (si * 8 + k) * NST + ms
            return modF[:, c0:c0 + 1]

        mAB = A.mark()
        wI = A.alloc("wI", 8 * IN_COLS)
        for k in range(8):
            C.dma("sp", wI[:, k * IN_COLS:(k + 1) * IN_COLS], w_in[l, k * 128:(k + 1) * 128, :], [w_in], [wI])
        o_aq, o_ff, o_fb, o_av, o_ag = 0, 256, 512, 768, 1024
        o_qkv, o_bg, o_ba, o_bb = 1280, 2048, 2304, 2312
        o_cu, o_xbc, o_dz, o_dt = 2320, 2576, 3088, 3344
        tm_groups = [(o_ff, 512, 0), (o_av, 512, 512), (o_bg, 272, 1024), (o_dz, 264, 1296)]
        fm_cols = [o_aq, o_aq + 128, o_ff, o_ff + 128, o_fb, o_fb + 128] + [o_qkv + i * 128 for i in range(6)] + \
                  [o_cu, o_cu + 128] + [o_xbc + i * 128 for i in range(4)]
        sts = [A.alloc("st", 8) for _ in range(2)]
        hTs = [A.alloc("hT", 8 * 512) for _ in range(1)]
        xts = [A.alloc("xt", D) for _ in range(2)]
        xns = [A.alloc("xn", D) for _ in range(2)]
        zts = [A.alloc("zt", ZT_COLS) for _ in range(2)]
        zfs = [A.alloc("zf", 512) for _ in range(2)]
        gcnt = 0
        for s in range(NS):
            for tg in range(0, NT, 4):
                ntile = min(4, NT - tg)
                ntok = ntile * 128
                hT = hTs[0]
                gcnt += 1
                for ti in range(ntile):
                    t = tg + ti
                    ms = NS if t * 128 < TC else s
                    xt = xts[t % 2]
                    xn = xns[t % 2]
                    st = sts[t % 2]
                    C.dma("sp", xt[:], X[s, t * 128:(t + 1) * 128, :], [X], [xt])
                    layernorm(C, xt, xn, st)
                    for half in range(2):
                        ps = PS[2 + half]
                        for kk in range(4):
                            k = half * 4 + kk
                            C.tr(ps, ps[:, kk * 128:(kk + 1) * 128], xn[:, k * 128:(k + 1) * 128], ident, [xn])
                        for kk in range(4):
                            k = half * 4 + kk
                            C.v("act", lambda e, ps=ps, kk=kk, k=k, ti=ti, ms=ms, hT=hT: e.activation(
                                hT[:, k * 512 + ti * 128: k * 512 + (ti + 1) * 128], ps[:, kk * 128:(kk + 1) * 128],
                                AF.Identity, bias=mf(0, k, ms), scale=mf(1, k, ms)), [ps, modF], [hT])
                C.dma("sp", HT[s, :, :, tg * 128: tg * 128 + ntok].rearrange("k p n -> p k n"),
                      hT[:].rearrange("p (k n) -> p k n", k=8)[:, :, 0:ntok], [hT], [HT])
                for ti in range(ntile):
                    t = tg + ti
                    zt = zts[t % 2]
                    for gi, (c0, wd, d0) in enumerate(tm_groups):
                        ps = PS[4 + gi % 2]
                        for k in range(8):
                            C.mm(ps, ps[:, 0:wd], hT[:, k * 512 + ti * 128: k * 512 + (ti + 1) * 128],
                                 wI[:, k * IN_COLS + c0: k * IN_COLS + c0 + wd], [hT, wI], start=(k == 0), stop=(k == 7))
                        C.v("dve" if gi % 2 == 0 else "act",
                            (lambda e, ps=ps, zt=zt, wd=wd, d0=d0: e.tensor_copy(zt[:, d0:d0 + wd], ps[:, 0:wd])) if gi % 2 == 0 else
                            (lambda e, ps=ps, zt=zt, wd=wd, d0=d0: e.copy(zt[:, d0:d0 + wd], ps[:, 0:wd])), [ps], [zt])
                    C.dma("sp", ZT[s, t * 128:(t + 1) * 128, :], zt[:], [zt], [ZT])
                for ci, c0 in enumerate(fm_cols):
                    ps = PS[6 + ci % 2]
                    zf = zfs[ci % 2]
                    for k in range(8):
                        C.mm(ps, ps[:, 0:ntok], wI[:, k * IN_COLS + c0: k * IN_COLS + c0 + 128], hT[:, k * 512: k * 512 + ntok],
                             [hT, wI], start=(k == 0), stop=(k == 7))
                    C.v("dve" if ci % 2 == 0 else "act",
                        (lambda e, ps=ps, zf=zf, ntok=ntok: e.tensor_copy(zf[:, 0:ntok], ps[:, 0:ntok])) if ci % 2 == 0 else
                        (lambda e, ps=ps, zf=zf, ntok=ntok: e.copy(zf[:, 0:ntok], ps[:, 0:ntok])), [ps], [zf])
                    C.dma("sp", ZF[s, ci * 128:(ci + 1) * 128, tg * 128: tg * 128 + ntok], zf[:, 0:ntok], [zf], [ZF])
        A.release(mAB)
        for s in range(NS):
            if "ssd" in cfg.parts:
                ssd_mixer(C, A, PS, K, cfg, l, s, ZT, ZF, O, W_)
            if "hgrn" in cfg.parts:
                hgrn_mixer(C, A, PS, K, cfg, l, s, ZT, ZF, O, W_)
            if "gdn" in cfg.parts:
                gdn_mixer(C, A, PS, K, cfg, l, s, ZT, ZF, O, W_)
            if "fnet" in cfg.parts:
                fnet_mixer(C, A, PS, K, cfg, l, s, ZF, O, W_)
        if "merge" in cfg.parts:
            for s in range(NS):
                merge_phase(C, A, PS, K, cfg, l, s, X, HT, O, MOD, W_)
        if "moe" in cfg.parts:
            for s in range(NS):
                moe_phase(C, A, PS, K, cfg, l, s, X, H2, YE, IDX, MOD, W_)
        A.release(mL)

    for k, t in dbg.items():
        src = {"dbg_X": X, "dbg_MOD": MOD, "dbg_ZT": ZT, "dbg_ZF": ZF, "dbg_O": O}[k]
        sap = src.ap
        if k == "dbg_O":
            c0 = dbg_cols[0]
            sap = src.ap[:, :, c0:c0 + t.ap.shape[2]]
        C.dma("sp", t.ap, sap, [src], [t])
    for s in range(NS):
        C.dma("sp", out[s, :, :], X[s, TC:, :], [X], [out])
    S.emit(nc, es)
    return nc, es


def _host_weights(inp, depth):
    f = lambda a: np.ascontiguousarray(np.asarray(a, dtype=np.float32))
    return {
        "ada_w": f(inp["ada_w"][:depth]), "ada_b": f(inp["ada_b"][:depth]), "w_in": f(inp["w_in"][:depth]),
        "consts": CONSTS_NP,
        "ssd_dt_bias": f(np.asarray(inp["ssd_dt_bias"])[:depth].reshape(depth, 8)),
        "ssd_a_log": f(np.asarray(inp["ssd_a_log"])[:depth].reshape(depth, 8)),
        "ssd_d": f(inp["ssd_d"][:depth]), "ssd_norm_w": f(inp["ssd_norm_w"][:depth]),
        "ssd_cwT": f(np.asarray(inp["ssd_conv_w"])[:depth].transpose(0, 2, 1)),
        "ssd_cbT": f(np.asarray(inp["ssd_conv_b"])[:depth][:, :, None]),
        "bd64": make_bd64(),
        "w_router": f(inp["w_router"][:depth]), "w_ff_gate": f(inp["w_ff_gate"][:depth]), "w_ff_up": f(inp["w_ff_up"][:depth]),
        "w_ff_down": f(inp["w_ff_down"][:depth]), "ln2_g": f(inp["ln2_g"][:depth]), "ln2_b": f(inp["ln2_b"][:depth]),
        "w_gate": f(inp["w_gate"][:depth]), "w_branch": f(inp["w_branch"][:depth]), "w_out": f(inp["w_out"][:depth]),
        "b_gateT": f(np.asarray(inp["b_gate"])[:depth].reshape(depth, 4, 8, 128).transpose(0, 3, 1, 2).reshape(depth, 128, 32)),
        "ln1_g": f(inp["ln1_g"][:depth]), "ln1_b": f(inp["ln1_b"][:depth]),
        "gdn_dt_bias": f(np.asarray(inp["gdn_dt_bias"])[:depth].reshape(depth, 8)),
        "gdn_a_log": f(np.asarray(inp["gdn_a_log"])[:depth].reshape(depth, 8)),
        "gdn_norm_w": f(inp["gdn_norm_w"][:depth]),
        "gdn_cwT": f(np.asarray(inp["gdn_conv_w"])[:depth].transpose(0, 2, 1)),
        "hgrn_lb_logits": f(np.asarray(inp["hgrn_lb_logits"])[:, :depth]), "hgrn_norm_w": f(inp["hgrn_norm_w"][:depth]),
    }


def _prepare(inp, n_cores, cfg=None):
    x = np.asarray(inp["x"], dtype=np.float32)
    ctx = np.asarray(inp["ctx"], dtype=np.float32)
    c = np.asarray(inp["c"], dtype=np.float32)
    c_ctx = np.asarray(inp["c_ctx"], dtype=np.float32)
    B, TL, _ = x.shape
    TC = ctx.shape[1]
    NS = B // n_cores
    depth = int(np.asarray(inp["ada_w"]).shape[0])
    if cfg is None:
        cfg = Cfg(NS, TC, TL, depth)
    wts = _host_weights(inp, depth)
    wts["dft_ctx"] = make_dft(TC)
    wts["dft_lat"] = make_dft(TL)
    pos = sincos_grid_np(TL // 64, 64, D)
    in_maps = []
    for i in range(n_cores):
        sl = slice(i * NS, (i + 1) * NS)
        call = np.concatenate([c[sl], c_ctx[None]], 0)
        cT = np.ascontiguousarray(call.reshape(NS + 1, 8, 128).transpose(2, 1, 0))
        m = {"x": np.ascontiguousarray(x[sl]), "ctx": np.ascontiguousarray(ctx[sl]), "cT": cT, "pos": pos}
        m.update(wts)
        in_maps.append(m)
    return cfg, in_maps


def kernel_sim(inp, cfg, simrun):
    cfg, in_maps = _prepare(inp, 1, cfg)
    nc, es = build(cfg)
    res = simrun(nc, in_maps)
    return np.concatenate([np.asarray(r["out"], dtype=np.float32) for r in res], axis=0)


def kernel(**inp):
    n_cores = 8
    cfg, in_maps = _prepare(inp, n_cores)
    nc, es = build(cfg)
    res = run_bass_kernel_spmd(nc, in_maps, core_ids=list(range(n_cores)))
    es.close()
    return np.concatenate([np.asarray(r["out"], dtype=np.float32) for r in res.results], axis=0)
```

```python
import math
from contextlib import ExitStack
import numpy as np
import concourse.bass as bass
import concourse.mybir as mybir
from concourse.bass_utils import run_bass_kernel_spmd

F32 = mybir.dt.float32
I32 = mybir.dt.int32
U32 = mybir.dt.uint32
AF = mybir.ActivationFunctionType
ALU = mybir.AluOpType
AX = mybir.AxisListType

D = 1024
W = 256
H = 4
HD = 64
E = 16
FF = 1024
IN_COLS = 3352
EPS = 1e-6
NDMA_SEMS = 40


class Res:
    __slots__ = ("name", "w", "r", "psum")

    def __init__(self, name):
        self.name = name
        self.w = None
        self.r = {}
        self.psum = False


class Sched:
    ENGS = ("pe", "dve", "act", "pool", "sp")

    def __init__(self):
        self.q = {e: [] for e in self.ENGS}
        self.cnt = {e: 0 for e in self.ENGS}
        self.seen = {e: {} for e in self.ENGS}
        self.dcnt = [0] * NDMA_SEMS
        self.dnext = 0
        self.n = 0

    def _waits(self, eng, reads, writes, extra=()):
        waits = {}

        def need(kv):
            if kv is None:
                return
            k, v = kv
            if v > waits.get(k, 0):
                waits[k] = v
        for R in reads:
            need(R.w)
            if R.psum:
                for k, v in R.r.items():
                    if k != eng:
                        need((k, v))
        for Wr in writes:
            need(Wr.w)
            for k, v in Wr.r.items():
                need((k, v))
        for kv in extra:
            need(kv)
        out = []
        seen = self.seen[eng]
        for k, v in waits.items():
            if seen.get(k, 0) < v:
                seen[k] = v
                out.append((k, v))
        return out

    def op(self, eng, fn, reads=(), writes=()):
        waits = self._waits(eng, reads, writes)
        if eng == "pe":
            waits = [kv for kv in waits if kv[0] != "pe"]
        self.cnt[eng] += 1
        c = self.cnt[eng]
        self.q[eng].append((waits, fn, (eng, 1)))
        for R in reads:
            if R.r.get(eng, 0) < c:
                R.r[eng] = c
        for Wr in writes:
            Wr.w = (eng, c)
            Wr.r = {}
        self.n += 1

    def dma(self, eng, out_ap, in_ap, reads=(), writes=(), **kw):
        s = self.dnext
        self.dnext = (self.dnext + 1) % NDMA_SEMS
        key = ("d", s)
        prev = self.dcnt[s]
        waits = self._waits(eng, reads, writes, extra=((key, prev),) if prev else ())
        self.dcnt[s] += 16
        c = self.dcnt[s]
        self.q[eng].append((waits, lambda e: e.dma_start(out=out_ap, in_=in_ap, **kw), (key, 16)))
        for R in reads:
            R.r[key] = c
        for Wr in writes:
            Wr.w = (key, c)
            Wr.r = {}
        self.n += 1

    def barrier(self):
        for eng in self.ENGS:
            waits = []
            seen = self.seen[eng]
            for o in self.ENGS:
                if o != eng and self.cnt[o] > seen.get(o, 0):
                    seen[o] = self.cnt[o]
                    waits.append((o, self.cnt[o]))
            for s in range(NDMA_SEMS):
                k = ("d", s)
                if self.dcnt[s] > seen.get(k, 0):
                    seen[k] = self.dcnt[s]
                    waits.append((k, self.dcnt[s]))
            if waits:
                self.q[eng].append((waits, None, None))

    def emit(self, nc, es):
        sems = {e: es.enter_context(nc.semaphore("s_" + e)) for e in self.ENGS}
        for s in range(NDMA_SEMS):
            sems[("d", s)] = es.enter_context(nc.semaphore("sd%d" % s))
        self.barrier()
        block = es.enter_context(nc.Block())

        def run(e, lst):
            for waits, fn, inc in lst:
                for k, v in waits:
                    e.wait_ge(sems[k], v)
                if fn is not None:
                    fn(e).then_inc(sems[inc[0]], inc[1])

        @block.tensor
        def _(e):
            run(e, self.q["pe"])

        @block.vector
        def _(e):
            run(e, self.q["dve"])

        @block.scalar
        def _(e):
            run(e, self.q["act"])

        @block.gpsimd
        def _(e):
            run(e, self.q["pool"])

        @block.sync
        def _(e):
            run(e, self.q["sp"])


class T:
    def __init__(self, ap, name):
        self.ap = ap
        self.res = Res(name)

    def __getitem__(self, k):
        return self.ap[k]


class Ctx:
    def __init__(self, nc, es):
        self.nc = nc
        self.es = es
        self.S = Sched()
        self.uid = 0

    def sb(self, name, shape, dt=F32):
        self.uid += 1
        t = self.es.enter_context(self.nc.sbuf_tensor("%s_%d" % (name, self.uid), list(shape), dt))
        return T(t, name)

    def ps(self, name, shape=(128, 512), dt=F32):
        self.uid += 1
        t = self.es.enter_context(self.nc.psum_tensor("%s_%d" % (name, self.uid), list(shape), dt))
        r = T(t, name)
        r.res.psum = True
        return r

    def dram(self, name, shape, dt=F32, kind="Internal"):
        t = self.nc.dram_tensor(name, list(shape), dt, kind=kind)
        return T(t.ap(), name)

    def mm(self, out_t, out_ap, lhsT, rhs, reads, start=True, stop=True):
        rd = [x.res for x in reads]
        wr = [out_t.res]
        if not start:
            rd = rd + [out_t.res]
        self.S.op("pe", lambda e: e.matmul(out_ap, lhsT, rhs, start=start, stop=stop), rd, wr)

    def tr(self, out_t, out_ap, in_ap, ident, reads):
        self.S.op("pe", lambda e: e.transpose(out_ap, in_ap, ident.ap[:]), [x.res for x in reads] + [ident.res], [out_t.res])

    def v(self, eng, fn, reads, writes):
        self.S.op(eng, fn, [x.res for x in reads], [x.res for x in writes])

    def dma(self, eng, out_ap, in_ap, reads, writes, **kw):
        self.S.dma(eng, out_ap, in_ap, [x.res for x in reads], [x.res for x in writes], **kw)

    def i(self, eng, name, *args, R=(), Wr=(), **kw):
        self.S.op(eng, lambda e: getattr(e, name)(*args, **kw), [x.res for x in R], [x.res for x in Wr])


def sincos_grid_np(rows, cols, dim):
    quarter = dim // 4
    omega = (1.0 / (10000.0 ** (np.arange(quarter, dtype=np.float32) / np.float32(quarter)))).astype(np.float32)
    er = np.arange(rows, dtype=np.float32)[:, None] * omega
    ec = np.arange(cols, dtype=np.float32)[:, None] * omega
    er = np.concatenate([np.sin(er), np.cos(er)], axis=-1)
    ec = np.concatenate([np.sin(ec), np.cos(ec)], axis=-1)
    emb = np.concatenate([np.broadcast_to(er[:, None, :], (rows, cols, dim // 2)),
                          np.broadcast_to(ec[None, :, :], (rows, cols, dim // 2))], axis=-1)
    return np.ascontiguousarray(emb.reshape(rows * cols, dim).astype(np.float32))


class Cfg:
    def __init__(self, NS=2, TC=256, TL=2048, DEPTH=4):
        self.NS, self.TC, self.TL, self.DEPTH = NS, TC, TL, DEPTH
        self.T = TC + TL
        self.NT = self.T // 128
        self.NST = NS + 1
        self.parts = ("ssd", "fnet", "hgrn", "gdn", "merge", "moe")
        self.gdn_stage = 9


WNAMES = [("ada_w", (D, 6 * D)), ("ada_b", (6 * D,)), ("w_in", (D, IN_COLS))]


def layernorm(C, xt, out_t, st):
    nd = float(xt.ap.shape[-1])
    C.v("dve", lambda e: e.reduce_sum(st[:, 0:1], xt[:], AX.X), [xt], [st])
    C.v("act", lambda e: e.activation(out_t[:], xt[:], AF.Square, accum_out=st[:, 1:2]), [xt, st], [out_t, st])
    C.v("dve", lambda e: e.tensor_scalar(st[:, 2:3], st[:, 0:1], 1.0 / nd, None, ALU.mult), [st], [st])
    C.v("dve", lambda e: e.tensor_tensor(st[:, 3:4], st[:, 2:3], st[:, 2:3], ALU.mult), [st], [st])
    C.v("dve", lambda e: e.scalar_tensor_tensor(st[:, 4:5], st[:, 1:2], 1.0 / nd, st[:, 3:4], ALU.mult, ALU.subtract), [st], [st])
    C.v("act", lambda e: e.activation(st[:, 5:6], st[:, 4:5], AF.Ln, bias=EPS), [st], [st])
    C.v("act", lambda e: e.activation(st[:, 5:6], st[:, 5:6], AF.Exp, scale=-0.5), [st], [st])
    C.v("dve", lambda e: e.scalar_tensor_tensor(st[:, 6:7], st[:, 2:3], -1.0, st[:, 5:6], ALU.mult, ALU.mult), [st], [st])
    C.v("act", lambda e: e.activation(out_t[:], xt[:], AF.Identity, bias=st[:, 6:7], scale=st[:, 5:6]), [xt, st], [out_t])


class Arena:
    def __init__(self, C, cols):
        self.C = C
        self.t = C.es.enter_context(C.nc.sbuf_tensor("arena", [128, cols], F32))
        self.cols = cols
        self.off = 0

    def alloc(self, name, cols, parts=128):
        assert self.off + cols <= self.cols, "SBUF arena overflow at %s (%d + %d)" % (name, self.off, cols)
        t = T(self.t[0:parts, self.off:self.off + cols], name)
        self.off += cols
        return t

    def mark(self):
        return self.off

    def release(self, mark):
        self.C.S.barrier()
        self.off = mark


def make_consts():
    t = np.arange(128)
    c = {}
    c["ident"] = np.eye(128, dtype=np.float32)
    c["ones"] = np.ones((128, 128), np.float32)
    c["tri_f"] = (t[:, None] <= t[None, :]).astype(np.float32)
    c["tri_b"] = (t[:, None] >= t[None, :]).astype(np.float32)
    c["tris_f"] = (t[:, None] > t[None, :]).astype(np.float32)
    c["tris_b"] = (t[:, None] < t[None, :]).astype(np.float32)
    NEG = -30000.0
    negf = np.where(t[None, :] >= t[:, None], 0.0, NEG).astype(np.float32)
    negb = np.where(t[None, :] <= t[:, None], 0.0, NEG).astype(np.float32)
    c["neg4_f"] = np.tile(negf, (1, 4))
    c["neg4_b"] = np.tile(negb, (1, 4))
    c.update(make_gla_consts())
    c.update(make_gdn_consts())
    c.update(make_moe_consts())
    names = list(c.keys())
    offs = {}
    o = 0
    for n in names:
        offs[n] = (o, c[n].shape[1])
        o += c[n].shape[1]
    return np.ascontiguousarray(np.concatenate([c[n] for n in names], axis=1)), offs


def ssd_mixer(C, A, PS, K, cfg, l, s, ZT, ZF, O, W_):
    TC, TT, NT = cfg.TC, cfg.T, cfg.NT
    NTC = TC // 128
    m0 = A.mark()
    prm = A.alloc("ssd_prm", 64)
    C.dma("sp", prm[:, 0:8], W_["ssd_dt_bias"][l:l + 1, :].partition_broadcast(128), [W_["ssd_dt_bias"]], [prm])
    C.dma("sp", prm[:, 8:16], W_["ssd_a_log"][l:l + 1, :].partition_broadcast(128), [W_["ssd_a_log"]], [prm])
    C.dma("sp", prm[:, 16:20], W_["ssd_d"][l:l + 1, :].partition_broadcast(128), [W_["ssd_d"]], [prm])
    C.v("act", lambda e: e.activation(prm[:, 8:16], prm[:, 8:16], AF.Exp), [prm], [prm])
    C.v("dve", lambda e: e.tensor_scalar(prm[:, 8:16], prm[:, 8:16], -1.0, None, ALU.mult), [prm], [prm])
    nw = A.alloc("ssd_nw", 256)
    C.dma("sp", nw[:], W_["ssd_norm_w"][l:l + 1, :].partition_broadcast(128), [W_["ssd_norm_w"]], [nw])
    cw = A.alloc("ssd_cw", 6 * 6)
    grp = [(0, 128), (128, 128), (256, 64), (320, 64), (384, 64), (448, 64)]
    cv = [A.alloc("ssd_cv%d" % g, TT) for g in range(6)]
    raw = A.alloc("ssd_raw", TT)
    for g, (c0, n) in enumerate(grp):
        C.dma("sp", cw[0:n, g * 6:g * 6 + 5], W_["ssd_cwT"][l, c0:c0 + n, :], [W_["ssd_cwT"]], [cw])
        C.dma("sp", cw[0:n, g * 6 + 5:g * 6 + 6], W_["ssd_cbT"][l, c0:c0 + n, :], [W_["ssd_cbT"]], [cw])
        C.dma("sp", raw[0:n, :], ZF[s, 14 * 128 + c0: 14 * 128 + c0 + n, :], [ZF], [raw])
        acc = cv[g]
        C.v("dve", lambda e, acc=acc, n=n, g=g: e.tensor_scalar(acc[0:n, :], raw[0:n, :], cw[0:n, g * 6 + 2:g * 6 + 3], None, ALU.mult),
            [raw, cw], [acc])
        for (a, b) in ((0, TC), (TC, TT)):
            for k in (0, 1, 3, 4):
                sh = k - 2
                lo, hi = max(a, a - sh), min(b, b - sh)
                C.v("dve", lambda e, acc=acc, n=n, g=g, k=k, lo=lo, hi=hi, sh=sh: e.scalar_tensor_tensor(
                    acc[0:n, lo:hi], raw[0:n, lo + sh:hi + sh], cw[0:n, g * 6 + k:g * 6 + k + 1], acc[0:n, lo:hi], ALU.mult, ALU.add),
                    [raw, cw, acc], [acc])
        C.v("act", lambda e, acc=acc, n=n, g=g: e.activation(acc[0:n, :], acc[0:n, :], AF.Silu, bias=cw[0:n, g * 6 + 5:g * 6 + 6]),
            [acc, cw], [acc])
    dtA = A.alloc("ssd_dt", NT * 8)
    laA = A.alloc("ssd_la", NT * 8)
    C.dma("sp", dtA[:].rearrange("p (t c) -> p t c", c=8), ZT[s, :, 1552:1560].rearrange("(t p) c -> p t c", p=128), [ZT], [dtA])
    dt3 = dtA[:].rearrange("p (t c) -> p t c", c=8)
    la3 = laA[:].rearrange("p (t c) -> p t c", c=8)
    C.v("dve", lambda e: e.tensor_tensor(dt3, dt3, prm[:, 0:8].unsqueeze(1).to_broadcast([128, NT, 8]), ALU.add), [dtA, prm], [dtA])
    C.v("act", lambda e: e.activation(dtA[:], dtA[:], AF.Exp), [dtA], [dtA])
    C.v("act", lambda e: e.activation(dtA[:], dtA[:], AF.Ln, bias=1.0), [dtA], [dtA])
    C.v("dve", lambda e: e.tensor_tensor(la3, dt3, prm[:, 8:16].unsqueeze(1).to_broadcast([128, NT, 8]), ALU.mult), [dtA, prm], [laA])
    OF = A.alloc("ssd_of", NT * 256)
    XS = A.alloc("ssd_xs", NT * 256)
    Sst = A.alloc("ssd_S", 256)
    R4 = A.alloc("ssd_R4", 512)
    LT = A.alloc("ssd_LT", 512)
    Eg = A.alloc("ssd_Eg", 512)
    AT = A.alloc("ssd_AT", 512)
    qd = A.alloc("ssd_qd", 512)
    vv = A.alloc("ssd_v", 256)
    kd = A.alloc("ssd_kd", 256)
    bst = A.alloc("ssd_bst", 128)
    sm = A.alloc("ssd_sm", 16)
    zt = A.alloc("ssd_z", 256)
    yy = A.alloc("ssd_y", 256)
    y2 = A.alloc("ssd_y2", 256)
    pG, pU, pS, pO, pT, pD = PS[0], PS[1], PS[2], PS[3], PS[4], PS[5]
    for d in range(2):
        sfx = "_f" if d == 0 else "_b"
        tri, tris, neg4 = K("tri" + sfx), K("tris" + sfx), K("neg4" + sfx)
        last = 127 if d == 0 else 0
        order = list(range(NT)) if d == 0 else (list(range(NTC - 1, -1, -1)) + list(range(NT - 1, NTC - 1, -1)))
        C.v("dve", lambda e: e.memset(Sst[0:64, :], 0.0), [], [Sst])
        for t in order:
            ts = slice(t * 128, (t + 1) * 128)
            la = laA[:, t * 8 + d * 4: t * 8 + d * 4 + 4]
            dt = dtA[:, t * 8 + d * 4: t * 8 + d * 4 + 4]
            if d == 0:
                for c in range(2):
                    C.tr(pT, pT[:, c * 128:(c + 1) * 128], cv[c][:, ts], K.ident, [cv[c]])
                C.v("act", lambda e, t=t: e.copy(XS[:, t * 256:(t + 1) * 256], pT[:, 0:256]), [pT], [XS])
            xs = XS[:, t * 256:(t + 1) * 256]
            for g in range(2):
                C.S.op("pe", lambda e, g=g, ts=ts: e.transpose(pT[:, 256 + g * 64: 256 + (g + 1) * 64], cv[2 + g][0:64, ts], K.ident[0:64, 0:64]),
                       [cv[2 + g].res, K.t.res], [pT.res])
            C.v("act", lambda e: e.copy(bst[:], pT[:, 256:384]), [pT], [bst])
            C.v("dve", lambda e, xs=xs, dt=dt: e.tensor_tensor(vv[:].rearrange("p (h e) -> p h e", h=4), xs.rearrange("p (h e) -> p h e", h=4),
                                                               dt.unsqueeze(2).to_broadcast([128, 4, 64]), ALU.mult), [XS, dtA], [vv])
            C.v("dve", lambda e, la=la, tri=tri: e.tensor_tensor(R4[:].rearrange("p (h i) -> p h i", h=4), tri.unsqueeze(1).to_broadcast([128, 4, 128]),
                                                                 la.unsqueeze(2).to_broadcast([128, 4, 128]), ALU.mult), [K.t, laA], [R4])
            C.mm(pG, pG[:, :], K("ones"), R4[:], [K.t, R4], start=True, stop=False)
            C.mm(pG, pG[:, :], K("ident"), neg4, [K.t], start=False, stop=True)
            C.mm(pU, pU[:, :], K("ones"), R4[:], [K.t, R4])
            C.mm(pS, pS[:, 0:4], tri, la, [K.t, laA], start=True, stop=True)
            C.mm(pS, pS[:, 4:8], tris, la, [K.t, laA], start=True, stop=True)
            C.v("dve", lambda e: e.tensor_scalar(sm[:, 0:4], pS[:, 0:4], -1.0, None, ALU.mult), [pS], [sm])
            C.v("act", lambda e: e.activation(sm[:, 4:8], pS[:, 4:8], AF.Exp), [pS], [sm])
            for h in range(4):
                C.v("act", lambda e, h=h: e.activation(LT[:, h * 128:(h + 1) * 128], pG[:, h * 128:(h + 1) * 128], AF.Exp, bias=sm[:, h:h + 1]),
                    [pG, sm], [LT])
            C.v("act", lambda e: e.activation(Eg[0:64, :], pU[0:64, :], AF.Exp), [pU], [Eg])
            for g in range(2):
                C.mm(pS, pS[:, 128 + g * 128: 256 + g * 128], cv[2 + g][0:64, ts], cv[4 + g][0:64, ts], [cv[2 + g], cv[4 + g]])
            C.v("dve", lambda e: e.tensor_tensor(AT[:].rearrange("p (g a i) -> p g a i", g=2, a=2),
                                                 pS[:, 128:384].rearrange("p (g i) -> p g i", g=2).unsqueeze(2).to_broadcast([128, 2, 2, 128]),
                                                 LT[:].rearrange("p (g a i) -> p g a i", g=2, a=2), ALU.mult), [pS, LT], [AT])
            for h in range(4):
                C.v("dve", lambda e, h=h, ts=ts: e.tensor_tensor(
                    qd[0:64, h * 128:(h + 1) * 128], cv[4 + h // 2][0:64, ts], Eg[0:64, h * 128:(h + 1) * 128], ALU.mult),
                    [cv[4 + h // 2], Eg], [qd])
            C.v("dve", lambda e: e.tensor_tensor(kd[:].rearrange("p (g a n) -> p g a n", g=2, a=2),
                                                 bst[:].rearrange("p (g n) -> p g n", g=2).unsqueeze(2).to_broadcast([128, 2, 2, 64]),
                                                 sm[:, 4:8].rearrange("p (g a) -> p g a", g=2).unsqueeze(3).to_broadcast([128, 2, 2, 64]), ALU.mult),
                [bst, sm], [kd])
            for h in range(4):
                C.mm(pO, pO[:, h * 64:(h + 1) * 64], AT[:, h * 128:(h + 1) * 128], vv[:, h * 64:(h + 1) * 64], [AT, vv], start=True, stop=False)
                C.mm(pO, pO[:, h * 64:(h + 1) * 64], qd[0:64, h * 128:(h + 1) * 128], Sst[0:64, h * 64:(h + 1) * 64], [qd, Sst], start=False, stop=True)
            for h in range(4):
                C.mm(pD, pD[0:64, h * 64:(h + 1) * 64], kd[:, h * 64:(h + 1) * 64], vv[:, h * 64:(h + 1) * 64], [kd, vv])
            C.v("dve", lambda e, last=last: e.tensor_tensor(Sst[0:64, :].rearrange("p (h e) -> p h e", h=4), Sst[0:64, :].rearrange("p (h e) -> p h e", h=4),
                                                            Eg[0:64, :].rearrange("p (h i) -> p h i", h=4)[:, :, last:last + 1].to_broadcast([64, 4, 64]), ALU.mult),
                [Sst, Eg], [Sst])
            C.v("dve", lambda e: e.tensor_tensor(Sst[0:64, :], Sst[0:64, :], pD[0:64, 0:256], ALU.add), [Sst, pD], [Sst])
            if d == 0:
                C.v("act", lambda e, t=t: e.copy(OF[:, t * 256:(t + 1) * 256], pO[:, 0:256]), [pO], [OF])
            else:
                C.dma("sp", zt[:], ZT[s, t * 128:(t + 1) * 128, 1296:1552], [ZT], [zt])
                C.v("dve", lambda e, t=t: e.tensor_tensor(yy[:], pO[:, 0:256], OF[:, t * 256:(t + 1) * 256], ALU.add), [pO, OF], [yy])
                C.v("dve", lambda e, xs=xs: e.tensor_tensor(y2[:].rearrange("p (h e) -> p h e", h=4), xs.rearrange("p (h e) -> p h e", h=4),
                                                            prm[:, 16:20].unsqueeze(2).to_broadcast([128, 4, 64]), ALU.mult), [XS, prm], [y2])
                C.v("dve", lambda e: e.tensor_tensor(yy[:], yy[:], y2[:], ALU.add), [yy, y2], [yy])
                C.v("act", lambda e: e.activation(zt[:], zt[:], AF.Silu), [zt], [zt])
                C.v("dve", lambda e: e.tensor_tensor(yy[:], yy[:], zt[:], ALU.mult), [yy, zt], [yy])
                C.v("act", lambda e: e.activation(y2[:], yy[:], AF.Square, accum_out=sm[:, 8:9]), [yy, sm], [y2, sm])
                C.v("act", lambda e: e.activation(sm[:, 9:10], sm[:, 8:9], AF.Ln, bias=EPS, scale=1.0 / 256.0), [sm], [sm])
                C.v("act", lambda e: e.activation(sm[:, 9:10], sm[:, 9:10], AF.Exp, scale=-0.5), [sm], [sm])
                C.v("dve", lambda e: e.scalar_tensor_tensor(y2[:], yy[:], sm[:, 9:10], nw[:], ALU.mult, ALU.mult), [yy, sm, nw], [y2])
                C.dma("sp", O[s, t * 128:(t + 1) * 128, 768:1024], y2[:], [y2], [O])
    A.release(m0)


def make_dft(Tn):
    NTs = Tn // 128
    t = np.arange(Tn, dtype=np.float64)
    ang = 2.0 * np.pi * ((t[:, None] * t[None, :]) % Tn) / Tn
    sc = 1.0 / math.sqrt(64.0 * Tn)
    m = np.stack([np.cos(ang) * sc, -np.sin(ang) * sc]).astype(np.float32)
    m = m.reshape(2, NTs, 128, NTs, 128).transpose(0, 3, 2, 1, 4)
    return np.ascontiguousarray(m)


def make_bd64():
    ch = np.arange(256)
    same = (ch[:, None] // 64) == (ch[None, :] // 64)
    ang = 2.0 * np.pi * (((ch[:, None] % 64) * (ch[None, :] % 64)) % 64) / 64.0
    m = np.stack([np.where(same, np.cos(ang), 0.0), np.where(same, np.sin(ang), 0.0)]).astype(np.float32)
    return np.ascontiguousarray(m.reshape(2, 2, 128, 256))


def fnet_mixer(C, A, PS, K, cfg, l, s, ZF, O, W_):
    TC, TT = cfg.TC, cfg.T
    m0 = A.mark()
    bd = A.alloc("fn_bd", 1024)
    bd4 = bd[:].rearrange("p (a k n) -> p a k n", a=2, k=2)
    C.dma("sp", bd4, W_["bd64"][:, :, :, :].rearrange("a k p n -> p a k n"), [W_["bd64"]], [bd])
    yos = [A.alloc("fn_yo", 256) for _ in range(2)]
    m1 = A.mark()
    for name, a, b in (("ctx", 0, TC), ("lat", TC, TT)):
        Tn = b - a
        NTs = Tn // 128
        cu = A.alloc("fn_cu", 2 * Tn)
        cu3 = cu[:].rearrange("p (k t) -> p k t", k=2)
        for kc in range(2):
            C.dma("sp", cu3[:, kc, :], ZF[s, (12 + kc) * 128:(13 + kc) * 128, a:b], [ZF], [cu])
        XCS = A.alloc("fn_xcs", NTs * 512)
        X4 = XCS[:].rearrange("p (t a n) -> p t a n", t=NTs, a=2)
        for tt in range(NTs):
            for cs in range(2):
                ps = PS[cs]
                for kc in range(2):
                    C.mm(ps, ps[:, 0:256], cu3[:, kc, tt * 128:(tt + 1) * 128], bd4[:, cs, kc, :], [cu, bd], start=(kc == 0), stop=(kc == 1))
                C.v("dve" if cs == 0 else "act",
                    (lambda e, ps=ps, tt=tt, cs=cs, X4=X4: e.tensor_copy(X4[:, tt, cs, :], ps[:, 0:256])) if cs == 0 else
                    (lambda e, ps=ps, tt=tt, cs=cs, X4=X4: e.copy(X4[:, tt, cs, :], ps[:, 0:256])), [ps], [XCS])
        dms = [A.alloc("fn_dm", 2 * NTs * 128) for _ in range(2)]
        def load_dm(kk):
            d_ = dms[kk % 2]
            C.dma("sp", d_[:].rearrange("p (a t k) -> p a t k", a=2, t=NTs), W_["dft_" + name][:, kk, :, :, :].rearrange("a p t k -> p a t k"),
                  [W_["dft_" + name]], [d_])

        load_dm(0)
        for kt in range(NTs):
            dm = dms[kt % 2]
            dm4 = dm[:].rearrange("p (a t k) -> p a t k", a=2, t=NTs)
            if kt + 1 < NTs:
                load_dm(kt + 1)
            ps = PS[2 + kt % 2]
            n = 0
            for cs in range(2):
                for tt in range(NTs):
                    C.mm(ps, ps[:, 0:256], dm4[:, cs, tt, :], X4[:, tt, cs, :], [dm, XCS], start=(n == 0), stop=(n == 2 * NTs - 1))
                    n += 1
            yo = yos[kt % 2]
            C.v("dve", lambda e, ps=ps, yo=yo: e.tensor_copy(yo[:], ps[:, 0:256]), [ps], [yo])
            C.dma("sp", O[s, a + kt * 128: a + (kt + 1) * 128, 512:768], yo[:], [yo], [O])
        A.release(m1)
    A.release(m0)


def make_gla_consts():
    t = np.arange(128)
    WQ = np.zeros((128, 5, 128), np.float32)
    WK = np.zeros((128, 4, 128), np.float32)
    MK = np.zeros((128, 4, 128), np.float32)
    T_, I_ = t[:, None], t[None, :]
    r = 16 * (I_ // 16)
    WQ[:, 0, :] = ((T_ >= r) & (T_ <= I_))
    WK[:, 0, :] = -((T_ >= r) & (T_ <= I_)).astype(np.float32)
    MK[:, 0, :] = ((T_ // 16 == I_ // 16) & (T_ <= I_))
    for lev, B in ((1, 32), (2, 64), (3, 128)):
        m = B * (I_ // B) + B // 2
        WQ[:, lev, :] = ((I_ >= m) & (T_ >= m) & (T_ <= I_))
        WK[:, lev, :] = ((I_ < m) & (T_ > I_) & (T_ < m))
        mj = B * (T_ // B) + B // 2
        MK[:, lev, :] = ((T_ // B == I_ // B) & (T_ < mj) & (I_ >= mj))
    WQ[:, 4, :] = (T_ <= I_)
    out = {}
    for sfx, rev in (("_f", False), ("_b", True)):
        wq, wk, mk = (WQ[::-1, :, ::-1], WK[::-1, :, ::-1], MK[::-1, :, ::-1]) if rev else (WQ, WK, MK)
        out["wq" + sfx] = np.ascontiguousarray(np.concatenate([wq.reshape(128, 640), np.ones((128, 1), np.float32)], 1))
        out["wk" + sfx] = np.ascontiguousarray(wk.reshape(128, 512))
        out["mk" + sfx] = np.ascontiguousarray(mk.reshape(128, 512))
    return out


def hgrn_mixer(C, A, PS, K, cfg, l, s, ZT, ZF, O, W_):
    TC, TT, NT, DEPTH = cfg.TC, cfg.T, cfg.NT, cfg.DEPTH
    NTC = TC // 128
    m0 = A.mark()
    nw = A.alloc("hg_nw", 256)
    C.dma("sp", nw[:], W_["hgrn_norm_w"][l:l + 1, :].partition_broadcast(128), [W_["hgrn_norm_w"]], [nw])
    qT = A.alloc("hg_qT", 2 * TT)
    for c in range(2):
        C.dma("sp", qT[:, c * TT:(c + 1) * TT], ZF[s, c * 128:(c + 1) * 128, :], [ZF], [qT])
    C.i("act", "activation", qT[:], qT[:], AF.Silu, R=[qT], Wr=[qT])
    vT = A.alloc("hg_v", NT * 256)
    C.dma("sp", vT[:].rearrange("p (t c) -> p t c", c=256), ZT[s, :, 512:768].rearrange("(t p) c -> p t c", p=128), [ZT], [vT])
    OF = A.alloc("hg_of", NT * 256)
    kT = A.alloc("hg_kT", 2 * TT)
    lf = A.alloc("hg_lf", NT * 256)
    kt = A.alloc("hg_kt", NT * 256)
    lbB = A.alloc("hg_lb", 256)
    omB = A.alloc("hg_om", 256)
    lg = A.alloc("hg_lg", DEPTH * 256)
    omc = A.alloc("hg_omc", 4)
    Sst = [A.alloc("hg_S%d" % i, 64) for i in range(2)]
    EQ = A.alloc("hg_EQ", 512)
    EX = A.alloc("hg_EX", 129)
    EK = A.alloc("hg_EK", 512)
    qS = A.alloc("hg_qS", 128)
    kh = A.alloc("hg_kh", 128)
    AM = A.alloc("hg_AM", 512)
    yy = A.alloc("hg_y", 256)
    y2 = A.alloc("hg_y2", 256)
    gt = A.alloc("hg_g", 256)
    sm = A.alloc("hg_sm", 8)
    pEq, pEx, pEk, pKh, pAt, pO, pD, pT = PS
    for d in range(2):
        sfx = "_f" if d == 0 else "_b"
        wq, wk, mk, tris = K("wq" + sfx), K("wk" + sfx), K("mk" + sfx), K("tris" + sfx)
        C.dma("sp", lg[:].rearrange("p (j c) -> p j c", j=DEPTH), W_["hgrn_lb_logits"][d:d + 1, :, :].partition_broadcast(128),
              [W_["hgrn_lb_logits"]], [lg])
        C.i("act", "activation", lg[:], lg[:], AF.Exp, R=[lg], Wr=[lg])
        C.i("dve", "tensor_copy", omB[:], lg[:, 0:256], R=[lg], Wr=[omB])
        for j in range(1, DEPTH):
            C.i("dve", "tensor_tensor", omB[:], omB[:], lg[:, j * 256:(j + 1) * 256], ALU.add, R=[lg, omB], Wr=[omB])
        C.i("dve", "reciprocal", omB[:], omB[:], R=[omB], Wr=[omB])
        C.i("dve", "memset", lbB[:], 0.0, Wr=[lbB])
        for j in range(1, l + 1):
            C.i("dve", "tensor_tensor", lbB[:], lbB[:], lg[:, j * 256:(j + 1) * 256], ALU.add, R=[lg, lbB], Wr=[lbB])
        C.i("dve", "tensor_tensor", lbB[:], lbB[:], omB[:], ALU.mult, R=[lbB, omB], Wr=[lbB])
        C.i("dve", "tensor_scalar", omB[:], lbB[:], -1.0, 1.0, ALU.mult, ALU.add, R=[lbB], Wr=[omB])
        for c in range(2):
            C.tr(pT, pT[:, 0:128], omB[:, c * 128:(c + 1) * 128], K.ident, [omB])
            C.i("dve", "tensor_copy", omc[:, c:c + 1], pT[:, 0:1], R=[pT], Wr=[omc])
            C.i("dve", "tensor_scalar", omc[:, 2 + c:3 + c], pT[:, 0:1], -1.0, None, ALU.mult, R=[pT], Wr=[omc])
        for c in range(2):
            C.dma("sp", kT[:, c * TT:(c + 1) * TT], ZF[s, (2 + 2 * d + c) * 128:(3 + 2 * d + c) * 128, :], [ZF], [kT])
            C.i("act", "activation", kT[:, c * TT:(c + 1) * TT], kT[:, c * TT:(c + 1) * TT], AF.Sigmoid, R=[kT], Wr=[kT])
            C.i("dve", "tensor_scalar", kT[:, c * TT:(c + 1) * TT], kT[:, c * TT:(c + 1) * TT], omc[:, 2 + c:3 + c], omc[:, c:c + 1],
                ALU.mult, ALU.add, R=[kT, omc], Wr=[kT])
        C.dma("sp", lf[:].rearrange("p (t c) -> p t c", c=256), ZT[s, :, d * 256:(d + 1) * 256].rearrange("(t p) c -> p t c", p=128), [ZT], [lf])
        C.i("act", "activation", lf[:], lf[:], AF.Sigmoid, R=[lf], Wr=[lf])
        lf3 = lf[:].rearrange("p (t c) -> p t c", c=256)
        kt3 = kt[:].rearrange("p (t c) -> p t c", c=256)
        omB3 = omB[:].unsqueeze(1).to_broadcast([128, NT, 256])
        C.i("dve", "tensor_tensor", lf3, lf3, omB3, ALU.mult, R=[lf, omB], Wr=[lf])
        C.i("dve", "tensor_tensor", kt3, omB3, lf3, ALU.subtract, R=[lf, omB], Wr=[kt])
        C.i("dve", "tensor_tensor", lf3, lf3, lbB[:].unsqueeze(1).to_broadcast([128, NT, 256]), ALU.add, R=[lf, lbB], Wr=[lf])
        C.i("act", "activation", lf[:], lf[:], AF.Ln, R=[lf], Wr=[lf])
        order = list(range(NT)) if d == 0 else (list(range(NTC - 1, -1, -1)) + list(range(NT - 1, NTC - 1, -1)))
        for hp in range(2):
            C.i("dve", "memset", Sst[hp][:], 0.0, Wr=[Sst[hp]])
        for t in order:
            ts = slice(t * 128, (t + 1) * 128)
            for hp in range(2):
                lfs = lf[:, t * 256 + hp * 128: t * 256 + (hp + 1) * 128]
                qts = qT[:, hp * TT + t * 128: hp * TT + (t + 1) * 128]
                kts = kT[:, hp * TT + t * 128: hp * TT + (t + 1) * 128]
                C.mm(pEq, pEq[:, :], lfs, wq[:, 0:512], [lf, K.t])
                C.mm(pEx, pEx[:, 0:129], lfs, wq[:, 512:641], [lf, K.t])
                C.mm(pEk, pEk[:, :], lfs, wk, [lf, K.t])
                C.mm(pKh, pKh[:, 0:128], tris, lfs, [lf, K.t])
                C.i("act", "activation", EQ[:], pEq[:, :], AF.Exp, R=[pEq], Wr=[EQ])
                C.i("act", "activation", EX[:], pEx[:, 0:129], AF.Exp, R=[pEx], Wr=[EX])
                C.i("act", "activation", EK[:], pEk[:, :], AF.Exp, R=[pEk], Wr=[EK])
                C.i("act", "activation", kh[:], pKh[:, 0:128], AF.Exp, R=[pKh], Wr=[kh])
                C.i("dve", "tensor_tensor", EQ[:].rearrange("p (l i) -> p l i", l=4), EQ[:].rearrange("p (l i) -> p l i", l=4),
                    qts.unsqueeze(1).to_broadcast([128, 4, 128]), ALU.mult, R=[EQ, qT], Wr=[EQ])
                C.i("dve", "tensor_tensor", EK[:].rearrange("p (l i) -> p l i", l=4), EK[:].rearrange("p (l i) -> p l i", l=4),
                    kts.unsqueeze(1).to_broadcast([128, 4, 128]), ALU.mult, R=[EK, kT], Wr=[EK])
                C.i("dve", "tensor_tensor", qS[:], EX[:, 0:128], qts, ALU.mult, R=[EX, qT], Wr=[qS])
                C.i("dve", "tensor_tensor", kh[:], kh[:], kt[:, t * 256 + hp * 128: t * 256 + (hp + 1) * 128], ALU.mult, R=[kh, kt], Wr=[kh])
                for hh in range(2):
                    pr = slice(hh * 64, (hh + 1) * 64)
                    h = hp * 2 + hh
                    for lev in range(4):
                        C.mm(pAt, pAt[:, lev * 128:(lev + 1) * 128], EK[pr, lev * 128:(lev + 1) * 128], EQ[pr, lev * 128:(lev + 1) * 128], [EK, EQ])
                    C.i("dve", "tensor_tensor", AM[:], pAt[:, :], mk, ALU.mult, R=[pAt, K.t], Wr=[AM])
                    vs = vT[:, t * 256 + h * 64: t * 256 + (h + 1) * 64]
                    for lev in range(4):
                        C.mm(pO, pO[:, h * 64:(h + 1) * 64], AM[:, lev * 128:(lev + 1) * 128], vs, [AM, vT], start=(lev == 0), stop=False)
                    C.mm(pO, pO[:, h * 64:(h + 1) * 64], qS[pr, :], Sst[hp][pr, :], [qS, Sst[hp]], start=False, stop=True)
                C.mm(pD, pD[:, 0:128], kh[:], vT[:, t * 256 + hp * 128: t * 256 + (hp + 1) * 128], [kh, vT])
                for hh in range(2):
                    pr = slice(hh * 64, (hh + 1) * 64)
                    C.i("dve", "scalar_tensor_tensor", Sst[hp][pr, :], Sst[hp][pr, :], EX[pr, 128:129], pD[pr, hh * 64:(hh + 1) * 64],
                        ALU.mult, ALU.add, R=[Sst[hp], EX, pD], Wr=[Sst[hp]])
            if d == 0:
                C.i("act", "copy", OF[:, t * 256:(t + 1) * 256], pO[:, 0:256], R=[pO], Wr=[OF])
            else:
                C.dma("sp", gt[:], ZT[s, t * 128:(t + 1) * 128, 768:1024], [ZT], [gt])
                C.i("dve", "tensor_tensor", yy[:], pO[:, 0:256], OF[:, t * 256:(t + 1) * 256], ALU.add, R=[pO, OF], Wr=[yy])
                C.i("dve", "tensor_tensor", y2[:], yy[:], yy[:], ALU.mult, R=[yy], Wr=[y2])
                C.i("dve", "tensor_reduce", sm[:, 0:4], y2[:].rearrange("p (h e) -> p h e", h=4), AX.X, ALU.add, R=[y2], Wr=[sm])
                C.i("act", "activation", sm[:, 0:4], sm[:, 0:4], AF.Ln, bias=EPS, scale=1.0 / 64.0, R=[sm], Wr=[sm])
                C.i("act", "activation", sm[:, 0:4], sm[:, 0:4], AF.Exp, scale=-0.5, R=[sm], Wr=[sm])
                C.i("act", "activation", gt[:], gt[:], AF.Silu, R=[gt], Wr=[gt])
                C.i("dve", "tensor_tensor", yy[:].rearrange("p (h e) -> p h e", h=4), yy[:].rearrange("p (h e) -> p h e", h=4),
                    sm[:, 0:4].unsqueeze(2).to_broadcast([128, 4, 64]), ALU.mult, R=[yy, sm], Wr=[yy])
                C.i("dve", "tensor_tensor", yy[:], yy[:], nw[:], ALU.mult, R=[yy, nw], Wr=[yy])
                C.i("dve", "tensor_tensor", y2[:], yy[:], gt[:], ALU.mult, R=[yy, gt], Wr=[y2])
                C.dma("sp", O[s, t * 128:(t + 1) * 128, 0:256], y2[:], [y2], [O])
    A.release(m0)


def make_gdn_consts():
    t = np.arange(128)
    I_, J_ = t[:, None], t[None, :]
    c = {}
    for sfx, valid in (("_f", J_ < I_), ("_b", J_ > I_)):
        c["poss4" + sfx] = np.tile(np.where(valid, 0.0, 30000.0).astype(np.float32), (1, 4))
    bm = lambda b: (I_ // b == J_ // b)
    c["bm32"] = bm(32).astype(np.float32)
    c["be64"] = (bm(64) & ~bm(32)).astype(np.float32)
    c["be128"] = (~bm(64)).astype(np.float32)
    c["blk64"] = bm(64).astype(np.float32)
    return c


def gdn_mixer(C, A, PS, K, cfg, l, s, ZT, ZF, O, W_):
    TC, TT, NT = cfg.TC, cfg.T, cfg.NT
    NTC = TC // 128
    m0 = A.mark()
    prm = A.alloc("gd_prm", 32)
    C.dma("sp", prm[:, 0:8], W_["gdn_dt_bias"][l:l + 1, :].partition_broadcast(128), [W_["gdn_dt_bias"]], [prm])
    C.dma("sp", prm[:, 8:16], W_["gdn_a_log"][l:l + 1, :].partition_broadcast(128), [W_["gdn_a_log"]], [prm])
    C.i("act", "activation", prm[:, 8:16], prm[:, 8:16], AF.Exp, R=[prm], Wr=[prm])
    C.i("dve", "tensor_scalar", prm[:, 8:16], prm[:, 8:16], -1.0, None, ALU.mult, R=[prm], Wr=[prm])
    nw = A.alloc("gd_nw", 256)
    C.dma("sp", nw[:], W_["gdn_norm_w"][l:l + 1, :].partition_broadcast(128), [W_["gdn_norm_w"]], [nw])
    cw = A.alloc("gd_cw", 6 * 5)
    cv = [A.alloc("gd_cv%d" % g, TT) for g in range(6)]
    raw = A.alloc("gd_raw", TT)
    for g in range(6):
        C.dma("sp", cw[:, g * 5:g * 5 + 5], W_["gdn_cwT"][l, g * 128:(g + 1) * 128, :], [W_["gdn_cwT"]], [cw])
        C.dma("sp", raw[:], ZF[s, (6 + g) * 128:(7 + g) * 128, :], [ZF], [raw])
        acc = cv[g]
        C.i("dve", "tensor_scalar", acc[:], raw[:], cw[:, g * 5 + 2:g * 5 + 3], None, ALU.mult, R=[raw, cw], Wr=[acc])
        for (a, b) in ((0, TC), (TC, TT)):
            for k in (0, 1, 3, 4):
                sh = k - 2
                lo, hi = max(a, a - sh), min(b, b - sh)
                C.i("dve", "scalar_tensor_tensor", acc[:, lo:hi], raw[:, lo + sh:hi + sh], cw[:, g * 5 + k:g * 5 + k + 1], acc[:, lo:hi],
                    ALU.mult, ALU.add, R=[raw, cw, acc], Wr=[acc])
        C.i("act", "activation", acc[:], acc[:], AF.Silu, R=[acc], Wr=[acc])
    sq = A.alloc("gd_sq", 512)
    for g in range(4):
        for c0 in range(0, TT, 512):
            n = min(512, TT - c0)
            ps = PS[g % 2]
            C.i("act", "activation", sq[:, 0:n], cv[g][:, c0:c0 + n], AF.Square, R=[cv[g]], Wr=[sq])
            C.mm(ps, ps[:, 0:n], K("blk64"), sq[:, 0:n], [K.t, sq])
            C.i("act", "activation", sq[:, 0:n], ps[:, 0:n], AF.Ln, bias=EPS, R=[ps], Wr=[sq])
            C.i("act", "activation", sq[:, 0:n], sq[:, 0:n], AF.Exp, scale=-0.5, R=[sq], Wr=[sq])
            if g < 2:
                C.i("dve", "scalar_tensor_tensor", cv[g][:, c0:c0 + n], cv[g][:, c0:c0 + n], 0.125, sq[:, 0:n], ALU.mult, ALU.mult,
                    R=[cv[g], sq], Wr=[cv[g]])
            else:
                C.i("dve", "tensor_tensor", cv[g][:, c0:c0 + n], cv[g][:, c0:c0 + n], sq[:, 0:n], ALU.mult, R=[cv[g], sq], Wr=[cv[g]])
    laA = A.alloc("gd_la", NT * 8)
    btA = A.alloc("gd_bt", NT * 8)
    C.dma("sp", laA[:].rearrange("p (t c) -> p t c", c=8), ZT[s, :, 1280:1288].rearrange("(t p) c -> p t c", p=128), [ZT], [laA])
    C.dma("sp", btA[:].rearrange("p (t c) -> p t c", c=8), ZT[s, :, 1288:1296].rearrange("(t p) c -> p t c", p=128), [ZT], [btA])
    la3 = laA[:].rearrange("p (t c) -> p t c", c=8)
    C.i("dve", "tensor_tensor", la3, la3, prm[:, 0:8].unsqueeze(1).to_broadcast([128, NT, 8]), ALU.add, R=[laA, prm], Wr=[laA])
    C.i("act", "activation", laA[:], laA[:], AF.Exp, R=[laA], Wr=[laA])
    C.i("act", "activation", laA[:], laA[:], AF.Ln, bias=1.0, R=[laA], Wr=[laA])
    C.i("dve", "tensor_tensor", la3, la3, prm[:, 8:16].unsqueeze(1).to_broadcast([128, NT, 8]), ALU.mult, R=[laA, prm], Wr=[laA])
    C.i("act", "activation", btA[:], btA[:], AF.Sigmoid, R=[btA], Wr=[btA])
    OF = A.alloc("gd_of", NT * 256)
    Sst = [A.alloc("gd_S%d" % i, 64) for i in range(2)]
    R4, LT, LL, Eg = A.alloc("gd_R4", 512), A.alloc("gd_LT", 512), A.alloc("gd_L", 512), A.alloc("gd_Eg", 512)
    ktok, vtok = A.alloc("gd_ktok", 256), A.alloc("gd_vtok", 256)
    bv, bk, kd = A.alloc("gd_bv", 256), A.alloc("gd_bk", 256), A.alloc("gd_kd", 256)
    qd = A.alloc("gd_qd", 256)
    sm = A.alloc("gd_sm", 32)
    Am, AmT, QKL = A.alloc("gd_A", 128), A.alloc("gd_AT", 128), A.alloc("gd_QKL", 128)
    P_, PT_, N_, NT_ = A.alloc("gd_P", 128), A.alloc("gd_PT", 128), A.alloc("gd_N", 128), A.alloc("gd_NT", 128)
    P2, PT2, Y1, Y2, E1, E1T = (A.alloc("gd_P2", 128), A.alloc("gd_PT2", 128), A.alloc("gd_Y1", 128), A.alloc("gd_Y2", 128),
                                A.alloc("gd_E1", 128), A.alloc("gd_E1T", 128))
    nwT = A.alloc("gd_nwT", 128)
    X1 = A.alloc("gd_X1", 256)
    vn = [A.alloc("gd_vn%d" % i, 128) for i in range(2)]
    yy, y2, gt = A.alloc("gd_y", 256), A.alloc("gd_y2", 256), A.alloc("gd_g", 256)
    pG, pG2, pU, pM, pA, pB, pV, pO = PS
    ident = K("ident")
    GS = cfg.gdn_stage
    for d in range(2):
        if GS < 2:
            break
        sfx = "_f" if d == 0 else "_b"
        tri, tris, neg4, poss4 = K("tri" + sfx), K("tris" + sfx), K("neg4" + sfx), K("poss4" + sfx)
        last = 127 if d == 0 else 0
        order = list(range(NT)) if d == 0 else (list(range(NTC - 1, -1, -1)) + list(range(NT - 1, NTC - 1, -1)))
        for hp in range(2):
            C.i("dve", "memset", Sst[hp][:], 0.0, Wr=[Sst[hp]])
        for t in order:
            ts = slice(t * 128, (t + 1) * 128)
            la = laA[:, t * 8 + d * 4: t * 8 + d * 4 + 4]
            bt = btA[:, t * 8 + d * 4: t * 8 + d * 4 + 4]
            for c in range(2):
                C.tr(pV, pV[:, 256 + c * 128: 384 + c * 128], cv[2 + c][:, ts], K.ident, [cv[2 + c]])
            C.i("act", "copy", ktok[:], pV[:, 256:512], R=[pV], Wr=[ktok])
            for c in range(2):
                C.tr(pV, pV[:, 256 + c * 128: 384 + c * 128], cv[4 + c][:, ts], K.ident, [cv[4 + c]])
            C.i("act", "copy", vtok[:], pV[:, 256:512], R=[pV], Wr=[vtok])
            C.i("dve", "tensor_tensor", R4[:].rearrange("p (h i) -> p h i", h=4), tri.unsqueeze(1).to_broadcast([128, 4, 128]),
                la.unsqueeze(2).to_broadcast([128, 4, 128]), ALU.mult, R=[K.t, laA], Wr=[R4])
            C.mm(pG, pG[:, :], K("ones"), R4[:], [K.t, R4], start=True, stop=False)
            C.mm(pG, pG[:, :], ident, neg4, [K.t], start=False, stop=True)
            C.mm(pG2, pG2[:, :], K("ones"), R4[:], [K.t, R4], start=True, stop=False)
            C.mm(pG2, pG2[:, :], ident, poss4, [K.t], start=False, stop=True)
            C.mm(pU, pU[:, :], K("ones"), R4[:], [K.t, R4])
            C.mm(pM, pM[:, 0:4], tri, la, [K.t, laA])
            C.mm(pM, pM[:, 4:8], tris, la, [K.t, laA])
            C.i("dve", "tensor_scalar", sm[:, 0:4], pM[:, 0:4], -1.0, None, ALU.mult, R=[pM], Wr=[sm])
            C.i("dve", "tensor_copy", sm[:, 4:8], pM[:, 0:4], R=[pM], Wr=[sm])
            C.i("act", "activation", sm[:, 8:12], pM[:, 4:8], AF.Exp, R=[pM], Wr=[sm])
            C.i("act", "activation", sm[:, 12:16], pM[:, 0:4], AF.Exp, R=[pM], Wr=[sm])
            C.i("dve", "tensor_tensor", sm[:, 16:20], sm[:, 12:16], bt, ALU.mult, R=[sm, btA], Wr=[sm])
            for h in range(4):
                C.i("act", "activation", LT[:, h * 128:(h + 1) * 128], pG[:, h * 128:(h + 1) * 128], AF.Exp, bias=sm[:, h:h + 1], R=[pG, sm], Wr=[LT])
                C.i("act", "activation", LL[:, h * 128:(h + 1) * 128], pG2[:, h * 128:(h + 1) * 128], AF.Exp, bias=sm[:, 4 + h:5 + h], scale=-1.0,
                    R=[pG2, sm], Wr=[LL])
            C.i("act", "activation", Eg[:], pU[:, :], AF.Exp, R=[pU], Wr=[Eg])
            C.i("dve", "tensor_tensor", bv[:].rearrange("p (h e) -> p h e", h=4), vtok[:].rearrange("p (h e) -> p h e", h=4),
                bt.unsqueeze(2).to_broadcast([128, 4, 64]), ALU.mult, R=[vtok, btA], Wr=[bv])
            C.i("dve", "tensor_tensor", bk[:].rearrange("p (h e) -> p h e", h=4), ktok[:].rearrange("p (h e) -> p h e", h=4),
                sm[:, 16:20].unsqueeze(2).to_broadcast([128, 4, 64]), ALU.mult, R=[ktok, sm], Wr=[bk])
            C.i("dve", "tensor_tensor", kd[:].rearrange("p (h e) -> p h e", h=4), ktok[:].rearrange("p (h e) -> p h e", h=4),
                sm[:, 8:12].unsqueeze(2).to_broadcast([128, 4, 64]), ALU.mult, R=[ktok, sm], Wr=[kd])
            if GS < 3:
                continue
            for hp in range(2):
                for hh in range(2):
                    h = hp * 2 + hh
                    pr = slice(hh * 64, (hh + 1) * 64)
                    kTs = cv[2 + hp][pr, ts]
                    qTs = cv[hp][pr, ts]
                    C.mm(pM, pM[:, 128:256], kTs, kTs, [cv[2 + hp]])
                    C.mm(pM, pM[:, 256:384], kTs, qTs, [cv[2 + hp], cv[hp]])
                    C.i("dve", "scalar_tensor_tensor", Am[:], pM[:, 128:256], bt[:, h:h + 1], LL[:, h * 128:(h + 1) * 128], ALU.mult, ALU.mult,
                        R=[pM, btA, LL], Wr=[Am])
                    C.i("dve", "tensor_tensor", QKL[:], pM[:, 256:384], LT[:, h * 128:(h + 1) * 128], ALU.mult, R=[pM, LT], Wr=[QKL])
                    C.tr(pM, pM[:, 384:512], Am[:], K.ident, [Am])
                    C.i("act", "copy", AmT[:], pM[:, 384:512], R=[pM], Wr=[AmT])
                    C.i("dve", "tensor_tensor", qd[pr, hp * 128:(hp + 1) * 128], qTs, Eg[pr, h * 128:(h + 1) * 128], ALU.mult, R=[cv[hp], Eg], Wr=[qd])
                    if GS < 4.1:
                        continue
                    C.i("dve", "tensor_tensor", P_[:], Am[:], K("bm32"), ALU.mult, R=[Am, K.t], Wr=[P_])
                    C.i("dve", "tensor_tensor", PT_[:], AmT[:], K("bm32"), ALU.mult, R=[AmT, K.t], Wr=[PT_])
                    C.i("dve", "tensor_tensor", N_[:], ident, P_[:], ALU.subtract, R=[K.t, P_], Wr=[N_])
                    C.i("dve", "tensor_tensor", NT_[:], ident, PT_[:], ALU.subtract, R=[K.t, PT_], Wr=[NT_])
                    C.i("dve", "tensor_tensor", E1[:], Am[:], K("be64"), ALU.mult, R=[Am, K.t], Wr=[E1])
                    C.i("dve", "tensor_tensor", E1T[:], AmT[:], K("be64"), ALU.mult, R=[AmT, K.t], Wr=[E1T])
                    cp, cpt = P_, PT_
                    if GS < 4.15:
                        continue
                    for lev in range(4 if GS >= 4.4 else 1):
                        np_, npt = (P2, PT2) if lev % 2 == 0 else (P_, PT_)
                        C.mm(pA, pA[:, 0:128], cpt[:], cp[:], [cpt, cp])
                        if GS >= 4.17:
                            C.mm(pA, pA[:, 128:256], cp[:], cpt[:], [cpt, cp])
                        if GS >= 4.18:
                            C.i("act", "copy", np_[:], pA[:, 0:128], R=[pA], Wr=[np_])
                        if GS >= 4.19:
                            C.i("act", "copy", npt[:], pA[:, 128:256], R=[pA], Wr=[npt])
                        if GS < 4.3:
                            continue
                        C.mm(pB, pB[:, 0:128], npt[:], N_[:], [npt, N_])
                        C.mm(pB, pB[:, 128:256], np_[:], NT_[:], [np_, NT_])
                        C.i("act", "copy", X1[:], pB[:, 0:256], R=[pB], Wr=[X1])
                        C.i("dve", "tensor_tensor", N_[:], N_[:], X1[:, 0:128], ALU.add, R=[N_, X1], Wr=[N_])
                        C.i("dve", "tensor_tensor", NT_[:], NT_[:], X1[:, 128:256], ALU.add, R=[NT_, X1], Wr=[NT_])
                        cp, cpt = np_, npt
                    if GS < 4.6:
                        continue
                    C.mm(pA, pA[:, 256:384], E1T[:], N_[:], [E1T, N_])
                    C.mm(pA, pA[:, 384:512], E1[:], NT_[:], [E1, NT_])
                    C.i("act", "copy", Y1[:], pA[:, 256:384], R=[pA], Wr=[Y1])
                    C.i("act", "copy", Y2[:], pA[:, 384:512], R=[pA], Wr=[Y2])
                    C.mm(pB, pB[:, 256:384], NT_[:], Y1[:], [NT_, Y1])
                    C.mm(pB, pB[:, 384:512], N_[:], Y2[:], [N_, Y2])
                    C.i("act", "copy", X1[:], pB[:, 256:512], R=[pB], Wr=[X1])
                    C.i("dve", "tensor_tensor", N_[:], N_[:], X1[:, 0:128], ALU.subtract, R=[N_, X1], Wr=[N_])
                    C.i("dve", "tensor_tensor", NT_[:], NT_[:], X1[:, 128:256], ALU.subtract, R=[NT_, X1], Wr=[NT_])
                    if GS < 4.8:
                        continue
                    C.i("dve", "tensor_tensor", E1[:], Am[:], K("be128"), ALU.mult, R=[Am, K.t], Wr=[E1])
                    C.mm(pA, pA[:, 0:128], E1[:], NT_[:], [E1, NT_])
                    C.i("act", "copy", Y2[:], pA[:, 0:128], R=[pA], Wr=[Y2])
                    C.mm(pB, pB[:, 0:128], N_[:], Y2[:], [N_, Y2])
                    C.i("act", "copy", X1[:, 0:128], pB[:, 0:128], R=[pB], Wr=[X1])
                    C.i("dve", "tensor_tensor", NT_[:], NT_[:], X1[:, 0:128], ALU.subtract, R=[NT_, X1], Wr=[NT_])
                    if GS < 5:
                        continue
                    C.mm(pV, pV[:, 128:256], bk[:, hp * 128:(hp + 1) * 128], NT_[:], [bk, NT_])
                    C.i("dve", "tensor_scalar", nwT[pr, :], pV[pr, 128:256], -1.0, None, ALU.mult, R=[pV], Wr=[nwT])
                    C.mm(pV, pV[:, 0:64], NT_[:], bv[:, h * 64:(h + 1) * 64], [NT_, bv], start=True, stop=False)
                    C.mm(pV, pV[:, 0:64], nwT[pr, :], Sst[hp][pr, :], [nwT, Sst[hp]], start=False, stop=True)
                    C.i("act", "copy", vn[hp][:, hh * 64:(hh + 1) * 64], pV[:, 0:64], R=[pV], Wr=[vn[hp]])
                    C.mm(pO, pO[:, h * 64:(h + 1) * 64], qd[pr, hp * 128:(hp + 1) * 128], Sst[hp][pr, :], [qd, Sst[hp]], start=True, stop=False)
                    C.mm(pO, pO[:, h * 64:(h + 1) * 64], QKL[:], vn[hp][:, hh * 64:(hh + 1) * 64], [QKL, vn[hp]], start=False, stop=True)
                if GS < 6:
                    continue
                C.mm(pO, pO[:, 256:384], kd[:, hp * 128:(hp + 1) * 128], vn[hp][:], [kd, vn[hp]])
                for hh in range(2):
                    h = hp * 2 + hh
                    pr = slice(hh * 64, (hh + 1) * 64)
                    C.i("dve", "scalar_tensor_tensor", Sst[hp][pr, :], Sst[hp][pr, :], Eg[pr, h * 128 + last: h * 128 + last + 1],
                        pO[pr, 256 + hh * 64: 256 + (hh + 1) * 64], ALU.mult, ALU.add, R=[Sst[hp], Eg, pO], Wr=[Sst[hp]])
            if GS < 6.5:
                continue
            if d == 0:
                C.i("act", "copy", OF[:, t * 256:(t + 1) * 256], pO[:, 0:256], R=[pO], Wr=[OF])
            else:
                if GS < 6.6:
                    continue
                C.dma("sp", gt[:], ZT[s, t * 128:(t + 1) * 128, 1024:1280], [ZT], [gt])
                if GS < 6.7:
                    continue
                C.i("act", "copy", yy[:], pO[:, 0:256], R=[pO], Wr=[yy])
                if GS < 6.8:
                    continue
                C.i("dve", "tensor_tensor", yy[:], yy[:], OF[:, t * 256:(t + 1) * 256], ALU.add, R=[yy, OF], Wr=[yy])
                C.i("dve", "tensor_tensor", y2[:], yy[:], yy[:], ALU.mult, R=[yy], Wr=[y2])
                C.i("dve", "tensor_reduce", sm[:, 24:28], y2[:].rearrange("p (h e) -> p h e", h=4), AX.X, ALU.add, R=[y2], Wr=[sm])
                C.i("act", "activation", sm[:, 24:28], sm[:, 24:28], AF.Ln, bias=EPS, scale=1.0 / 64.0, R=[sm], Wr=[sm])
                C.i("act", "activation", sm[:, 24:28], sm[:, 24:28], AF.Exp, scale=-0.5, R=[sm], Wr=[sm])
                C.i("act", "activation", gt[:], gt[:], AF.Silu, R=[gt], Wr=[gt])
                C.i("dve", "tensor_tensor", yy[:].rearrange("p (h e) -> p h e", h=4), yy[:].rearrange("p (h e) -> p h e", h=4),
                    sm[:, 24:28].unsqueeze(2).to_broadcast([128, 4, 64]), ALU.mult, R=[yy, sm], Wr=[yy])
                C.i("dve", "tensor_tensor", yy[:], yy[:], nw[:], ALU.mult, R=[yy, nw], Wr=[yy])
                C.i("dve", "tensor_tensor", y2[:], yy[:], gt[:], ALU.mult, R=[yy, gt], Wr=[y2])
                C.dma("sp", O[s, t * 128:(t + 1) * 128, 256:512], y2[:], [y2], [O])
    A.release(m0)


def token_groups(cfg, with_ctx):
    NTC, NT = cfg.TC // 128, cfg.NT
    gs = []
    if with_ctx:
        for a in range(0, NTC, 4):
            gs.append((a, min(4, NTC - a), True))
    for a in range(NTC, NT, 4):
        gs.append((a, min(4, NT - a), False))
    return gs


def merge_phase(C, A, PS, K, cfg, l, s, X, HT, O, MOD, W_):
    NS, TC, TT, NT, DEPTH = cfg.NS, cfg.TC, cfg.T, cfg.NT, cfg.DEPTH
    alpha = (2.0 * DEPTH) ** 0.25
    with_ctx = l < DEPTH - 1
    m0 = A.mark()
    wgs = [A.alloc("mg_wg", 4096) for _ in range(2)]
    wbs = [A.alloc("mg_wb", 1024) for _ in range(2)]
    wcnt = 0
    wo = A.alloc("mg_wo", 8192)
    hT = A.alloc("mg_hT", 4096)
    oT = A.alloc("mg_oT", 4096)
    mT = A.alloc("mg_mT", 4096)
    bgT = A.alloc("mg_bg", 32)
    g1B = [A.alloc("mg_g1_%d" % i, 1024) for i in range(2)]
    lnG, lnB = A.alloc("mg_lnG", 1024), A.alloc("mg_lnB", 1024)
    ot = A.alloc("mg_ot", 1024)
    xt, rr, xn = A.alloc("mg_xt", 1024), A.alloc("mg_r", 1024), A.alloc("mg_xn", 1024)
    sig, tmp = A.alloc("mg_sig", 512), A.alloc("mg_tmp", 512)
    st = A.alloc("mg_st", 8)
    C.dma("sp", bgT[:], W_["b_gateT"][l, :, :], [W_["b_gateT"]], [bgT])
    C.dma("sp", g1B[0][:], MOD[l, s:s + 1, 2048:3072].partition_broadcast(128), [MOD], [g1B[0]])
    C.dma("sp", g1B[1][:], MOD[l, NS:NS + 1, 2048:3072].partition_broadcast(128), [MOD], [g1B[1]])
    C.dma("sp", lnG[:], W_["ln1_g"][l:l + 1, :].partition_broadcast(128), [W_["ln1_g"]], [lnG])
    C.dma("sp", lnB[:], W_["ln1_b"][l:l + 1, :].partition_broadcast(128), [W_["ln1_b"]], [lnB])
    C.dma("sp", wo[:].rearrange("p (k n) -> p k n", k=8), W_["w_out"][l, :, :].rearrange("(k p) n -> p k n", p=128), [W_["w_out"]], [wo])
    for (t0, ntile, is_ctx) in token_groups(cfg, with_ctx):
        ntok = ntile * 128
        c0 = t0 * 128
        C.dma("sp", hT[:].rearrange("p (k n) -> p k n", k=8)[:, :, 0:ntok], HT[s, :, :, c0:c0 + ntok].rearrange("k p n -> p k n"), [HT], [hT])
        for ti in range(ntile):
            C.dma("sp", ot[:], O[s, c0 + ti * 128: c0 + (ti + 1) * 128, :], [O], [ot])
            for half in range(2):
                ps = PS[half]
                for kk in range(4):
                    C.tr(ps, ps[:, kk * 128:(kk + 1) * 128], ot[:, (half * 4 + kk) * 128:(half * 4 + kk + 1) * 128], K.ident, [ot])
                for kk in range(4):
                    k = half * 4 + kk
                    C.i("act" if kk % 2 else "dve", "copy" if kk % 2 else "tensor_copy",
                        oT[:, k * 512 + ti * 128: k * 512 + (ti + 1) * 128], ps[:, kk * 128:(kk + 1) * 128], R=[ps], Wr=[oT])
        for g in range(4):
          for nh in range(2):
            wg, wb = wgs[wcnt % 2], wbs[wcnt % 2]
            wcnt += 1
            C.dma("sp", wg[:].rearrange("p (k n) -> p k n", k=8), W_["w_gate"][l, g, :, nh * 512:(nh + 1) * 512].rearrange("(k p) n -> p k n", p=128),
                  [W_["w_gate"]], [wg])
            C.dma("sp", wb[:].rearrange("p (k n) -> p k n", k=2), W_["w_branch"][l, g, :, nh * 512:(nh + 1) * 512].rearrange("(k p) n -> p k n", p=128),
                  [W_["w_branch"]], [wb])
            for n in range(nh * 4, nh * 4 + 4):
                nn = n - nh * 4
                pg, pb = PS[2 + n % 2], PS[4 + n % 2]
                for k in range(8):
                    C.mm(pg, pg[:, 0:ntok], wg[:, k * 512 + nn * 128: k * 512 + (nn + 1) * 128], hT[:, k * 512: k * 512 + ntok], [wg, hT],
                         start=(k == 0), stop=(k == 7))
                for k in range(2):
                    C.mm(pb, pb[:, 0:ntok], wb[:, k * 512 + nn * 128: k * 512 + (nn + 1) * 128], oT[:, (2 * g + k) * 512: (2 * g + k) * 512 + ntok], [wb, oT],
                         start=(k == 0), stop=(k == 1))
                C.i("act", "activation", sig[:, 0:ntok], pg[:, 0:ntok], AF.Sigmoid, bias=bgT[:, g * 8 + n: g * 8 + n + 1], R=[pg, bgT], Wr=[sig])
                if g == 0:
                    C.i("dve", "tensor_tensor", mT[:, n * 512: n * 512 + ntok], sig[:, 0:ntok], pb[:, 0:ntok], ALU.mult, R=[sig, pb], Wr=[mT])
                else:
                    C.i("dve", "tensor_tensor", tmp[:, 0:ntok], sig[:, 0:ntok], pb[:, 0:ntok], ALU.mult, R=[sig, pb], Wr=[tmp])
                    C.i("dve", "tensor_tensor", mT[:, n * 512: n * 512 + ntok], mT[:, n * 512: n * 512 + ntok], tmp[:, 0:ntok], ALU.add, R=[mT, tmp], Wr=[mT])
        gB = g1B[1] if is_ctx else g1B[0]
        for ti in range(ntile):
            r0 = c0 + ti * 128
            C.dma("sp", xt[:], X[s, r0:r0 + 128, :], [X], [xt])
            for half in range(2):
                py = PS[6 + half]
                hs = slice(half * 512, (half + 1) * 512)
                for k in range(8):
                    C.mm(py, py[:, :], mT[:, k * 512 + ti * 128: k * 512 + (ti + 1) * 128], wo[:, k * 1024 + half * 512: k * 1024 + (half + 1) * 512],
                         [mT, wo], start=(k == 0), stop=(k == 7))
                C.i("dve", "tensor_tensor", rr[:, hs], py[:, :], gB[:, hs], ALU.mult, R=[py, gB], Wr=[rr])
                C.i("dve", "scalar_tensor_tensor", rr[:, hs], xt[:, hs], alpha, rr[:, hs], ALU.mult, ALU.add, R=[xt, rr], Wr=[rr])
            layernorm(C, rr, xn, st)
            C.i("dve", "tensor_tensor", xn[:], xn[:], lnG[:], ALU.mult, R=[xn, lnG], Wr=[xn])
            C.i("dve", "tensor_tensor", xn[:], xn[:], lnB[:], ALU.add, R=[xn, lnB], Wr=[xn])
            C.dma("sp", X[s, r0:r0 + 128, :], xn[:], [xn], [X])
    A.release(m0)


def make_moe_consts():
    c = {}
    p = np.arange(128, dtype=np.float32)
    c["tcol"] = (p[:, None] + 128.0 * np.arange(16, dtype=np.float32)[None, :]).astype(np.float32)
    c["iota512"] = np.tile(np.arange(512, dtype=np.float32)[None, :], (128, 1))
    return c


def moe_phase(C, A, PS, K, cfg, l, s, X, H2, YE, IDX, MOD, W_):
    NS, TC, TT, NT, DEPTH = cfg.NS, cfg.TC, cfg.T, cfg.NT, cfg.DEPTH
    alpha = (2.0 * DEPTH) ** 0.25
    with_ctx = l < DEPTH - 1
    streams = ([(0, TC, NS)] if with_ctx else []) + [(TC, TT, s)]
    m0 = A.mark()
    lnG, lnB = A.alloc("mo_lnG", 1024), A.alloc("mo_lnB", 1024)
    C.dma("sp", lnG[:], W_["ln2_g"][l:l + 1, :].partition_broadcast(128), [W_["ln2_g"]], [lnG])
    C.dma("sp", lnB[:], W_["ln2_b"][l:l + 1, :].partition_broadcast(128), [W_["ln2_b"]], [lnB])
    for (a, b, ms) in streams:
        Tn = b - a
        NTs = Tn // 128
        cap = 2 * Tn // E
        ncc = (cap + 127) // 128
        rows = min(128, cap)
        m1 = A.mark()
        modB = A.alloc("mo_modB", 3 * 1024)
        C.dma("sp", modB[:], MOD[l, ms:ms + 1, 3072:6144].partition_broadcast(128), [MOD], [modB])
        C.i("dve", "tensor_scalar", modB[:, 1024:2048], modB[:, 1024:2048], 1.0, None, ALU.add, R=[modB], Wr=[modB])
        AFF = A.alloc("mo_aff", Tn)
        wv, ixf = A.alloc("mo_wv", cap), A.alloc("mo_ixf", cap)
        idxT, wT = A.alloc("mo_idxT", ncc * 16), A.alloc("mo_wT", ncc * 16)
        m2 = A.mark()
        wr = A.alloc("mo_wr", 8 * 16)
        C.dma("sp", wr[:].rearrange("p (k e) -> p k e", k=8), W_["w_router"][l, :, :].rearrange("(k p) e -> p k e", p=128), [W_["w_router"]], [wr])
        xts = [A.alloc("mo_xt", 1024) for _ in range(2)]
        xns = [A.alloc("mo_xn", 1024) for _ in range(2)]
        h2T = A.alloc("mo_h2T", 1024)
        sts = [A.alloc("mo_st", 8) for _ in range(2)]
        lg, sm = A.alloc("mo_lg", 16), A.alloc("mo_sm", 8)
        for tt in range(NTs):
            r0 = a + tt * 128
            xt, xn, st = xts[tt % 2], xns[tt % 2], sts[tt % 2]
            C.dma("sp", xt[:], X[s, r0:r0 + 128, :], [X], [xt])
            layernorm(C, xt, xn, st)
            C.i("dve", "tensor_tensor", xn[:], xn[:], modB[:, 1024:2048], ALU.mult, R=[xn, modB], Wr=[xn])
            C.i("dve", "tensor_tensor", xn[:], xn[:], modB[:, 0:1024], ALU.add, R=[xn, modB], Wr=[xn])
            C.dma("sp", H2[s, r0:r0 + 128, :], xn[:], [xn], [H2])
            for half in range(2):
                ps = PS[half]
                for kk in range(4):
                    C.tr(ps, ps[:, kk * 128:(kk + 1) * 128], xn[:, (half * 4 + kk) * 128:(half * 4 + kk + 1) * 128], K.ident, [xn])
                C.i("act" if half else "dve", "copy" if half else "tensor_copy", h2T[:, half * 512:(half + 1) * 512], ps[:, :], R=[ps], Wr=[h2T])
            pl = PS[2]
            for k in range(8):
                C.mm(pl, pl[:, 0:16], h2T[:, k * 128:(k + 1) * 128], wr[:, k * 16:(k + 1) * 16], [h2T, wr], start=(k == 0), stop=(k == 7))
            C.i("dve", "reduce_max", sm[:, 0:1], pl[:, 0:16], AX.X, R=[pl], Wr=[sm])
            C.i("dve", "tensor_scalar", sm[:, 1:2], sm[:, 0:1], -1.0, None, ALU.mult, R=[sm], Wr=[sm])
            C.i("act", "activation", lg[:], pl[:, 0:16], AF.Exp, bias=sm[:, 1:2], accum_out=sm[:, 2:3], R=[pl, sm], Wr=[lg, sm])
            C.i("dve", "reciprocal", sm[:, 3:4], sm[:, 2:3], R=[sm], Wr=[sm])
            C.i("dve", "tensor_scalar", lg[:], lg[:], sm[:, 3:4], None, ALU.mult, R=[lg, sm], Wr=[lg])
            C.S.op("pe", lambda e, lg=lg, tt=tt: e.transpose(PS[3][0:16, (tt % 4) * 128:(tt % 4 + 1) * 128], lg[:, 0:16], K("ident")),
                   [lg.res, K.t.res], [PS[3].res])
            C.i("act", "copy", AFF[0:16, tt * 128:(tt + 1) * 128], PS[3][0:16, (tt % 4) * 128:(tt % 4 + 1) * 128], R=[PS[3]], Wr=[AFF])
        ix = A.alloc("mo_ix", cap)
        ixu = T(ix.ap.bitcast(U32), "ixu")
        ixu.res = ix.res
        for r in range(cap // 8):
            C.i("dve", "max", out=wv[0:16, r * 8:(r + 1) * 8], in_=AFF[0:16, :], R=[AFF], Wr=[wv])
            C.i("dve", "max_index", out=ixu[0:16, r * 8:(r + 1) * 8], in_max=wv[0:16, r * 8:(r + 1) * 8], in_values=AFF[0:16, :], R=[AFF, wv], Wr=[ix])
            C.i("dve", "match_replace", out=AFF[0:16, :], in_to_replace=wv[0:16, r * 8:(r + 1) * 8], in_values=AFF[0:16, :], imm_value=-1.0,
                R=[AFF, wv], Wr=[AFF])
        C.i("dve", "tensor_copy", ixf[0:16, :], ixu[0:16, :], R=[ix], Wr=[ixf])
        C.dma("sp", IDX[0:16, 0:cap], ixf[0:16, :], [ixf], [IDX])
        for cc in range(ncc):
            for src, dst in ((ixf, idxT), (wv, wT)):
                C.S.op("pe", lambda e, src=src, cc=cc, rows=rows: e.transpose(PS[3][0:rows, 0:16], src[0:16, cc * 128: cc * 128 + rows], K("ident")[0:16, 0:16]),
                       [src.res, K.t.res], [PS[3].res])
                C.i("dve", "tensor_copy", dst[0:rows, cc * 16:(cc + 1) * 16], PS[3][0:rows, 0:16], R=[PS[3]], Wr=[dst])
        A.release(m2)
        m3 = A.mark()
        Wg, Wu, Wd = A.alloc("mo_Wg", 8192), A.alloc("mo_Wu", 8192), A.alloc("mo_Wd", 8192)
        XeT, hidT, ye = A.alloc("mo_XeT", 8 * cap), A.alloc("mo_hid", 8 * cap), A.alloc("mo_ye", ncc * 1024)
        h2s = [A.alloc("mo_h2", 1024) for _ in range(2)]
        Pm = [A.alloc("mo_P", cap) for _ in range(2)]
        idr, tmp = A.alloc("mo_idr", cap), A.alloc("mo_tmp", cap)
        xe = A.alloc("mo_xe", ncc * 1024)
        def load_w(wt, nm, ee):
            C.dma("sp", wt[:].rearrange("p (k n) -> p k n", k=8), W_[nm][l, ee, :, :].rearrange("(k p) n -> p k n", p=128), [W_[nm]], [wt])

        idrs = [idr, A.alloc("mo_idr2", cap)]
        C.dma("sp", idrs[0][:], IDX[0:1, 0:cap].partition_broadcast(128), [IDX], [idrs[0]])
        for wt, nm in ((Wg, "w_ff_gate"), (Wu, "w_ff_up"), (Wd, "w_ff_down")):
            load_w(wt, nm, 0)
        for e_ in range(E):
            idr = idrs[e_ % 2]
            if e_ + 1 < E:
                C.dma("sp", idrs[(e_ + 1) % 2][:], IDX[e_ + 1:e_ + 2, 0:cap].partition_broadcast(128), [IDX], [idrs[(e_ + 1) % 2]])
            for tt in range(NTs):
                h2t, P = h2s[tt % 2], Pm[tt % 2]
                C.dma("sp", h2t[:], H2[s, a + tt * 128: a + (tt + 1) * 128, :], [H2], [h2t])
                C.i("dve", "tensor_scalar", P[:], idr[:], K("tcol")[:, tt:tt + 1], None, ALU.is_equal, R=[idr, K.t], Wr=[P])
                if e_ > 0 and tt == min(1, NTs - 1):
                    load_w(Wd, "w_ff_down", e_)
                for cc in range(ncc):
                    for half in range(2):
                        pb = PS[cc * 2 + half]
                        C.mm(pb, pb[0:rows, :], P[:, cc * 128: cc * 128 + rows], h2t[:, half * 512:(half + 1) * 512], [h2t, P],
                             start=(tt == 0), stop=(tt == NTs - 1))
            for cc in range(ncc):
                for half in range(2):
                    pb = PS[cc * 2 + half]
                    C.i("act" if half else "dve", "copy" if half else "tensor_copy", xe[0:rows, cc * 1024 + half * 512: cc * 1024 + (half + 1) * 512],
                        pb[0:rows, :], R=[pb], Wr=[xe])
            for cc in range(ncc):
                for half in range(2):
                    pb = PS[4 + half]
                    for kk in range(4):
                        k = half * 4 + kk
                        C.S.op("pe", lambda e, pb=pb, kk=kk, k=k, cc=cc, rows=rows, xe=xe: e.transpose(pb[:, kk * 128: kk * 128 + rows], xe[0:rows, cc * 1024 + k * 128: cc * 1024 + (k + 1) * 128],
                                                                                       K("ident")[0:rows, 0:rows]), [xe.res, K.t.res], [pb.res])
                    for kk in range(4):
                        k = half * 4 + kk
                        C.i("act" if kk % 2 else "dve", "copy" if kk % 2 else "tensor_copy", XeT[:, k * cap + cc * 128: k * cap + cc * 128 + rows],
                            pb[:, kk * 128: kk * 128 + rows], R=[pb], Wr=[XeT])
            for f in range(8):
                pg, pu = PS[6], PS[7]
                for k in range(8):
                    C.mm(pg, pg[:, 0:cap], Wg[:, k * 1024 + f * 128: k * 1024 + (f + 1) * 128], XeT[:, k * cap:(k + 1) * cap], [Wg, XeT], start=(k == 0), stop=(k == 7))
                for k in range(8):
                    C.mm(pu, pu[:, 0:cap], Wu[:, k * 1024 + f * 128: k * 1024 + (f + 1) * 128], XeT[:, k * cap:(k + 1) * cap], [Wu, XeT], start=(k == 0), stop=(k == 7))
                C.i("act", "activation", tmp[:], pg[:, 0:cap], AF.Silu, R=[pg], Wr=[tmp])
                C.i("dve", "tensor_tensor", hidT[:, f * cap:(f + 1) * cap], tmp[:], pu[:, 0:cap], ALU.mult, R=[tmp, pu], Wr=[hidT])
            if e_ + 1 < E:
                load_w(Wg, "w_ff_gate", e_ + 1)
                load_w(Wu, "w_ff_up", e_ + 1)
            for cc in range(ncc):
                for half in range(2):
                    pd = PS[half]
                    for f in range(8):
                        C.mm(pd, pd[0:rows, :], hidT[:, f * cap + cc * 128: f * cap + cc * 128 + rows], Wd[:, f * 1024 + half * 512: f * 1024 + (half + 1) * 512],
                             [hidT, Wd], start=(f == 0), stop=(f == 7))
                    C.i("act" if half else "dve", "copy" if half else "tensor_copy", ye[0:rows, cc * 1024 + half * 512: cc * 1024 + (half + 1) * 512],
                        pd[0:rows, :], R=[pd], Wr=[ye])
                C.dma("sp", YE[e_, cc * 128: cc * 128 + rows, :], ye[0:rows, cc * 1024:(cc + 1) * 1024], [ye], [YE])
        A.release(m3)
        yes_ = [A.alloc("mo_yeB", ncc * 1024) for _ in range(2)]
        PwT = [A.alloc("mo_PwT", ncc * 512) for _ in range(2)]
        idxs = A.alloc("mo_idxs", ncc * 16)
        xt, rr, xn, st = A.alloc("mo_xtB", 1024), A.alloc("mo_rB", 1024), A.alloc("mo_xnB", 1024), A.alloc("mo_stB", 8)
        for g0 in range(0, NTs, 4):
            ntile = min(4, NTs - g0)
            ntok = ntile * 128
            C.i("dve", "tensor_scalar", idxs[0:rows, :], idxT[0:rows, :], float(-g0 * 128), None, ALU.add, R=[idxT], Wr=[idxs])
            for e_ in range(E):
                yb, pw = yes_[e_ % 2], PwT[e_ % 2]
                C.dma("sp", yb[0:rows, :].rearrange("p (c n) -> p c n", c=ncc), YE[e_, 0:ncc * rows, :].rearrange("(c p) n -> p c n", p=rows), [YE], [yb])
                for cc in range(ncc):
                    C.i("dve", "tensor_scalar", pw[0:rows, cc * 512: cc * 512 + ntok], K("iota512")[0:rows, 0:ntok], idxs[0:rows, cc * 16 + e_: cc * 16 + e_ + 1],
                        wT[0:rows, cc * 16 + e_: cc * 16 + e_ + 1], ALU.is_equal, ALU.mult, R=[K.t, idxs, wT], Wr=[pw])
                for ti in range(ntile):
                    for half in range(2):
                        py = PS[ti * 2 + half]
                        for cc in range(ncc):
                            C.mm(py, py[:, :], pw[0:rows, cc * 512 + ti * 128: cc * 512 + (ti + 1) * 128], yb[0:rows, cc * 1024 + half * 512: cc * 1024 + (half + 1) * 512],
                                 [pw, yb], start=(e_ == 0 and cc == 0), stop=(e_ == E - 1 and cc == ncc - 1))
            for ti in range(ntile):
                r0 = a + (g0 + ti) * 128
                C.dma("sp", xt[:], X[s, r0:r0 + 128, :], [X], [xt])
                for half in range(2):
                    py = PS[ti * 2 + half]
                    hs = slice(half * 512, (half + 1) * 512)
                    C.i("dve", "tensor_tensor", rr[:, hs], py[:, :], modB[:, 2048 + half * 512: 2048 + (half + 1) * 512], ALU.mult, R=[py, modB], Wr=[rr])
                    C.i("dve", "scalar_tensor_tensor", rr[:, hs], xt[:, hs], alpha, rr[:, hs], ALU.mult, ALU.add, R=[xt, rr], Wr=[rr])
                layernorm(C, rr, xn, st)
                C.i("dve", "tensor_tensor", xn[:], xn[:], lnG[:], ALU.mult, R=[xn, lnG], Wr=[xn])
                C.i("dve", "tensor_tensor", xn[:], xn[:], lnB[:], ALU.add, R=[xn, lnB], Wr=[xn])
                C.dma("sp", X[s, r0:r0 + 128, :], xn[:], [xn], [X])
        A.release(m1)
    A.release(m0)


CONSTS_NP, CONST_OFFS = make_consts()


def build(cfg, debug_out=(), dbg_cols=(768,)):
    NS, TC, TL, DEPTH, TT, NT, NST = cfg.NS, cfg.TC, cfg.TL, cfg.DEPTH, cfg.T, cfg.NT, cfg.NST
    nc = bass.Bass("TRN2", target_bir_lowering=False)
    es = ExitStack()
    C = Ctx(nc, es)
    S = C.S
    A = Arena(C, 52000)
    PS = [C.ps("ps%d" % i) for i in range(8)]

    def ein(name, shape):
        return C.dram(name, shape, kind="ExternalInput")

    x_in = ein("x", [NS, TL, D])
    ctx_in = ein("ctx", [NS, TC, D])
    cT_in = ein("cT", [128, 8, NST])
    pos_in = ein("pos", [TL, D])
    consts_in = ein("consts", list(CONSTS_NP.shape))
    W_ = {}
    for nm, shp in (("ssd_dt_bias", [DEPTH, 8]), ("ssd_a_log", [DEPTH, 8]), ("ssd_d", [DEPTH, 4]), ("ssd_norm_w", [DEPTH, 256]),
                    ("ssd_cwT", [DEPTH, 512, 5]), ("ssd_cbT", [DEPTH, 512, 1]),
                    ("hgrn_lb_logits", [2, DEPTH, 256]), ("hgrn_norm_w", [DEPTH, 256]),
                    ("gdn_dt_bias", [DEPTH, 8]), ("gdn_a_log", [DEPTH, 8]), ("gdn_norm_w", [DEPTH, 256]), ("gdn_cwT", [DEPTH, 768, 5]),
                    ("w_gate", [DEPTH, 4, D, D]), ("b_gateT", [DEPTH, 128, 32]), ("w_branch", [DEPTH, 4, W, D]), ("w_out", [DEPTH, D, D]),
                    ("ln1_g", [DEPTH, D]), ("ln1_b", [DEPTH, D]),
                    ("w_router", [DEPTH, D, E]), ("w_ff_gate", [DEPTH, E, D, FF]), ("w_ff_up", [DEPTH, E, D, FF]), ("w_ff_down", [DEPTH, E, FF, D]),
                    ("ln2_g", [DEPTH, D]), ("ln2_b", [DEPTH, D]),
                    ("bd64", [2, 2, 128, 256]), ("dft_ctx", [2, TC // 128, 128, TC // 128, 128]),
                    ("dft_lat", [2, TL // 128, 128, TL // 128, 128])):
        W_[nm] = ein(nm, shp)
    ada_w = ein("ada_w", [DEPTH, D, 6 * D])
    ada_b = ein("ada_b", [DEPTH, 6 * D])
    w_in = ein("w_in", [DEPTH, D, IN_COLS])
    out = C.dram("out", [NS, TL, D], kind="ExternalOutput")
    dbg = {k: C.dram(k, shp, kind="ExternalOutput") for k, shp in debug_out}

    X = C.dram("X", [NS, TT, D])
    MOD = C.dram("MOD", [DEPTH, NST, 6 * D])
    ZT_COLS = 1560
    ZT = C.dram("ZT", [NS, TT, ZT_COLS])
    ZF = C.dram("ZF", [NS, 18 * 128, TT])
    HT = C.dram("HT", [NS, 8, 128, TT])
    H2 = C.dram("H2", [NS, TT, D])
    YE = C.dram("YE", [E, 2 * TL // E, D])
    IDX = C.dram("IDX", [E, 2 * TL // E])

    O = C.dram("O", [NS, TT, D])
    KT = A.alloc("consts", CONSTS_NP.shape[1])
    C.dma("sp", KT[:], consts_in[:, :], [consts_in], [KT])

    def K(name):
        o, n = CONST_OFFS[name]
        return KT[:, o:o + n]
    K.t = KT
    ident = T(K("ident"), "ident")
    ident.res = KT.res
    K.ident = ident

    m0 = A.mark()
    xts = [A.alloc("xt", D) for _ in range(2)]
    pts = [A.alloc("pt", D) for _ in range(2)]
    for s in range(NS):
        C.dma("sp", X[s, 0:TC, :], ctx_in[s, :, :], [ctx_in], [X])
        for t in range(TL // 128):
            xt = xts[t % 2]
            pt = pts[t % 2]
            C.dma("sp", xt[:], x_in[s, t * 128:(t + 1) * 128, :], [x_in], [xt])
            C.dma("sp", pt[:], pos_in[t * 128:(t + 1) * 128, :], [pos_in], [pt])
            C.v("dve", lambda e, xt=xt, pt=pt: e.tensor_tensor(xt[:], xt[:], pt[:], ALU.add), [xt, pt], [xt])
            C.dma("sp", X[s, TC + t * 128:TC + (t + 1) * 128, :], xt[:], [xt], [X])
    A.release(m0)

    scT = A.alloc("scT", 8 * NST)
    C.dma("sp", scT[:], cT_in[:, :, :].rearrange("p k s -> p (k s)"), [cT_in], [scT])
    C.v("act", lambda e: e.activation(scT[:], scT[:], AF.Silu), [scT], [scT])
    m1 = A.mark()
    for l in range(DEPTH):
        abB = A.alloc("abB", 6 * D, parts=NST)
        modrow = A.alloc("modrow", 6 * D, parts=NST)
        C.dma("sp", abB[:], ada_b[l:l + 1, :].partition_broadcast(NST), [ada_b], [abB])
        aws = [A.alloc("aw", 8 * 512) for _ in range(2)]
        for ng in range(12):
            aw = aws[ng % 2]
            C.dma("sp", aw[:].rearrange("p (k n) -> p k n", k=8),
                  ada_w[l, :, ng * 512:(ng + 1) * 512].rearrange("(k p) n -> p k n", p=128), [ada_w], [aw])
            ps = PS[ng % 2]
            for k in range(8):
                C.mm(ps, ps[0:NST, :], scT[:, k * NST:(k + 1) * NST], aw[:, k * 512:(k + 1) * 512], [scT, aw],
                     start=(k == 0), stop=(k == 7))
            C.v("dve", lambda e, ps=ps, ng=ng, modrow=modrow, abB=abB: e.tensor_tensor(
                modrow[:, ng * 512:(ng + 1) * 512], ps[0:NST, :], abB[:, ng * 512:(ng + 1) * 512], ALU.add),
                [ps, abB], [modrow])
            if ng % 4 == 3:
                pass
        C.dma("sp", MOD[l, :, :], modrow[:], [modrow], [MOD])
        A.release(m1)

    for l in range(DEPTH):
        mL = A.mark()
        modF = A.alloc("modF", 4 * 8 * NST)
        mm_ = A.mark()
        modrow = A.alloc("modrow2", 6 * D, parts=NST)
        C.dma("sp", modrow[:], MOD[l, :, :], [MOD], [modrow])
        secs = (0, 1, 3, 4)
        ps = PS[0]
        for si, sec in enumerate(secs):
            for k in range(8):
                col = (si * 8 + k) * NST
                S.op("pe", lambda e, col=col, sec=sec, k=k, ps=ps, modrow=modrow: e.transpose(
                    ps[:, col:col + NST], modrow[0:NST, sec * D + k * 128: sec * D + (k + 1) * 128], ident[0:NST, 0:NST]),
                    [modrow.res, ident.res], [ps.res])
        C.v("dve", lambda e, ps=ps, modF=modF: e.tensor_copy(modF[:], ps[:, 0:32 * NST]), [ps], [modF])
        for si in (1, 3):
            C.v("dve", lambda e, si=si, modF=modF: e.tensor_scalar(
                modF[:, si * 8 * NST:(si + 1) * 8 * NST], modF[:, si * 8 * NST:(si + 1) * 8 * NST], 1.0, None, ALU.add),
                [modF], [modF])
        A.release(mm_)

        def mf(si, k, ms):
            c0 = (si * 8 + k) * NST + ms
            return modF[:, c0:c0 + 1]

        mAB = A.mark()
        wI = A.alloc("wI", 8 * IN_COLS)
        for k in range(8):
            C.dma("sp", wI[:, k * IN_COLS:(k + 1) * IN_COLS], w_in[l, k * 128:(k + 1) * 128, :], [w_in], [wI])
        o_aq, o_ff, o_fb, o_av, o_ag = 0, 256, 512, 768, 1024
        o_qkv, o_bg, o_ba, o_bb = 1280, 2048, 2304, 2312
        o_cu, o_xbc, o_dz, o_dt = 2320, 2576, 3088, 3344
        tm_groups = [(o_ff, 512, 0), (o_av, 512, 512), (o_bg, 272, 1024), (o_dz, 264, 1296)]
        fm_cols = [o_aq, o_aq + 128, o_ff, o_ff + 128, o_fb, o_fb + 128] + [o_qkv + i * 128 for i in range(6)] + \
                  [o_cu, o_cu + 128] + [o_xbc + i * 128 for i in range(4)]
        sts = [A.alloc("st", 8) for _ in range(2)]
        hTs = [A.alloc("hT", 8 * 512) for _ in range(1)]
        xts = [A.alloc("xt", D) for _ in range(2)]
        xns = [A.alloc("xn", D) for _ in range(2)]
        zts = [A.alloc("zt", ZT_COLS) for _ in range(2)]
        zfs = [A.alloc("zf", 512) for _ in range(2)]
        gcnt = 0
        for s in range(NS):
            for tg in range(0, NT, 4):
                ntile = min(4, NT - tg)
                ntok = ntile * 128
                hT = hTs[0]
                gcnt += 1
                for ti in range(ntile):
                    t = tg + ti
                    ms = NS if t * 128 < TC else s
                    xt = xts[t % 2]
                    xn = xns[t % 2]
                    st = sts[t % 2]
                    C.dma("sp", xt[:], X[s, t * 128:(t + 1) * 128, :], [X], [xt])
                    layernorm(C, xt, xn, st)
                    for half in range(2):
                        ps = PS[2 + half]
                        for kk in range(4):
                            k = half * 4 + kk
                            C.tr(ps, ps[:, kk * 128:(kk + 1) * 128], xn[:, k * 128:(k + 1) * 128], ident, [xn])
                        for kk in range(4):
                            k = half * 4 + kk
                            C.v("act", lambda e, ps=ps, kk=kk, k=k, ti=ti, ms=ms, hT=hT: e.activation(
                                hT[:, k * 512 + ti * 128: k * 512 + (ti + 1) * 128], ps[:, kk * 128:(kk + 1) * 128],
                                AF.Identity, bias=mf(0, k, ms), scale=mf(1, k, ms)), [ps, modF], [hT])
                C.dma("sp", HT[s, :, :, tg * 128: tg * 128 + ntok].rearrange("k p n -> p k n"),
                      hT[:].rearrange("p (k n) -> p k n", k=8)[:, :, 0:ntok], [hT], [HT])
                for ti in range(ntile):
                    t = tg + ti
                    zt = zts[t % 2]
                    for gi, (c0, wd, d0) in enumerate(tm_groups):
                        ps = PS[4 + gi % 2]
                        for k in range(8):
                            C.mm(ps, ps[:, 0:wd], hT[:, k * 512 + ti * 128: k * 512 + (ti + 1) * 128],
                                 wI[:, k * IN_COLS + c0: k * IN_COLS + c0 + wd], [hT, wI], start=(k == 0), stop=(k == 7))
                        C.v("dve" if gi % 2 == 0 else "act",
                            (lambda e, ps=ps, zt=zt, wd=wd, d0=d0: e.tensor_copy(zt[:, d0:d0 + wd], ps[:, 0:wd])) if gi % 2 == 0 else
                            (lambda e, ps=ps, zt=zt, wd=wd, d0=d0: e.copy(zt[:, d0:d0 + wd], ps[:, 0:wd])), [ps], [zt])
                    C.dma("sp", ZT[s, t * 128:(t + 1) * 128, :], zt[:], [zt], [ZT])
                for ci, c0 in enumerate(fm_cols):
                    ps = PS[6 + ci % 2]
                    zf = zfs[ci % 2]
                    for k in range(8):
                        C.mm(ps, ps[:, 0:ntok], wI[:, k * IN_COLS + c0: k * IN_COLS + c0 + 128], hT[:, k * 512: k * 512 + ntok],
                             [hT, wI], start=(k == 0), stop=(k == 7))
                    C.v("dve" if ci % 2 == 0 else "act",
                        (lambda e, ps=ps, zf=zf, ntok=ntok: e.tensor_copy(zf[:, 0:ntok], ps[:, 0:ntok])) if ci % 2 == 0 else
                        (lambda e, ps=ps, zf=zf, ntok=ntok: e.copy(zf[:, 0:ntok], ps[:, 0:ntok])), [ps], [zf])
                    C.dma("sp", ZF[s, ci * 128:(ci + 1) * 128, tg * 128: tg * 128 + ntok], zf[:, 0:ntok], [zf], [ZF])
        A.release(mAB)
        for s in range(NS):
            if "ssd" in cfg.parts:
                ssd_mixer(C, A, PS, K, cfg, l, s, ZT, ZF, O, W_)
            if "hgrn" in cfg.parts:
                hgrn_mixer(C, A, PS, K, cfg, l, s, ZT, ZF, O, W_)
            if "gdn" in cfg.parts:
                gdn_mixer(C, A, PS, K, cfg, l, s, ZT, ZF, O, W_)
            if "fnet" in cfg.parts:
                fnet_mixer(C, A, PS, K, cfg, l, s, ZF, O, W_)
        if "merge" in cfg.parts:
            for s in range(NS):
                merge_phase(C, A, PS, K, cfg, l, s, X, HT, O, MOD, W_)
        if "moe" in cfg.parts:
            for s in range(NS):
                moe_phase(C, A, PS, K, cfg, l, s, X, H2, YE, IDX, MOD, W_)
        A.release(mL)

    for k, t in dbg.items():
        src = {"dbg_X": X, "dbg_MOD": MOD, "dbg_ZT": ZT, "dbg_ZF": ZF, "dbg_O": O}[k]
        sap = src.ap
        if k == "dbg_O":
            c0 = dbg_cols[0]
            sap = src.ap[:, :, c0:c0 + t.ap.shape[2]]
        C.dma("sp", t.ap, sap, [src], [t])
    for s in range(NS):
        C.dma("sp", out[s, :, :], X[s, TC:, :], [X], [out])
    S.emit(nc, es)
    return nc, es


def _host_weights(inp, depth):
    f = lambda a: np.ascontiguousarray(np.asarray(a, dtype=np.float32))
    return {
        "ada_w": f(inp["ada_w"][:depth]), "ada_b": f(inp["ada_b"][:depth]), "w_in": f(inp["w_in"][:depth]),
        "consts": CONSTS_NP,
        "ssd_dt_bias": f(np.asarray(inp["ssd_dt_bias"])[:depth].reshape(depth, 8)),
        "ssd_a_log": f(np.asarray(inp["ssd_a_log"])[:depth].reshape(depth, 8)),
        "ssd_d": f(inp["ssd_d"][:depth]), "ssd_norm_w": f(inp["ssd_norm_w"][:depth]),
        "ssd_cwT": f(np.asarray(inp["ssd_conv_w"])[:depth].transpose(0, 2, 1)),
        "ssd_cbT": f(np.asarray(inp["ssd_conv_b"])[:depth][:, :, None]),
        "bd64": make_bd64(),
        "w_router": f(inp["w_router"][:depth]), "w_ff_gate": f(inp["w_ff_gate"][:depth]), "w_ff_up": f(inp["w_ff_up"][:depth]),
        "w_ff_down": f(inp["w_ff_down"][:depth]), "ln2_g": f(inp["ln2_g"][:depth]), "ln2_b": f(inp["ln2_b"][:depth]),
        "w_gate": f(inp["w_gate"][:depth]), "w_branch": f(inp["w_branch"][:depth]), "w_out": f(inp["w_out"][:depth]),
        "b_gateT": f(np.asarray(inp["b_gate"])[:depth].reshape(depth, 4, 8, 128).transpose(0, 3, 1, 2).reshape(depth, 128, 32)),
        "ln1_g": f(inp["ln1_g"][:depth]), "ln1_b": f(inp["ln1_b"][:depth]),
        "gdn_dt_bias": f(np.asarray(inp["gdn_dt_bias"])[:depth].reshape(depth, 8)),
        "gdn_a_log": f(np.asarray(inp["gdn_a_log"])[:depth].reshape(depth, 8)),
        "gdn_norm_w": f(inp["gdn_norm_w"][:depth]),
        "gdn_cwT": f(np.asarray(inp["gdn_conv_w"])[:depth].transpose(0, 2, 1)),
        "hgrn_lb_logits": f(np.asarray(inp["hgrn_lb_logits"])[:, :depth]), "hgrn_norm_w": f(inp["hgrn_norm_w"][:depth]),
    }


def _prepare(inp, n_cores, cfg=None):
    x = np.asarray(inp["x"], dtype=np.float32)
    ctx = np.asarray(inp["ctx"], dtype=np.float32)
    c = np.asarray(inp["c"], dtype=np.float32)
    c_ctx = np.asarray(inp["c_ctx"], dtype=np.float32)
    B, TL, _ = x.shape
    TC = ctx.shape[1]
    NS = B // n_cores
    depth = int(np.asarray(inp["ada_w"]).shape[0])
    if cfg is None:
        cfg = Cfg(NS, TC, TL, depth)
    wts = _host_weights(inp, depth)
    wts["dft_ctx"] = make_dft(TC)
    wts["dft_lat"] = make_dft(TL)
    pos = sincos_grid_np(TL // 64, 64, D)
    in_maps = []
    for i in range(n_cores):
        sl = slice(i * NS, (i + 1) * NS)
        call = np.concatenate([c[sl], c_ctx[None]], 0)
        cT = np.ascontiguousarray(call.reshape(NS + 1, 8, 128).transpose(2, 1, 0))
        m = {"x": np.ascontiguousarray(x[sl]), "ctx": np.ascontiguousarray(ctx[sl]), "cT": cT, "pos": pos}
        m.update(wts)
        in_maps.append(m)
    return cfg, in_maps


def kernel_sim(inp, cfg, simrun):
    cfg, in_maps = _prepare(inp, 1, cfg)
    nc, es = build(cfg)
    res = simrun(nc, in_maps)
    return np.concatenate([np.asarray(r["out"], dtype=np.float32) for r in res], axis=0)


def kernel(**inp):
    n_cores = 8
    cfg, in_maps = _prepare(inp, n_cores)
    nc, es = build(cfg)
    res = run_bass_kernel_spmd(nc, in_maps, core_ids=list(range(n_cores)))
    es.close()
    return np.concatenate([np.asarray(r["out"], dtype=np.float32) for r in res.results], axis=0)
```
